# Optimizing a Trainium2 kernel written in Bass

```python
import jax, jax.numpy as jnp
from jax import lax
import numpy as np

D_MODEL = 2048
BATCH = 4
SEQ = 2048
DEPTH = 2
DEC_BATCH = 32
DEC_SEQ = 64
PAST_LEN = 2048

CHUNK = 64
Q_BLOCK = 128
SB_HEAD_DIM = 128
SB_HEADS = D_MODEL // 256
SB_WIDTH = SB_HEADS * SB_HEAD_DIM
SGU_GROUP_DIM = 128
SGU_GROUPS = D_MODEL // 256
SGU_WIDTH = SGU_GROUPS * SGU_GROUP_DIM
SGU_LEN = 128
D_FF = -(-8 * D_MODEL // (3 * 256)) * 256
IN_COLS = 3 * SB_WIDTH + 2 * SGU_WIDTH + 2 * D_MODEL
EPS = 1e-6

kernel_name = "stickbreak_sgu_hybrid_stream_step"


def rmsnorm(x, g):
    x32 = x.astype(jnp.float32)
    y = x32 * lax.rsqrt(jnp.mean(x32 * x32, axis=-1, keepdims=True) + EPS)
    return (y * g.astype(jnp.float32)).astype(x.dtype)


def stick_breaking_block(q_blk, q_pos, k, v, k_pos):
    z = jnp.einsum('bqhd,bkhd->bhqk', q_blk.astype(jnp.float32), k.astype(jnp.float32)) * (SB_HEAD_DIM ** -0.5)
    mask = k_pos[None, :] < q_pos[:, None]
    log_beta = jax.nn.log_sigmoid(z)
    log_stay = jnp.where(mask, jax.nn.log_sigmoid(-z), 0.0)
    log_rest = lax.cumsum(log_stay, axis=3, reverse=True) - log_stay
    w = jnp.where(mask, jnp.exp(log_beta + log_rest), 0.0)
    o = jnp.einsum('bhqk,bkhd->bqhd', w, v.astype(jnp.float32))
    return o.astype(q_blk.dtype)


def stick_breaking_prompt(q, k, v):
    B, S = q.shape[0], q.shape[1]
    nb = S // Q_BLOCK
    pos = jnp.arange(S)
    qb = q.reshape(B, nb, Q_BLOCK, SB_HEADS, SB_HEAD_DIM).transpose(1, 0, 2, 3, 4)
    pb = pos.reshape(nb, Q_BLOCK)
    out = lax.map(lambda a: stick_breaking_block(a[0], a[1], k, v, pos), (qb, pb))
    return out.transpose(1, 0, 2, 3, 4).reshape(B, S, SB_HEADS, SB_HEAD_DIM)


def stick_breaking_sample(q, k_new, v_new, cache_k_l, cache_v_l):
    past = cache_k_l.shape[1]
    t = q.shape[1]
    k = jnp.concatenate([cache_k_l, k_new], axis=1)
    v = jnp.concatenate([cache_v_l, v_new], axis=1)
    return stick_breaking_block(q, past + jnp.arange(t), k, v, jnp.arange(past + t))


def spatial_gate(u, vn, w_s, b_s):
    B, T = vn.shape[0], vn.shape[1]
    L = min(T, SGU_LEN)
    nb = T // L
    c = jnp.arange(L) // CHUNK
    mask = c[None, :] <= c[:, None]
    w = jnp.where(mask, w_s[:, :L, :L], 0.0)
    vg = vn.reshape(B, nb, L, SGU_GROUPS, SGU_GROUP_DIM)
    s = jnp.einsum('gts,bnsgc->bntgc', w, vg) + b_s[:, :L].T[None, None, :, :, None]
    return u * s.reshape(B, T, SGU_WIDTH)


def layer(x, l, cache_k, cache_v, norm_mix, w_in, b_gate, sgu_norm, w_spatial, b_spatial,
          w_branch_a, w_branch_b, w_out, norm_ffn, w_gate_up, w_down):
    B, T = x.shape[0], x.shape[1]
    xn = rmsnorm(x, norm_mix[l])
    proj = xn @ w_in[l]
    q, k, v, u, vs, g = jnp.split(proj, [SB_WIDTH, 2 * SB_WIDTH, 3 * SB_WIDTH,
                                         3 * SB_WIDTH + SGU_WIDTH, 3 * SB_WIDTH + 2 * SGU_WIDTH], axis=-1)
    q = q.reshape(B, T, SB_HEADS, SB_HEAD_DIM)
    k = k.reshape(B, T, SB_HEADS, SB_HEAD_DIM)
    v = v.reshape(B, T, SB_HEADS, SB_HEAD_DIM)
    if cache_k is None:
        a = stick_breaking_prompt(q, k, v)
    else:
        a = stick_breaking_sample(q, k, v, cache_k[l], cache_v[l])
    a = a.reshape(B, T, SB_WIDTH)
    u = jax.nn.gelu(u)
    vs = rmsnorm(jax.nn.gelu(vs), sgu_norm[l])
    bb = spatial_gate(u, vs, w_spatial[l], b_spatial[l])
    gates = jax.nn.sigmoid(g + b_gate[l])
    g_a, g_b = gates[..., :D_MODEL], gates[..., D_MODEL:]
    merged = g_a * (a @ w_branch_a[l]) + g_b * (bb @ w_branch_b[l])
    h = x + merged @ w_out[l]
    hn = rmsnorm(h, norm_ffn[l])
    gu = hn @ w_gate_up[l]
    h = h + (jax.nn.silu(gu[..., :D_FF]) * gu[..., D_FF:]) @ w_down[l]
    return h, k, v, vs


def trunk(x, cache_k, cache_v, norm_mix, w_in, b_gate, sgu_norm, w_spatial, b_spatial,
          w_branch_a, w_branch_b, w_out, norm_ffn, w_gate_up, w_down, norm_final):
    ks, vs_list, svs = [], [], []
    for l in range(DEPTH):
        x, k, v, sv = layer(x, l, cache_k, cache_v, norm_mix, w_in, b_gate, sgu_norm, w_spatial,
                            b_spatial, w_branch_a, w_branch_b, w_out, norm_ffn, w_gate_up, w_down)
        ks.append(k)
        vs_list.append(v)
        svs.append(sv)
    return rmsnorm(x, norm_final), jnp.stack(ks), jnp.stack(vs_list), jnp.stack(svs)


def setup_inputs(seed: int = 0) -> dict:
    key = jax.random.key(seed)
    ks = jax.random.split(key, 20)
    f32 = jnp.float32
    n = lambda k, shape, s: (jax.random.normal(k, shape, f32) * s)
    return {
        "x_prompt": n(ks[0], (BATCH, SEQ, D_MODEL), 1.0),
        "x_sample": n(ks[1], (DEC_BATCH, DEC_SEQ, D_MODEL), 1.0),
        "cache_k": n(ks[2], (DEPTH, DEC_BATCH, PAST_LEN, SB_HEADS, SB_HEAD_DIM), 1.0),
        "cache_v": n(ks[3], (DEPTH, DEC_BATCH, PAST_LEN, SB_HEADS, SB_HEAD_DIM), 1.0),
        "norm_mix": 1.0 + n(ks[4], (DEPTH, D_MODEL), 0.02),
        "w_in": n(ks[5], (DEPTH, D_MODEL, IN_COLS), D_MODEL ** -0.5),
        "b_gate": n(ks[6], (DEPTH, 2 * D_MODEL), 0.02),
        "sgu_norm": 1.0 + n(ks[7], (DEPTH, SGU_WIDTH), 0.02),
        "w_spatial": n(ks[8], (DEPTH, SGU_GROUPS, SGU_LEN, SGU_LEN), SGU_LEN ** -0.5),
        "b_spatial": 1.0 + n(ks[9], (DEPTH, SGU_GROUPS, SGU_LEN), 0.1),
        "w_branch_a": n(ks[10], (DEPTH, SB_WIDTH, D_MODEL), SB_WIDTH ** -0.5),
        "w_branch_b": n(ks[11], (DEPTH, SGU_WIDTH, D_MODEL), SGU_WIDTH ** -0.5),
        "w_out": n(ks[12], (DEPTH, D_MODEL, D_MODEL), D_MODEL ** -0.5),
        "norm_ffn": 1.0 + n(ks[13], (DEPTH, D_MODEL), 0.02),
        "w_gate_up": n(ks[14], (DEPTH, D_MODEL, 2 * D_FF), D_MODEL ** -0.5),
        "w_down": n(ks[15], (DEPTH, D_FF, D_MODEL), D_FF ** -0.5),
        "norm_final": 1.0 + n(ks[16], (D_MODEL,), 0.02),
    }


def reference(x_prompt, x_sample, cache_k, cache_v, norm_mix, w_in, b_gate, sgu_norm, w_spatial,
              b_spatial, w_branch_a, w_branch_b, w_out, norm_ffn, w_gate_up, w_down, norm_final):
    y_prompt, k_prompt, v_prompt, _ = trunk(
        x_prompt, None, None, norm_mix, w_in, b_gate, sgu_norm, w_spatial, b_spatial,
        w_branch_a, w_branch_b, w_out, norm_ffn, w_gate_up, w_down, norm_final)
    y_sample, k_sample, v_sample, sgu_v_sample = trunk(
        x_sample, cache_k, cache_v, norm_mix, w_in, b_gate, sgu_norm, w_spatial, b_spatial,
        w_branch_a, w_branch_b, w_out, norm_ffn, w_gate_up, w_down, norm_final)
    return (y_prompt, y_sample, k_prompt, v_prompt, k_sample, v_sample, sgu_v_sample)
```

```python
import contextlib
import numpy as np
import concourse.bass as bass
import concourse.mybir as mybir
from concourse.bass_utils import run_bass_kernel_spmd

F32 = mybir.dt.float32
BF16 = mybir.dt.bfloat16
AF = mybir.ActivationFunctionType
ALU = mybir.AluOpType

CELL = 128


class Op:
    __slots__ = ("eng", "fn", "deps", "needed", "sigidx", "dsem", "dval", "tag")

    def __init__(self, eng, fn, tag=""):
        self.eng = eng
        self.fn = fn
        self.deps = []
        self.needed = False
        self.sigidx = 0
        self.dsem = None
        self.dval = 0
        self.tag = tag


class Sched:
    ENGS = ("pe", "act", "dve", "pool", "sp")

    def __init__(self):
        self.ops = {e: [] for e in self.ENGS}
        self.lastw = {}
        self.readers = {}
        self.dma_counts = {}
        self.last_dma = {}
        self.out_dmas = []

    def op(self, eng, fn, reads=(), writes=(), dma=None, tag="", is_out=False):
        o = Op(eng, fn, tag)
        deps = {}
        def _bank(acc):
            sp_, lo_, hi_ = acc
            if sp_ == "ps":
                lo_ = lo_ // 2048 * 2048
                hi_ = (hi_ + 2047) // 2048 * 2048
            return sp_, lo_, hi_
        reads = [_bank(a) for a in reads]
        writes = [_bank(a) for a in writes]
        for (space, lo, hi) in reads:
            lw = self.lastw.setdefault(space, {})
            rd = self.readers.setdefault(space, {})
            for c in range(lo // CELL, (hi - 1) // CELL + 1):
                w = lw.get(c)
                if w is not None:
                    deps[id(w)] = w
                rl = rd.setdefault(c, [])
                if space == "ps":
                    for r in rl:
                        if r.eng != eng:
                            deps[id(r)] = r
                rl.append(o)
        for (space, lo, hi) in writes:
            lw = self.lastw.setdefault(space, {})
            rd = self.readers.setdefault(space, {})
            for c in range(lo // CELL, (hi - 1) // CELL + 1):
                w = lw.get(c)
                if w is not None:
                    deps[id(w)] = w
                rl = rd.get(c)
                if rl:
                    lastc = {}
                    for r in rl:
                        if r.dsem is None:
                            lastc[r.eng] = r
                        else:
                            deps[id(r)] = r
                    for r in lastc.values():
                        deps[id(r)] = r
                lw[c] = o
                rd[c] = []
        for d in deps.values():
            if d is o:
                continue
            if d.dsem is None and d.eng == "pe" and eng == "pe" and dma is None:
                continue
            d.needed = True
            o.deps.append(d)
        if dma is not None:
            o.dsem = dma
            prev = self.last_dma.get(dma)
            if prev is not None and all(d is not prev for d in o.deps):
                o.deps.append(prev)
            self.last_dma[dma] = o
            self.dma_counts[dma] = self.dma_counts.get(dma, 0) + 16
            o.dval = self.dma_counts[dma]
            if is_out:
                self.out_dmas.append(o)
        self.ops[eng].append(o)
        return o

    def finish(self):
        o = Op("sp", None, "final")
        o.deps = list(self.out_dmas)
        self.ops["sp"].append(o)

    def emit(self, nc):
        with contextlib.ExitStack() as st:
            esem = {e: st.enter_context(nc.semaphore("s_" + e)) for e in self.ENGS}
            dsem = {k: st.enter_context(nc.semaphore("d_" + str(k))) for k in self.dma_counts}
            for e in self.ENGS:
                n = 0
                for o in self.ops[e]:
                    if o.dsem is None and o.needed:
                        n += 1
                        o.sigidx = n
            block = st.enter_context(nc.Block())

            def mk(e):
                def body(eng):
                    waited = {}
                    for o in self.ops[e]:
                        need = {}
                        for d in o.deps:
                            if d.dsem is not None:
                                s, v = dsem[d.dsem], d.dval
                            else:
                                s, v = esem[d.eng], d.sigidx
                            k = id(s)
                            if v > waited.get(k, 0) and v > need.get(k, (None, 0))[1]:
                                need[k] = (s, v)
                        for k, (s, v) in need.items():
                            eng.wait_ge(s, v)
                            waited[k] = v
                        if o.fn is None:
                            continue
                        inst = o.fn(eng)
                        if o.dsem is not None:
                            inst.then_inc(dsem[o.dsem], 16)
                        elif o.needed:
                            inst.then_inc(esem[e], 1)
                return body

            block.tensor(mk("pe"))
            block.scalar(mk("act"))
            block.vector(mk("dve"))
            block.gpsimd(mk("pool"))
            block.sync(mk("sp"))


class Buf:
    def __init__(self, space, ap, off, shape, es):
        self.space, self.ap, self.off, self.shape, self.es = space, ap, off, tuple(shape), es
        n = es
        for s in shape:
            n *= s
        self.nbytes = n
        st, strides = es, []
        for s in reversed(self.shape):
            strides.append(st)
            st *= s
        self.strides = list(reversed(strides))

    def whole(self):
        return (self.space, self.off, self.off + self.nbytes)

    def reg(self, *idx):
        lo = hi = 0
        for i, s in enumerate(self.shape):
            ix = idx[i] if i < len(idx) else None
            if ix is None:
                a, b = 0, s
            elif isinstance(ix, int):
                a, b = ix, ix + 1
            else:
                a, b = ix
            lo += a * self.strides[i]
            hi += (b - 1) * self.strides[i]
        return (self.space, self.off + lo, self.off + hi + self.es)


D = 2048
NCH = 16
NH = 8
HD = 128
SBW = 1024
DFF = 5632
NFC = 44
INC = 9216
TP = 1024
TS = 256
TBT = TP + TS
C_Q, C_K, C_V, C_U, C_VS, C_GA, C_GB = 0, 1024, 2048, 3072, 4096, 5120, 7168
EPS = 1e-6
SCALE = float(HD) ** -0.5
DBG = {}


def dkey(name, i=0):
    return ("dr_" + name, i * CELL, i * CELL + 1)


def build_program():
    nc = bass.Bass("TRN2", target_bir_lowering=False)
    S = Sched()

    def din(name, shape, dt=F32):
        if name in DBG.get("shrink", ()):
            return nc.dram_tensor(name, list(shape), dt).ap()
        return nc.dram_tensor(name, list(shape), dt, kind="ExternalInput").ap()

    def dout(name, shape, dt=F32):
        return nc.dram_tensor(name, list(shape), dt, kind="ExternalOutput").ap()

    xa = din("xa", [D, TP])
    xb = din("xb", [D, TBT])
    flag_d = din("flag", [128, 1])
    ck = din("ck", [2, 4, 2048, SBW])
    cv = din("cv", [2, 4, 2048, SBW])
    norm_mix = din("norm_mix", [2, D])
    w_in = din("w_in", [2, D, INC])
    b_gate = din("b_gate", [2, 2 * D])
    sgu_norm = din("sgu_norm", [2, SBW])
    w_spatial = din("w_spatial", [2, 8, 128, 128])
    b_spatial = din("b_spatial", [2, 8, 128])
    w_a = din("w_branch_a", [2, SBW, D])
    w_b = din("w_branch_b", [2, SBW, D])
    w_out = din("w_out", [2, D, D])
    norm_ffn = din("norm_ffn", [2, D])
    w_gu = din("w_gate_up", [2, D, 2 * DFF])
    w_dn = din("w_down", [2, DFF, D])
    norm_final = din("norm_final", [D])

    y_o = dout("y", [D, TBT])
    k_o = dout("ko", [2, SBW, TBT])
    v_o = dout("vo", [2, TBT, SBW])
    sv_o = dout("sv", [2, TS, SBW])

    skind = "ExternalOutput" if DBG else "Internal"
    resA = nc.dram_tensor("resA", [D, TP], F32, kind=skind).ap()
    resB = nc.dram_tensor("resB", [D, TBT], F32, kind=skind).ap()
    pk = nc.dram_tensor("pk", [2, SBW, TP], BF16, kind=skind).ap()
    pv = nc.dram_tensor("pv", [2, TP, SBW], BF16, kind=skind).ap()

    with contextlib.ExitStack() as st:
        AW = 53200
        arena_t = st.enter_context(nc.sbuf_tensor("arena", [128, AW], F32))
        arena = arena_t[:]
        ps_t = [st.enter_context(nc.psum_tensor("ps%d" % i, [128, 512], F32)) for i in range(8)]
        PS = [Buf("ps", ps_t[i][:], i * 2048, [512], 4) for i in range(8)]
        PSB = [ps_t[i][:].bitcast(BF16) for i in range(8)]

        cur = [0]

        def carve_at(off, shape, dt):
            es = 4 if dt == F32 else 2
            n = es
            for s_ in shape:
                n *= s_
            assert off % 4 == 0 and n % 4 == 0
            assert off + n <= AW * 4, (off, n)
            ap = arena[:, off // 4:(off + n) // 4]
            if dt != F32:
                ap = ap.bitcast(dt)
            if len(shape) == 2:
                ap = ap.rearrange("p (a b) -> p a b", b=shape[1])
            elif len(shape) == 3:
                ap = ap.rearrange("p (a b c) -> p a b c", b=shape[1], c=shape[2])
            return Buf("sb", ap, off, shape, es)

        def carve(shape, dt):
            es = 4 if dt == F32 else 2
            n = es
            for s_ in shape:
                n *= s_
            n = (n + 127) // 128 * 128
            b = carve_at(cur[0], shape, dt)
            cur[0] += n
            return b

        ident_b = carve([128], BF16)
        tri_b = carve([128], BF16)
        m2_b = carve([128], BF16)
        zero_b = carve([128], BF16)
        cm_f = carve([128], F32)
        cm8_f = carve([8, 64], F32)
        ident_f = carve([128], F32)
        ones_b = carve([128], BF16)
        flag_s = carve([32], F32)
        gcol = carve([5, 16], F32)
        bgc = carve([2, 32], F32)
        XN = carve([16, TBT], BF16)
        WS = [carve([4096], BF16) for _ in range(3)]
        RSTD = carve([TBT], F32)
        T5 = [carve([TBT], F32) for _ in range(4)]
        PH = cur[0]
        BU = carve([8, TBT], BF16)
        BQ = carve([10, 1024], BF16)
        BK = carve([8, TBT], BF16)
        BV = carve([10, 1024], BF16)
        MX = cur[0]
        sg = BK.off
        WmT = carve_at(sg, [8, 128], BF16); sg += 2048
        WbT = carve_at(sg, [8, 128], BF16); sg += 2048
        bSp = carve_at(sg, [8, 128], F32); sg += 4096
        bSs = carve_at(sg, [8, 128], F32); sg += 4096
        sguG = carve_at(sg, [1024], F32); sg += 4096
        Wsp_f = carve_at(sg, [8, 128], F32); sg += 4096
        Wbd_f = carve_at(sg, [8, 128], F32); sg += 4096
        VSF = [carve_at(sg + i * 4096, [1024], F32) for i in range(2)]; sg += 8192
        assert sg <= BV.off + BV.nbytes
        KPV = [(carve([1024], BF16), carve([8, 128], BF16)) for _ in range(2)]
        KC = [carve([2, 1024], BF16) for _ in range(2)]
        VC = [carve([2, 1024], BF16) for _ in range(2)]
        KT = [carve([8, 256], BF16) for _ in range(2)]
        mixer_end = cur[0]
        KVST = [carve_at(KC[0].off + i * 2048, [512], F32) for i in range(4)]
        a0 = T5[0].off
        ATT = []
        for i in range(2):
            E_ = carve_at(a0, [512], F32); a0 += 2048
            EN_ = carve_at(a0, [512], F32); a0 += 2048
            SP_ = carve_at(a0, [512], BF16); a0 += 1024
            WT_ = carve_at(a0, [512], BF16); a0 += 1024
            ATT.append((E_, EN_, SP_, WT_))
        assert a0 <= T5[3].off + T5[3].nbytes
        ACC = carve_at(PH, [16, TBT], F32)
        ACTG = [carve_at(PH + ACC.nbytes + i * 4 * TBT * 2, [4, TBT], BF16) for i in range(2)]
        ffn_end = PH + ACC.nbytes + 2 * 4 * TBT * 2
        print("arena bytes: mixer_end", mixer_end, "ffn_end", ffn_end, "cap", AW * 4)
        assert max(mixer_end, ffn_end) <= AW * 4

        rr = {"t5": 0, "ws": 0, "psd": 0, "ps1": 0}

        def t5():
            i = rr["t5"]; rr["t5"] = (i + 1) % 4
            return T5[i]

        def psset():
            i = rr["psd"]; rr["psd"] = (i + 1) % 2
            return [PS[3 * i], PS[3 * i + 1], PS[3 * i + 2]]

        def ps1():
            i = rr["ps1"]; rr["ps1"] = (i + 1) % 8
            return i

        def tiles_of(T):
            out, t0 = [], 0
            while t0 < T:
                n = min(512, T - t0)
                out.append((t0, n)); t0 += n
            return out

        def load_w(src2d, r0, kc, c0, ncols):
            i = rr["ws"]; rr["ws"] = (i + 1) % 3
            slot = WS[i]
            assert kc * ncols <= 4096
            view = slot.ap[:, 0:kc * ncols].rearrange("p (k n) -> p k n", n=ncols)
            src = src2d[r0:r0 + kc * 128, c0:c0 + ncols].rearrange("(k p) n -> p k n", p=128)
            S.op("pool", lambda e: e.dma_start(out=view, in_=src), writes=[slot.whole()], dma="w%d" % i)
            return view, slot

        def mm_fm(pset, wview, wslot, col, rhsbuf, kcs, T, first=True, last=True, rhs_k0=0):
            tl = tiles_of(T)
            nk = len(kcs)
            for ki, k in enumerate(kcs):
                for ti, (t0, n) in enumerate(tl):
                    S.op("pe", (lambda e, ti=ti, t0=t0, n=n, ki=ki, k=k: e.matmul(
                        pset[ti].ap[:, 0:n], lhsT=wview[:, ki, col:col + 128], rhs=rhsbuf.ap[:, rhs_k0 + k, t0:t0 + n],
                        start=(first and ki == 0), stop=(last and ki == nk - 1))),
                        reads=[wslot.whole(), rhsbuf.reg(rhs_k0 + k, (t0, t0 + n))], writes=[pset[ti].reg((0, n))])

        S.op("pool", lambda e: e.memset(ident_b.ap, 1.0), writes=[ident_b.whole()])
        S.op("pool", lambda e: e.affine_select(out=ident_b.ap, in_=ident_b.ap, pattern=[[-1, 128]], compare_op=ALU.is_equal,
                                               fill=0.0, base=0, channel_multiplier=1), reads=[ident_b.whole()], writes=[ident_b.whole()])
        S.op("pool", lambda e: e.memset(ident_f.ap, 1.0), writes=[ident_f.whole()])
        S.op("pool", lambda e: e.affine_select(out=ident_f.ap, in_=ident_f.ap, pattern=[[-1, 128]], compare_op=ALU.is_equal,
                                               fill=0.0, base=0, channel_multiplier=1), reads=[ident_f.whole()], writes=[ident_f.whole()])
        S.op("pool", lambda e: e.memset(tri_b.ap, 1.0), writes=[tri_b.whole()])
        S.op("pool", lambda e: e.affine_select(out=tri_b.ap, in_=tri_b.ap, pattern=[[-1, 128]], compare_op=ALU.is_ge,
                                               fill=0.0, base=0, channel_multiplier=1), reads=[tri_b.whole()], writes=[tri_b.whole()])
        S.op("pool", lambda e: e.memset(m2_b.ap, 1.0), writes=[m2_b.whole()])
        S.op("pool", lambda e: e.affine_select(out=m2_b.ap, in_=m2_b.ap, pattern=[[1, 128]], compare_op=ALU.is_gt,
                                               fill=0.0, base=0, channel_multiplier=-1), reads=[m2_b.whole()], writes=[m2_b.whole()])
        S.op("pool", lambda e: e.memset(zero_b.ap, 0.0), writes=[zero_b.whole()])
        S.op("pool", lambda e: e.memset(ones_b.ap, 1.0), writes=[ones_b.whole()])
        S.op("pool", lambda e: e.memset(cm_f.ap, 1.0), writes=[cm_f.whole()])
        S.op("pool", lambda e: e.affine_select(out=cm_f.ap, in_=cm_f.ap, pattern=[[1, 128]], compare_op=ALU.is_gt,
                                               fill=0.0, base=0, channel_multiplier=-1), reads=[cm_f.whole()], writes=[cm_f.whole()])
        S.op("pool", lambda e: e.memset(cm8_f.ap, 1.0), writes=[cm8_f.whole()])
        for half in range(2):
            S.op("pool", lambda e, half=half: e.affine_select(
                out=cm8_f.ap[half * 64:(half + 1) * 64], in_=cm8_f.ap[half * 64:(half + 1) * 64], pattern=[[0, 8], [1, 64]],
                compare_op=ALU.is_gt, fill=0.0, base=0, channel_multiplier=-1), reads=[cm8_f.whole()], writes=[cm8_f.whole()])
        S.op("sp", lambda e: e.dma_start(out=flag_s.ap[:, 0:1], in_=flag_d[:, :]), writes=[flag_s.whole()], dma="c0")
        with nc.allow_non_contiguous_dma(reason="tiny per-feature vectors"):
            for i, src in enumerate([norm_mix[0], norm_mix[1], norm_ffn[0], norm_ffn[1], norm_final]):
                S.op("sp", lambda e, i=i, src=src: e.dma_start(out=gcol.ap[:, i, :], in_=src.rearrange("(c p) -> p c", p=128), allow_slow_non_contiguous=True),
                     writes=[gcol.whole()], dma="c0")
            for l in range(2):
                S.op("sp", lambda e, l=l: e.dma_start(out=bgc.ap[:, l, :], in_=b_gate[l].rearrange("(c p) -> p c", p=128), allow_slow_non_contiguous=True),
                     writes=[bgc.whole()], dma="c0")

        MISC = AW * 4 - 1664
        assert max(mixer_end, ffn_end) <= MISC
        eps_c = carve_at(MISC, [16], F32)
        rstdv = carve_at(MISC + 64, [16], F32)
        ssqv = carve_at(MISC + 128, [12, 4], F32)
        S.op("pool", lambda e: e.memset(eps_c.ap, EPS), writes=[eps_c.whole()])
        RESN = {id(resA): "resA", id(resB): "resB"}

        def rkey(res, c):
            return dkey(RESN[id(res)], c)

        def t5key(b):
            return "t5_%d" % T5.index(b)

        def load_x(xsrc, T):
            for c in range(NCH):
                S.op("sp", lambda e, c=c: e.dma_start(out=ACC.ap[:, c, 0:T], in_=xsrc[c * 128:(c + 1) * 128, 0:T]),
                     writes=[ACC.reg(c, (0, T))], dma="accl%d" % (c % 8))

        def norm_stats(res, T, from_sbuf):
            tl = tiles_of(T)
            if not from_sbuf:
                for c in range(NCH):
                    S.op("sp", lambda e, c=c: e.dma_start(out=ACC.ap[:, c, 0:T], in_=res[c * 128:(c + 1) * 128, 0:T]),
                         reads=[rkey(res, c)], writes=[ACC.reg(c, (0, T))], dma="accl%d" % (c % 8))
            pset = psset()
            for c in range(NCH):
                sqb = t5()
                sq_ap = sqb.ap.bitcast(BF16)
                S.op("act", lambda e, c=c, sq_ap=sq_ap: e.activation(out=sq_ap[:, 0:T], in_=ACC.ap[:, c, 0:T], func=AF.Square),
                     reads=[ACC.reg(c, (0, T))], writes=[sqb.reg((0, T // 2))])
                for ti, (t0, n) in enumerate(tl):
                    S.op("pe", lambda e, ti=ti, t0=t0, n=n, c=c, sq_ap=sq_ap: e.matmul(
                        pset[ti].ap[:, 0:n], lhsT=ones_b.ap, rhs=sq_ap[:, t0:t0 + n], start=(c == 0), stop=(c == NCH - 1)),
                        reads=[sqb.reg((0, T // 2)), ones_b.whole()], writes=[pset[ti].reg((0, n))])
            for ti, (t0, n) in enumerate(tl):
                S.op("act", lambda e, ti=ti, t0=t0, n=n: e.activation(out=RSTD.ap[:, t0:t0 + n], in_=pset[ti].ap[:, 0:n], func=AF.Ln, bias=eps_c.ap[:, 0:1], scale=1.0 / D),
                     reads=[pset[ti].reg((0, n)), eps_c.whole()], writes=[RSTD.reg((t0, t0 + n))])
                S.op("act", lambda e, t0=t0, n=n: e.activation(out=RSTD.ap[:, t0:t0 + n], in_=RSTD.ap[:, t0:t0 + n], func=AF.Exp, scale=-0.5),
                     reads=[RSTD.reg((t0, t0 + n))], writes=[RSTD.reg((t0, t0 + n))])

        def norm_apply(res, T, gi):
            for c in range(NCH):
                S.op("dve", lambda e, c=c: e.scalar_tensor_tensor(out=XN.ap[:, c, 0:T], in0=ACC.ap[:, c, 0:T], scalar=gcol.ap[:, gi, c:c + 1],
                                                                in1=RSTD.ap[:, 0:T], op0=ALU.mult, op1=ALU.mult),
                     reads=[ACC.reg(c, (0, T)), gcol.whole(), RSTD.reg((0, T))], writes=[XN.reg(c, (0, T))])

        def sgu_stage(l, T, write_sv):
            NTB = T // 128
            S.op("sp", lambda e: e.dma_start(out=sguG.ap, in_=sgu_norm[l].partition_broadcast(128)), writes=[sguG.whole()], dma="c1")
            tag = 'A%d_' % l if T == TP else 'B%d_' % l
            chk(tag + 'sgu_c', [('wmt', WmT), ('wbt', WbT), ('bsp', bSp), ('bss', bSs), ('sgug', sguG)])
            tl = tiles_of(T)
            for g4 in range(4):
                wv, wsl = load_w(w_in[l], 0, 16, C_U + g4 * 256, 256)
                chk(tag + 'sgu_w', [('ws', wsl)])
                for j in range(2):
                    g = g4 * 2 + j
                    pset = psset()
                    mm_fm(pset, wv, wsl, j * 128, XN, list(range(16)), T)
                    if DBG.get("stop") == tag + 'sgu_m':
                        tt = t5()
                        S.op("dve", lambda e: e.tensor_copy(out=tt.ap[:, 0:512], in_=pset[0].ap[:, 0:512]), reads=[pset[0].whole()], writes=[tt.whole()])
                        chk(tag + 'sgu_m', [('ps', tt)])
                    for ti, (t0, n) in enumerate(tl):
                        S.op("act", lambda e, ti=ti, t0=t0, n=n, g=g, pset=pset: e.activation(out=BU.ap[:, g, t0:t0 + n], in_=pset[ti].ap[:, 0:n], func=AF.Gelu_apprx_tanh),
                             reads=[pset[ti].reg((0, n))], writes=[BU.reg(g, (t0, t0 + n))])
                    if g == DBG.get('gstop', -1):
                        chk(tag + 'sgu_g', [('bu', BU)])
            chk(tag + 'sgu_u', [('bu', BU)])
            junk = T5[3]
            for wt in range(4):
                wv, wsl = load_w(w_in[l], 0, 16, C_VS + wt * 256, 256)
                for tb in range(NTB):
                    b = ps1()
                    for k in range(16):
                        S.op("pe", lambda e, b=b, k=k, tb=tb, wv=wv: e.matmul(PS[b].ap[:, 0:256], lhsT=XN.ap[:, k, tb * 128:(tb + 1) * 128], rhs=wv[:, k, :],
                                                                           start=(k == 0), stop=(k == 15)),
                             reads=[XN.reg(k, (tb * 128, (tb + 1) * 128)), wsl.whole()], writes=[PS[b].reg((0, 256))])
                    if tb < 8:
                        dst, dreg = BQ.ap[:, tb, wt * 256:(wt + 1) * 256], BQ.reg(tb, (wt * 256, (wt + 1) * 256))
                    else:
                        dst, dreg = VSF[tb - 8].ap[:, wt * 256:(wt + 1) * 256], VSF[tb - 8].reg((wt * 256, (wt + 1) * 256))
                    S.op("act", lambda e, b=b, dst=dst: e.activation(out=dst, in_=PS[b].ap[:, 0:256], func=AF.Gelu_apprx_tanh),
                         reads=[PS[b].reg((0, 256))], writes=[dreg])
                    S.op("act", lambda e, dst=dst, tb=tb, wt=wt: e.activation(out=junk.ap[:, 0:256], in_=dst, func=AF.Square, accum_out=ssqv.ap[:, tb, wt:wt + 1]),
                         reads=[dreg], writes=[junk.reg((0, 256)), ssqv.reg(tb, wt)])
            S.op("sp", lambda e: e.dma_start(out=bSp.ap, in_=b_spatial[l].partition_broadcast(128)), writes=[bSp.whole()], dma="c1")
            for hf in range(2):
                S.op("sp", lambda e, hf=hf: e.dma_start(out=bSs.ap[:, :, hf * 64:(hf + 1) * 64], in_=b_spatial[l][:, 0:64].partition_broadcast(128)),
                     writes=[bSs.whole()], dma="c1")
            S.op("sp", lambda e: e.dma_start(out=Wsp_f.ap, in_=w_spatial[l].rearrange("g t s -> t g s")), writes=[Wsp_f.whole()], dma="c1")
            S.op("dve", lambda e: e.memset(Wbd_f.ap, 0.0), writes=[Wbd_f.whole()])
            S.op("dve", lambda e: e.memset(WmT.ap, 0.0), writes=[WmT.whole()])
            for hf in range(2):
                S.op("sp", lambda e, hf=hf: e.dma_start(out=Wbd_f.ap[hf * 64:(hf + 1) * 64, :, hf * 64:(hf + 1) * 64],
                                                      in_=w_spatial[l][:, 0:64, 0:64].rearrange("g t s -> t g s")),
                     writes=[Wbd_f.whole()], dma="c1")
            for g in range(8):
                b = ps1()
                S.op("pe", lambda e, b=b, g=g: e.transpose(out=PS[b].ap[:, 0:128], in_=Wsp_f.ap[:, g, :], identity=ident_f.ap),
                     reads=[Wsp_f.whole(), ident_f.whole()], writes=[PS[b].reg((0, 128))])
                S.op("pe", lambda e, b=b, g=g: e.transpose(out=PS[b].ap[:, 128:256], in_=Wbd_f.ap[:, g, :], identity=ident_f.ap),
                     reads=[Wbd_f.whole(), ident_f.whole()], writes=[PS[b].reg((128, 256))])
                S.op("dve", lambda e, b=b, g=g: e.tensor_copy(out=WmT.ap[0:64, g, :], in_=PS[b].ap[0:64, 0:128]),
                     reads=[PS[b].reg((0, 256))], writes=[WmT.whole()])
                S.op("dve", lambda e, b=b, g=g: e.tensor_copy(out=WmT.ap[64:128, g, 64:128], in_=PS[b].ap[64:128, 64:128]),
                     reads=[PS[b].reg((0, 256))], writes=[WmT.whole()])
                S.op("dve", lambda e, b=b, g=g: e.tensor_copy(out=WbT.ap[:, g, :], in_=PS[b].ap[:, 128:256]),
                     reads=[PS[b].reg((0, 256))], writes=[WbT.whole()])
            S.op("dve", lambda e: e.tensor_reduce(out=rstdv.ap[:, 0:NTB], in_=ssqv.ap[:, 0:NTB, :], axis=mybir.AxisListType.X, op=ALU.add),
                 reads=[ssqv.whole()], writes=[rstdv.whole()])
            S.op("act", lambda e: e.activation(out=rstdv.ap[:, 0:NTB], in_=rstdv.ap[:, 0:NTB], func=AF.Ln, bias=eps_c.ap[:, 0:1], scale=1.0 / SBW),
                 reads=[rstdv.whole(), eps_c.whole()], writes=[rstdv.whole()])
            S.op("act", lambda e: e.activation(out=rstdv.ap[:, 0:NTB], in_=rstdv.ap[:, 0:NTB], func=AF.Exp, scale=-0.5),
                 reads=[rstdv.whole()], writes=[rstdv.whole()])
            for tb in range(NTB):
                if tb < 8:
                    S.op("dve", lambda e, tb=tb: e.scalar_tensor_tensor(out=BQ.ap[:, tb, :], in0=BQ.ap[:, tb, :], scalar=rstdv.ap[:, tb:tb + 1], in1=sguG.ap,
                                                                      op0=ALU.mult, op1=ALU.mult),
                         reads=[BQ.reg(tb), rstdv.whole(), sguG.whole()], writes=[BQ.reg(tb)])
                else:
                    vf = VSF[tb - 8]
                    S.op("dve", lambda e, tb=tb, vf=vf: e.scalar_tensor_tensor(out=vf.ap, in0=vf.ap, scalar=rstdv.ap[:, tb:tb + 1], in1=sguG.ap,
                                                                             op0=ALU.mult, op1=ALU.mult),
                         reads=[vf.whole(), rstdv.whole(), sguG.whole()], writes=[vf.whole()])
                    S.op("dve", lambda e, tb=tb, vf=vf: e.tensor_copy(out=BQ.ap[:, tb, :], in_=vf.ap), reads=[vf.whole()], writes=[BQ.reg(tb)])
                    if write_sv:
                        S.op("sp", lambda e, tb=tb, vf=vf: e.dma_start(out=sv_o[l, (tb - 8) * 128:(tb - 7) * 128, :], in_=vf.ap),
                             reads=[vf.whole()], dma="o_sv", is_out=True)
            chk(tag + 'sgu_vn', [('bq', BQ)])
            for g in range(8):
                for t4 in range(0, NTB, 4):
                    nb = min(4, NTB - t4)
                    b = ps1()
                    for j in range(nb):
                        tb = t4 + j
                        wt_ = WmT if tb < 8 else WbT
                        S.op("pe", lambda e, b=b, j=j, tb=tb, g=g, wt_=wt_: e.matmul(PS[b].ap[:, j * 128:(j + 1) * 128], lhsT=BQ.ap[:, tb, g * 128:(g + 1) * 128],
                                                                                   rhs=wt_.ap[:, g, :], start=True, stop=True),
                             reads=[BQ.reg(tb, (g * 128, (g + 1) * 128)), wt_.whole()], writes=[PS[b].reg((j * 128, (j + 1) * 128))])
                    tmp = t5()
                    bs_ = bSp if t4 < 8 else bSs
                    S.op("dve", lambda e, b=b, g=g, bs_=bs_, tmp=tmp, nb=nb: e.tensor_tensor(
                        out=tmp.ap[:, 0:nb * 128].rearrange("p (b t) -> p b t", t=128), in0=PS[b].ap[:, 0:nb * 128].rearrange("p (b t) -> p b t", t=128),
                        in1=bs_.ap[:, g:g + 1, :].to_broadcast([128, nb, 128]), op=ALU.add),
                        reads=[PS[b].reg((0, nb * 128)), bs_.whole()], writes=[tmp.reg((0, nb * 128))])
                    S.op("dve", lambda e, g=g, t4=t4, nb=nb, tmp=tmp: e.tensor_tensor(out=BU.ap[:, g, t4 * 128:(t4 + nb) * 128], in0=BU.ap[:, g, t4 * 128:(t4 + nb) * 128],
                                                                                    in1=tmp.ap[:, 0:nb * 128], op=ALU.mult),
                         reads=[BU.reg(g, (t4 * 128, (t4 + nb) * 128)), tmp.reg((0, nb * 128))], writes=[BU.reg(g, (t4 * 128, (t4 + nb) * 128))])
                    if g * 10 + t4 == DBG.get('gstop', -1):
                        chk(tag + 'sgu_s', [('bu', BU), ('tmp', tmp)])

        def qkv_stage(l, T, grpA, need_q=True):
            NTB = T // 128
            tl = tiles_of(T)
            BQf = BQ.ap.rearrange("p a b -> p (a b)").rearrange("p (h t) -> p h t", t=TBT)

            def bqreg(h, t0, t1):
                return ("sb", BQ.off + (h * TBT + t0) * 2, BQ.off + (h * TBT + t1) * 2)
            for which, cbase in (("q", C_Q), ("k", C_K)):
                if which == "q" and not need_q:
                    continue
                for g4 in range(4):
                    wv, wsl = load_w(w_in[l], 0, 16, cbase + g4 * 256, 256)
                    for j in range(2):
                        h = g4 * 2 + j
                        pset = psset()
                        mm_fm(pset, wv, wsl, j * 128, XN, list(range(16)), T)
                        for ti, (t0, n) in enumerate(tl):
                            if which == "q":
                                S.op("act", lambda e, ti=ti, t0=t0, n=n, h=h, pset=pset: e.copy(out=BQf[:, h, t0:t0 + n], in_=pset[ti].ap[:, 0:n]),
                                     reads=[pset[ti].reg((0, n))], writes=[bqreg(h, t0, t0 + n)])
                            elif grpA:
                                S.op("dve", lambda e, ti=ti, t0=t0, n=n, h=h, pset=pset: e.tensor_copy(out=BK.ap[:, h, t0:t0 + n], in_=pset[ti].ap[:, 0:n]),
                                     reads=[pset[ti].reg((0, n))], writes=[BK.reg(h, (t0, t0 + n))])
                        if which == "k" and grpA:
                            S.op("sp", lambda e, h=h: e.dma_start(out=pk[l, h * 128:(h + 1) * 128, :], in_=BK.ap[:, h, 0:TP]),
                                 reads=[BK.reg(h, (0, TP))], writes=[dkey("pk%d" % l, h)], dma="pkw")
                        if which == "k" and not grpA:
                            kf = t5()
                            for ti, (t0, n) in enumerate(tl):
                                S.op("act", lambda e, ti=ti, t0=t0, n=n, pset=pset, kf=kf: e.copy(out=kf.ap[:, t0:t0 + n], in_=pset[ti].ap[:, 0:n]),
                                     reads=[pset[ti].reg((0, n))], writes=[kf.reg((t0, t0 + n))])
                                S.op("dve", lambda e, t0=t0, n=n, h=h, kf=kf: e.tensor_copy(out=BK.ap[:, h, t0:t0 + n], in_=kf.ap[:, t0:t0 + n]),
                                     reads=[kf.reg((t0, t0 + n))], writes=[BK.reg(h, (t0, t0 + n))])
                            S.op("sp", lambda e, h=h, kf=kf: e.dma_start(out=k_o[l, h * 128:(h + 1) * 128, 0:T], in_=kf.ap[:, 0:T]),
                                 reads=[kf.reg((0, T))], dma=t5key(kf), is_out=True)
            for which, cbase in (("v", C_V),):
                for wt in range(4):
                    wv, wsl = load_w(w_in[l], 0, 16, cbase + wt * 256, 256)
                    for tb in range(NTB):
                        b = ps1()
                        for k in range(16):
                            S.op("pe", lambda e, b=b, k=k, tb=tb, wv=wv: e.matmul(PS[b].ap[:, 0:256], lhsT=XN.ap[:, k, tb * 128:(tb + 1) * 128], rhs=wv[:, k, :],
                                                                               start=(k == 0), stop=(k == 15)),
                                 reads=[XN.reg(k, (tb * 128, (tb + 1) * 128)), wsl.whole()], writes=[PS[b].reg((0, 256))])
                        if not grpA:
                            stg = KVST[(tb * 4 + wt) % 4]
                            S.op("act", lambda e, b=b, stg=stg: e.copy(out=stg.ap[:, 0:256], in_=PS[b].ap[:, 0:256]),
                                 reads=[PS[b].reg((0, 256))], writes=[stg.reg((0, 256))])
                            dst = (k_o if which == "k" else v_o)
                            S.op("sp", lambda e, stg=stg, dst=dst, tb=tb, wt=wt: e.dma_start(out=dst[l, tb * 128:(tb + 1) * 128, wt * 256:(wt + 1) * 256], in_=stg.ap[:, 0:256]),
                                 reads=[stg.reg((0, 256))], dma="o_kv%d" % ((tb * 4 + wt) % 4), is_out=True)
                        if which == "v":
                            if grpA:
                                S.op("dve", lambda e, b=b, tb=tb, wt=wt: e.tensor_scalar(out=BV.ap[:, tb, wt * 256:(wt + 1) * 256], in0=PS[b].ap[:, 0:256],
                                                                                       scalar1=flag_s.ap[:, 0:1], scalar2=None, op0=ALU.mult),
                                     reads=[PS[b].reg((0, 256)), flag_s.whole()], writes=[BV.reg(tb, (wt * 256, (wt + 1) * 256))])
                            else:
                                S.op("dve", lambda e, b=b, tb=tb, wt=wt: e.tensor_copy(out=BV.ap[:, tb, wt * 256:(wt + 1) * 256], in_=PS[b].ap[:, 0:256]),
                                     reads=[PS[b].reg((0, 256))], writes=[BV.reg(tb, (wt * 256, (wt + 1) * 256))])
            if grpA:
                for tb in range(NTB):
                    S.op("sp", lambda e, tb=tb: e.dma_start(out=pv[l, tb * 128:(tb + 1) * 128, :], in_=BV.ap[:, tb, :]),
                         reads=[BV.reg(tb)], writes=[dkey("pv%d" % l, tb)], dma="pvw")

        zrhs = carve_at(MISC + 512, [512], BF16)
        assert MISC + 512 + 1024 <= AW * 4
        S.op("pool", lambda e: e.memset(zrhs.ap, 0.0), writes=[zrhs.whole()])

        def zero_bank(b):
            S.op("pe", lambda e: e.matmul(PS[b].ap[:, 0:512], lhsT=zero_b.ap, rhs=zrhs.ap, start=True, stop=False, skip_group_check=True),
                 reads=[zero_b.whole(), zrhs.whole()], writes=[PS[b].whole()])

        def run_units(units):
            for u in units:
                zero_bank(u["C"]); zero_bank(u["O"])
            nmax = max(len(u["tiles"]) for u in units)

            def geo(u, i):
                s_fn, mask_fn, pv_fn, nparts, p0, n0 = u["tiles"][i]
                return slice(p0, p0 + nparts), slice(n0, 512), (n0, 512)

            def front(u, i):
                s_fn, mask_fn, pv_fn, nparts, p0, n0 = u["tiles"][i]
                E_, EN_, SP_, WT_ = ATT[u["att"]]
                sb = u["S"]
                pr, cs, rg = geo(u, i)
                S.op("act", lambda e: e.activation(out=E_.ap[pr, cs], in_=PS[sb].ap[pr, cs], func=AF.Exp, scale=SCALE),
                     reads=[PS[sb].reg(rg)], writes=[E_.reg(rg)])
                if mask_fn is not None:
                    mask_fn(E_)
                S.op("act", lambda e: e.activation(out=SP_.ap[pr, cs], in_=E_.ap[pr, cs], func=AF.Ln, bias=1.0, scale=1.0),
                     reads=[E_.reg(rg)], writes=[SP_.reg(rg)])

            for u in units:
                u["tiles"][0][0](u["S"])
            for u in units:
                front(u, 0)
            for i in range(nmax):
                act = [u for u in units if i < len(u["tiles"])]
                for u in act:
                    E_, EN_, SP_, WT_ = ATT[u["att"]]
                    cb = u["C"]
                    pr, cs, rg = geo(u, i)
                    S.op("pe", lambda e, cb=cb, SP_=SP_, pr=pr, cs=cs: e.matmul(PS[cb].ap[:, cs], lhsT=tri_b.ap[pr, :], rhs=SP_.ap[pr, cs], start=False, stop=False, skip_group_check=True),
                         reads=[SP_.reg(rg), tri_b.whole()], writes=[PS[cb].reg(rg)])
                    if i + 1 < len(u["tiles"]):
                        u["tiles"][i + 1][0](u["S"])
                for u in act:
                    E_, EN_, SP_, WT_ = ATT[u["att"]]
                    cb = u["C"]
                    pr, cs, rg = geo(u, i)
                    S.op("act", lambda e, cb=cb, EN_=EN_, pr=pr, cs=cs: e.activation(out=EN_.ap[pr, cs], in_=PS[cb].ap[pr, cs], func=AF.Exp, scale=-1.0),
                         reads=[PS[cb].reg(rg)], writes=[EN_.reg(rg)])
                for u in act:
                    s_fn, mask_fn, pv_fn, nparts, p0, n0 = u["tiles"][i]
                    E_, EN_, SP_, WT_ = ATT[u["att"]]
                    cb = u["C"]
                    pr, cs, rg = geo(u, i)
                    S.op("pe", lambda e, cb=cb, SP_=SP_, pr=pr, cs=cs: e.matmul(PS[cb].ap[:, cs], lhsT=m2_b.ap[pr, :], rhs=SP_.ap[pr, cs], start=False, stop=False, skip_group_check=True),
                         reads=[SP_.reg(rg), m2_b.whole()], writes=[PS[cb].reg(rg)])
                    S.op("dve", lambda e, E_=E_, EN_=EN_, WT_=WT_, pr=pr, cs=cs: e.tensor_tensor(out=WT_.ap[pr, cs], in0=E_.ap[pr, cs], in1=EN_.ap[pr, cs], op=ALU.mult),
                         reads=[E_.reg(rg), EN_.reg(rg)], writes=[WT_.reg(rg)])
                    pv_fn(u["O"], WT_)
                for u in act:
                    if i + 1 < len(u["tiles"]):
                        front(u, i + 1)
            for u in units:
                u["fin"](u["O"])

        def attention_stage(l, T, use_prev, has_s):
            BQf = BQ.ap.rearrange("p a b -> p (a b)").rearrange("p (h t) -> p h t", t=TBT)

            def bqreg(h, t0, t1):
                return ("sb", BQ.off + (h * TBT + t0) * 2, BQ.off + (h * TBT + t1) * 2)

            def prompt_unit(h, qg, slot, kp, vp):
                q0 = qg * 512
                blocks = [("own", kb) for kb in range(qg * 4 + 3, -1, -1)]
                if use_prev:
                    blocks += [("prev", kb) for kb in range(7, -1, -1)]
                tiles = []
                for kind, kb in blocks:
                    if kind == "own":
                        n0 = max(q0, kb * 128) - q0
                        diag = kb * 128 >= q0
                        klhs, klr = BK.ap[:, h, kb * 128:(kb + 1) * 128], BK.reg(h, (kb * 128, (kb + 1) * 128))
                        vlhs, vlr = BV.ap[:, kb, h * 128:(h + 1) * 128], BV.reg(kb, (h * 128, (h + 1) * 128))
                    else:
                        n0, diag = 0, False
                        klhs, klr = kp.ap[:, kb * 128:(kb + 1) * 128], kp.reg((kb * 128, (kb + 1) * 128))
                        vlhs, vlr = vp.ap[:, kb, :], vp.reg(kb)

                    def s_fn(sb, klhs=klhs, klr=klr, n0=n0):
                        S.op("pe", lambda e: e.matmul(PS[sb].ap[:, n0:512], lhsT=klhs, rhs=BQf[:, h, q0 + n0:q0 + 512], start=True, stop=True),
                             reads=[klr, bqreg(h, q0 + n0, q0 + 512)], writes=[PS[sb].reg((n0, 512))])

                    def m_fn(E_, n0=n0):
                        S.op("dve", lambda e: e.tensor_tensor(out=E_.ap[:, n0:n0 + 128], in0=E_.ap[:, n0:n0 + 128], in1=cm_f.ap, op=ALU.mult),
                             reads=[E_.reg((n0, n0 + 128)), cm_f.whole()], writes=[E_.reg((n0, n0 + 128))])

                    def pv_fn(ob, WT_, vlhs=vlhs, vlr=vlr, n0=n0):
                        S.op("pe", lambda e: e.matmul(PS[ob].ap[:, n0:512], lhsT=vlhs, rhs=WT_.ap[:, n0:512], start=False, stop=False, skip_group_check=True),
                             reads=[vlr, WT_.reg((n0, 512))], writes=[PS[ob].reg((n0, 512))])
                    tiles.append((s_fn, m_fn if diag else None, pv_fn, 128, 0, n0))

                def fin(ob):
                    S.op("act", lambda e: e.copy(out=BQf[:, h, q0:q0 + 512], in_=PS[ob].ap[:, 0:512]),
                         reads=[PS[ob].whole()], writes=[bqreg(h, q0, q0 + 512)])
                return {"S": slot, "C": 2 + slot, "O": 4 + slot, "att": slot, "tiles": tiles, "fin": fin}

            for h2 in range(0, NH, 2):
                kvs = []
                for j in range(2):
                    h = h2 + j
                    kp, vp = KPV[j]
                    if use_prev:
                        S.op("sp", lambda e, kp=kp, h=h: e.dma_start(out=kp.ap, in_=pk[l, h * 128:(h + 1) * 128, :]),
                             reads=[dkey("pk%d" % l, h)], writes=[kp.whole()], dma="kp%d" % j)
                        S.op("sp", lambda e, vp=vp, h=h: e.dma_start(out=vp.ap, in_=pv[l, :, h * 128:(h + 1) * 128].rearrange("(b p) d -> p b d", p=128)),
                             reads=[dkey("pv%d" % l, tb) for tb in range(8)], writes=[vp.whole()], dma="vp%d" % j)
                    kvs.append((kp, vp))
                for qg in range(2):
                    run_units([prompt_unit(h2 + j, qg, j, kvs[j][0], kvs[j][1]) for j in range(2)])
            if not has_s:
                return
            sunits = []
            for s in range(4):
                slot = s % 2
                qc = TP + s * 64
                p0 = (s % 2) * 64
                tbn = 8 + s // 2
                kc0 = TP + (s // 2) * 128
                tiles = []

                def s_new(sb, s=s, qc=qc, kc0=kc0):
                    for h in range(NH):
                        if s % 2 == 0:
                            S.op("pe", lambda e, h=h: e.matmul(PS[sb].ap[0:64, h * 64:(h + 1) * 64], lhsT=BK.ap[:, h, qc:qc + 64], rhs=BQf[:, h, qc:qc + 64], start=True, stop=True),
                                 reads=[BK.reg(h, (qc, qc + 64)), bqreg(h, qc, qc + 64)], writes=[PS[sb].reg((h * 64, (h + 1) * 64))])
                        else:
                            S.op("pe", lambda e, h=h: e.matmul(PS[sb].ap[:, h * 64:(h + 1) * 64], lhsT=BK.ap[:, h, kc0:kc0 + 128], rhs=BQf[:, h, qc:qc + 64], start=True, stop=True),
                                 reads=[BK.reg(h, (kc0, kc0 + 128)), bqreg(h, qc, qc + 64)], writes=[PS[sb].reg((h * 64, (h + 1) * 64))])

                def m_new(E_, p0=p0):
                    S.op("dve", lambda e: e.tensor_tensor(out=E_.ap[p0:p0 + 64, :].rearrange("p (h t) -> p h t", t=64), in0=E_.ap[p0:p0 + 64, :].rearrange("p (h t) -> p h t", t=64),
                                                          in1=cm8_f.ap[p0:p0 + 64], op=ALU.mult),
                         reads=[E_.whole(), cm8_f.whole()], writes=[E_.whole()])

                def pv_new(ob, WT_, p0=p0, tbn=tbn):
                    for h in range(NH):
                        S.op("pe", lambda e, h=h: e.matmul(PS[ob].ap[:, h * 64:(h + 1) * 64], lhsT=BV.ap[p0:p0 + 64, tbn, h * 128:(h + 1) * 128], rhs=WT_.ap[p0:p0 + 64, h * 64:(h + 1) * 64],
                                                          start=False, stop=False, skip_group_check=True),
                             reads=[BV.reg(tbn, (h * 128, (h + 1) * 128)), WT_.whole()], writes=[PS[ob].reg((h * 64, (h + 1) * 64))])
                tiles.append((s_new, m_new, pv_new, 64, p0, 0))
                kt = KT[slot]
                for cg in range(7, -1, -1):
                    kcb, vcb = KC[slot], VC[slot]
                    for bi in (1, 0):
                        def s_c(sb, bi=bi, cg=cg, kcb=kcb, vcb=vcb, s=s, qc=qc, kt=kt, slot=slot):
                            if bi == 1:
                                S.op("pool", lambda e: e.dma_start(out=kcb.ap, in_=ck[l, s, cg * 256:(cg + 1) * 256, :].rearrange("(b p) f -> p b f", p=128)),
                                     writes=[kcb.whole()], dma="kc%d" % slot)
                                S.op("pool", lambda e: e.dma_start(out=vcb.ap, in_=cv[l, s, cg * 256:(cg + 1) * 256, :].rearrange("(b p) f -> p b f", p=128)),
                                     writes=[vcb.whole()], dma="vc%d" % slot)
                                for b2 in range(2):
                                    tb_ = 6 + slot
                                    for h in range(NH):
                                        S.op("pe", lambda e, b2=b2, h=h, tb_=tb_: e.transpose(out=PSB[tb_][:, h * 128:(h + 1) * 128], in_=kcb.ap[:, b2, h * 128:(h + 1) * 128], identity=ident_b.ap),
                                             reads=[kcb.reg(b2, (h * 128, (h + 1) * 128)), ident_b.whole()], writes=[PS[tb_].reg((h * 64, (h + 1) * 64))])
                                    if b2 == 0:
                                        S.op("dve", lambda e, b2=b2, tb_=tb_: e.tensor_copy(out=kt.ap[:, :, b2 * 128:(b2 + 1) * 128], in_=PSB[tb_].rearrange("p (h k) -> p h k", k=128)),
                                             reads=[PS[tb_].whole()], writes=[kt.whole()])
                                    else:
                                        S.op("act", lambda e, b2=b2, tb_=tb_: e.copy(out=kt.ap[:, :, b2 * 128:(b2 + 1) * 128], in_=PSB[tb_].rearrange("p (h k) -> p h k", k=128)),
                                             reads=[PS[tb_].whole()], writes=[kt.whole()])
                            for h in range(NH):
                                S.op("pe", lambda e, h=h: e.matmul(PS[sb].ap[:, h * 64:(h + 1) * 64], lhsT=kt.ap[:, h, bi * 128:(bi + 1) * 128], rhs=BQf[:, h, qc:qc + 64], start=True, stop=True),
                                     reads=[kt.whole(), bqreg(h, qc, qc + 64)], writes=[PS[sb].reg((h * 64, (h + 1) * 64))])

                        def pv_c(ob, WT_, bi=bi, vcb=vcb):
                            for h in range(NH):
                                S.op("pe", lambda e, h=h: e.matmul(PS[ob].ap[:, h * 64:(h + 1) * 64], lhsT=vcb.ap[:, bi, h * 128:(h + 1) * 128], rhs=WT_.ap[:, h * 64:(h + 1) * 64],
                                                                  start=False, stop=False, skip_group_check=True),
                                     reads=[vcb.reg(bi, (h * 128, (h + 1) * 128)), WT_.whole()], writes=[PS[ob].reg((h * 64, (h + 1) * 64))])
                        tiles.append((s_c, None, pv_c, 128, 0, 0))

                def fin(ob, qc=qc):
                    S.op("act", lambda e: e.copy(out=BQf[:, :, qc:qc + 64], in_=PS[ob].ap[:, 0:512].rearrange("p (h t) -> p h t", t=64)),
                         reads=[PS[ob].whole()], writes=[("sb", BQ.off, BQ.off + BQ.nbytes)])
                sunits.append({"S": slot, "C": 2 + slot, "O": 4 + slot, "att": slot, "tiles": tiles, "fin": fin})
                if slot == 1:
                    run_units(sunits)
                    sunits = []

        def merge_stage(l, T, res, res_in=None):
            tl = tiles_of(T)
            assert BV.off == BK.off + BK.nbytes
            MG = carve_at(BK.off, [16, TBT], BF16)
            BQf = Buf("sb", BQ.ap.rearrange("p a b -> p (a b)").rearrange("p (h t) -> p h t", t=TBT), BQ.off, [8, TBT], 2)
            for m2 in range(8):
                wga, sga = load_w(w_in[l], 0, 16, C_GA + m2 * 256, 256)
                wa, sa = load_w(w_a[l], 0, 8, m2 * 256, 256)
                tmps = []
                for j in range(2):
                    m = m2 * 2 + j
                    pg = psset(); mm_fm(pg, wga, sga, j * 128, XN, list(range(16)), T)
                    tg = t5()
                    for ti, (t0, n) in enumerate(tl):
                        S.op("act", lambda e, ti=ti, t0=t0, n=n, m=m, pg=pg, tg=tg: e.activation(out=tg.ap[:, t0:t0 + n], in_=pg[ti].ap[:, 0:n], func=AF.Sigmoid, bias=bgc.ap[:, l, m:m + 1], scale=1.0),
                             reads=[pg[ti].reg((0, n)), bgc.whole()], writes=[tg.reg((t0, t0 + n))])
                    pa = psset(); mm_fm(pa, wa, sa, j * 128, BQf, list(range(8)), T)
                    for ti, (t0, n) in enumerate(tl):
                        S.op("dve", lambda e, ti=ti, t0=t0, n=n, pa=pa, tg=tg: e.tensor_tensor(out=tg.ap[:, t0:t0 + n], in0=pa[ti].ap[:, 0:n], in1=tg.ap[:, t0:t0 + n], op=ALU.mult),
                             reads=[pa[ti].reg((0, n)), tg.reg((t0, t0 + n))], writes=[tg.reg((t0, t0 + n))])
                    tmps.append(tg)
                wgb, sgb = load_w(w_in[l], 0, 16, C_GB + m2 * 256, 256)
                wb, sb_ = load_w(w_b[l], 0, 8, m2 * 256, 256)
                for j in range(2):
                    m = m2 * 2 + j
                    tg = tmps[j]
                    pg = psset(); mm_fm(pg, wgb, sgb, j * 128, XN, list(range(16)), T)
                    t2 = t5()
                    for ti, (t0, n) in enumerate(tl):
                        S.op("act", lambda e, ti=ti, t0=t0, n=n, m=m, pg=pg, t2=t2: e.activation(out=t2.ap[:, t0:t0 + n], in_=pg[ti].ap[:, 0:n], func=AF.Sigmoid, bias=bgc.ap[:, l, 16 + m:17 + m], scale=1.0),
                             reads=[pg[ti].reg((0, n)), bgc.whole()], writes=[t2.reg((t0, t0 + n))])
                    pb = psset(); mm_fm(pb, wb, sb_, j * 128, BU, list(range(8)), T)
                    for ti, (t0, n) in enumerate(tl):
                        S.op("dve", lambda e, ti=ti, t0=t0, n=n, pb=pb, t2=t2: e.tensor_tensor(out=t2.ap[:, t0:t0 + n], in0=pb[ti].ap[:, 0:n], in1=t2.ap[:, t0:t0 + n], op=ALU.mult),
                             reads=[pb[ti].reg((0, n)), t2.reg((t0, t0 + n))], writes=[t2.reg((t0, t0 + n))])
                        S.op("dve", lambda e, t0=t0, n=n, m=m, tg=tg, t2=t2: e.tensor_tensor(out=MG.ap[:, m, t0:t0 + n], in0=tg.ap[:, t0:t0 + n], in1=t2.ap[:, t0:t0 + n], op=ALU.add),
                             reads=[tg.reg((t0, t0 + n)), t2.reg((t0, t0 + n))], writes=[MG.reg(m, (t0, t0 + n))])
            for m2 in range(8):
                wo, so = load_w(w_out[l], 0, 16, m2 * 256, 256)
                for j in range(2):
                    m = m2 * 2 + j
                    po = psset(); mm_fm(po, wo, so, j * 128, MG, list(range(16)), T)
                    xt = t5()
                    rsrc = res if res_in is None else res_in
                    S.op("sp", lambda e, xt=xt, m=m, rsrc=rsrc: e.dma_start(out=xt.ap[:, 0:T], in_=rsrc[m * 128:(m + 1) * 128, 0:T]),
                         reads=([rkey(res, m)] if res_in is None else []), writes=[xt.reg((0, T))], dma=t5key(xt))
                    for ti, (t0, n) in enumerate(tl):
                        S.op("dve", lambda e, ti=ti, t0=t0, n=n, po=po, xt=xt: e.tensor_tensor(out=xt.ap[:, t0:t0 + n], in0=po[ti].ap[:, 0:n], in1=xt.ap[:, t0:t0 + n], op=ALU.add),
                             reads=[po[ti].reg((0, n)), xt.reg((t0, t0 + n))], writes=[xt.reg((t0, t0 + n))])
                    S.op("sp", lambda e, xt=xt, m=m: e.dma_start(out=res[m * 128:(m + 1) * 128, 0:T], in_=xt.ap[:, 0:T]),
                         reads=[xt.reg((0, T))], writes=[rkey(res, m)], dma=t5key(xt))

        def ffn_stage(l, T, res, store_res=True):
            tl = tiles_of(T)
            def gate_up(fg):
                ag = ACTG[fg % 2]
                for half in range(2):
                    wg, sg_ = load_w(w_gu[l], 0, 16, fg * 512 + half * 256, 256)
                    wu, su_ = load_w(w_gu[l], 0, 16, DFF + fg * 512 + half * 256, 256)
                    for j in range(2):
                        jj = half * 2 + j
                        pg = psset(); mm_fm(pg, wg, sg_, j * 128, XN, list(range(16)), T)
                        tg = t5()
                        for ti, (t0, n) in enumerate(tl):
                            S.op("act", lambda e, ti=ti, t0=t0, n=n, pg=pg, tg=tg: e.activation(out=tg.ap[:, t0:t0 + n], in_=pg[ti].ap[:, 0:n], func=AF.Silu),
                                 reads=[pg[ti].reg((0, n))], writes=[tg.reg((t0, t0 + n))])
                        pu = psset(); mm_fm(pu, wu, su_, j * 128, XN, list(range(16)), T)
                        for ti, (t0, n) in enumerate(tl):
                            S.op("dve", lambda e, ti=ti, t0=t0, n=n, pu=pu, tg=tg, jj=jj, ag=ag: e.tensor_tensor(out=ag.ap[:, jj, t0:t0 + n], in0=pu[ti].ap[:, 0:n], in1=tg.ap[:, t0:t0 + n], op=ALU.mult),
                                 reads=[pu[ti].reg((0, n)), tg.reg((t0, t0 + n))], writes=[ag.reg(jj, (t0, t0 + n))])
            def down(fg):
                ag = ACTG[fg % 2]
                for mh in range(2):
                    wd, sd = load_w(w_dn[l], fg * 512, 4, mh * 1024, 1024)
                    for j in range(8):
                        m = mh * 8 + j
                        pd = psset(); mm_fm(pd, wd, sd, j * 128, ag, list(range(4)), T)
                        for ti, (t0, n) in enumerate(tl):
                            if fg == 0:
                                S.op("act", lambda e, ti=ti, t0=t0, n=n, pd=pd, m=m: e.copy(out=ACC.ap[:, m, t0:t0 + n], in_=pd[ti].ap[:, 0:n]),
                                     reads=[pd[ti].reg((0, n))], writes=[ACC.reg(m, (t0, t0 + n))])
                            else:
                                S.op("dve", lambda e, ti=ti, t0=t0, n=n, pd=pd, m=m: e.tensor_tensor(out=ACC.ap[:, m, t0:t0 + n], in0=pd[ti].ap[:, 0:n], in1=ACC.ap[:, m, t0:t0 + n], op=ALU.add),
                                     reads=[pd[ti].reg((0, n)), ACC.reg(m, (t0, t0 + n))], writes=[ACC.reg(m, (t0, t0 + n))])
            gate_up(0)
            for fg in range(11):
                if fg + 1 < 11:
                    gate_up(fg + 1)
                down(fg)
            xts = {}

            def ld(m):
                xt = t5()
                xts[m] = xt
                S.op("sp", lambda e, xt=xt, m=m: e.dma_start(out=xt.ap[:, 0:T], in_=res[m * 128:(m + 1) * 128, 0:T]),
                     reads=[rkey(res, m)], writes=[xt.reg((0, T))], dma=t5key(xt))
            for m in range(4):
                ld(m)
            for m in range(NCH):
                xt = xts[m]
                S.op("dve", lambda e, xt=xt, m=m: e.tensor_tensor(out=ACC.ap[:, m, 0:T], in0=ACC.ap[:, m, 0:T], in1=xt.ap[:, 0:T], op=ALU.add),
                     reads=[ACC.reg(m, (0, T)), xt.reg((0, T))], writes=[ACC.reg(m, (0, T))])
                if m + 4 < NCH:
                    ld(m + 4)
            if store_res:
                for m in range(NCH):
                    S.op("sp", lambda e, m=m: e.dma_start(out=res[m * 128:(m + 1) * 128, 0:T], in_=ACC.ap[:, m, 0:T]),
                         reads=[ACC.reg(m, (0, T))], writes=[rkey(res, m)], dma="accs%d" % (m % 4))

        def final_stage(res, T):
            norm_stats(res, T, True)
            NTB = T // 128
            yv = y_o.rearrange("(tb p) f -> p tb f", p=128)
            for c in range(NCH):
                xt = t5()
                S.op("dve", lambda e, xt=xt, c=c: e.scalar_tensor_tensor(out=xt.ap[:, 0:T], in0=ACC.ap[:, c, 0:T], scalar=gcol.ap[:, 4, c:c + 1],
                                                                       in1=RSTD.ap[:, 0:T], op0=ALU.mult, op1=ALU.mult),
                     reads=[ACC.reg(c, (0, T)), gcol.whole(), RSTD.reg((0, T))], writes=[xt.reg((0, T))])
                S.op("sp", lambda e, xt=xt, c=c: e.dma_start(out=y_o[c * 128:(c + 1) * 128, 0:T], in_=xt.ap[:, 0:T]),
                     reads=[xt.reg((0, T))], dma=t5key(xt), is_out=True)

        class Stop(Exception):
            pass

        def dump(buf, name):
            dt_ = F32 if buf.es == 4 else BF16
            t = nc.dram_tensor("dbg_" + name, [128] + list(buf.shape), dt_, kind="ExternalOutput").ap()
            S.op("sp", lambda e: e.dma_start(out=t, in_=buf.ap), reads=[buf.whole()], dma="dbg_" + name, is_out=True)

        def chk(name, bufs):
            if DBG.get("stop") == name or name in DBG.get("dumps", ()):
                for nm, bf in bufs:
                    dump(bf, name + "_" + nm)
            if DBG.get("stop") == name:
                raise Stop()

        def layer(l, T, res, grpA, last=False, xin=None):
            tag = "%s%d_" % ("A" if grpA else "B", l)
            norm_stats(res, T, True)
            norm_apply(res, T, l)
            chk(tag + "norm", [("xn", XN), ("rstd", RSTD)])
            sgu_stage(l, T, write_sv=not grpA)
            chk(tag + "sgu", [("bu", BU), ("bq", BQ)])
            qkv_stage(l, T, grpA)
            chk(tag + "qkv", [("bq", BQ), ("bk", BK), ("bv", BV)])
            attention_stage(l, T, use_prev=not grpA, has_s=not grpA)
            chk(tag + "att", [("bq", BQ)])
            merge_stage(l, T, res, res_in=xin)
            chk(tag + "merge", [("bk", BK), ("bv", BV)])
            norm_stats(res, T, False)
            norm_apply(res, T, 2 + l)
            ffn_stage(l, T, res, store_res=not last)
            chk(tag + "ffn", [])

        try:
            load_x(xa, TP)
            layer(0, TP, resA, True, xin=xa)
            norm_stats(resA, TP, True)
            norm_apply(resA, TP, 1)
            qkv_stage(1, TP, True, need_q=False)
            chk("A1_kv", [])
            load_x(xb, TBT)
            layer(0, TBT, resB, False, xin=xb)
            layer(1, TBT, resB, False, last=True)
            final_stage(resB, TBT)
        except Stop:
            pass
        S.finish()
        S.emit(nc)
    return nc


_NC_CACHE = {}


def kernel(x_prompt, x_sample, cache_k, cache_v, norm_mix, w_in, b_gate, sgu_norm, w_spatial, b_spatial,
           w_branch_a, w_branch_b, w_out, norm_ffn, w_gate_up, w_down, norm_final):
    f = lambda a: np.ascontiguousarray(np.asarray(a, dtype=np.float32))
    x_prompt, x_sample, cache_k, cache_v = f(x_prompt), f(x_sample), f(cache_k), f(cache_v)
    shared = {
        "norm_mix": f(norm_mix), "w_in": f(w_in), "b_gate": f(b_gate), "sgu_norm": f(sgu_norm),
        "w_spatial": f(w_spatial), "b_spatial": f(b_spatial), "w_branch_a": f(w_branch_a),
        "w_branch_b": f(w_branch_b), "w_out": f(w_out), "norm_ffn": f(norm_ffn),
        "w_gate_up": f(w_gate_up), "w_down": f(w_down), "norm_final": f(norm_final),
    }
    if "nc" not in _NC_CACHE:
        _NC_CACHE["nc"] = build_program()
    nc = _NC_CACHE["nc"]
    ncores = DBG.get("ncores", 8)
    in_maps = []
    for c in range(ncores):
        c = c + DBG.get('core0', 0)
        b, hf = c // 2, c % 2
        m = dict(shared)
        m["xa"] = np.ascontiguousarray(x_prompt[b, 0:TP].T)
        m["xb"] = np.ascontiguousarray(np.concatenate(
            [x_prompt[b, hf * TP:(hf + 1) * TP], x_sample[4 * c:4 * c + 4].reshape(TS, D)], axis=0).T)
        m["flag"] = np.full((128, 1), float(hf), np.float32)
        m["ck"] = np.ascontiguousarray(cache_k[:, 4 * c:4 * c + 4].reshape(2, 4, 2048, SBW))
        m["cv"] = np.ascontiguousarray(cache_v[:, 4 * c:4 * c + 4].reshape(2, 4, 2048, SBW))
        for nm in DBG.get('shrink', ()):
            m.pop(nm, None)
        in_maps.append(m)
    res = run_bass_kernel_spmd(nc, in_maps, core_ids=list(range(ncores)))
    r = res.results
    if DBG:
        return r
    B, SEQ = x_prompt.shape[0], x_prompt.shape[1]
    y_p = np.empty((B, SEQ, D), np.float32)
    y_s = np.empty((32, 64, D), np.float32)
    k_p = np.empty((2, B, SEQ, NH, HD), np.float32)
    v_p = np.empty((2, B, SEQ, NH, HD), np.float32)
    k_s = np.empty((2, 32, 64, NH, HD), np.float32)
    v_s = np.empty((2, 32, 64, NH, HD), np.float32)
    sv_s = np.empty((2, 32, 64, SBW), np.float32)
    for c in range(8):
        b, hf = c // 2, c % 2
        y = np.asarray(r[c]["y"]).T; ko = np.asarray(r[c]["ko"]).transpose(0, 2, 1); vo = np.asarray(r[c]["vo"]); sv = np.asarray(r[c]["sv"])
        y_p[b, hf * TP:(hf + 1) * TP] = y[:TP]
        y_s[4 * c:4 * c + 4] = y[TP:].reshape(4, 64, D)
        k_p[:, b, hf * TP:(hf + 1) * TP] = ko[:, :TP].reshape(2, TP, NH, HD)
        v_p[:, b, hf * TP:(hf + 1) * TP] = vo[:, :TP].reshape(2, TP, NH, HD)
        k_s[:, 4 * c:4 * c + 4] = ko[:, TP:].reshape(2, 4, 64, NH, HD)
        v_s[:, 4 * c:4 * c + 4] = vo[:, TP:].reshape(2, 4, 64, NH, HD)
        sv_s[:, 4 * c:4 * c + 4] = sv.reshape(2, 4, 64, SBW)
    return (y_p, y_s, k_p, v_p, k_s, v_s, sv_s)
```

```python
import contextlib
import numpy as np
import concourse.bass as bass
import concourse.mybir as mybir
from concourse.bass_utils import run_bass_kernel_spmd

F32 = mybir.dt.float32
BF16 = mybir.dt.bfloat16
AF = mybir.ActivationFunctionType
ALU = mybir.AluOpType

CELL = 128


class Op:
    __slots__ = ("eng", "fn", "deps", "needed", "sigidx", "dsem", "dval", "tag")

    def __init__(self, eng, fn, tag=""):
        self.eng = eng
        self.fn = fn
        self.deps = []
        self.needed = False
        self.sigidx = 0
        self.dsem = None
        self.dval = 0
        self.tag = tag


class Sched:
    ENGS = ("pe", "act", "dve", "pool", "sp")

    def __init__(self):
        self.ops = {e: [] for e in self.ENGS}
        self.lastw = {}
        self.readers = {}
        self.dma_counts = {}
        self.last_dma = {}
        self.out_dmas = []

    def op(self, eng, fn, reads=(), writes=(), dma=None, tag="", is_out=False):
        o = Op(eng, fn, tag)
        deps = {}
        def _bank(acc):
            sp_, lo_, hi_ = acc
            if sp_ == "ps":
                lo_ = lo_ // 2048 * 2048
                hi_ = (hi_ + 2047) // 2048 * 2048
            return sp_, lo_, hi_
        reads = [_bank(a) for a in reads]
        writes = [_bank(a) for a in writes]
        for (space, lo, hi) in reads:
            lw = self.lastw.setdefault(space, {})
            rd = self.readers.setdefault(space, {})
            for c in range(lo // CELL, (hi - 1) // CELL + 1):
                w = lw.get(c)
                if w is not None:
                    deps[id(w)] = w
                rl = rd.setdefault(c, [])
                if space == "ps":
                    for r in rl:
                        if r.eng != eng:
                            deps[id(r)] = r
                rl.append(o)
        for (space, lo, hi) in writes:
            lw = self.lastw.setdefault(space, {})
            rd = self.readers.setdefault(space, {})
            for c in range(lo // CELL, (hi - 1) // CELL + 1):
                w = lw.get(c)
                if w is not None:
                    deps[id(w)] = w
                rl = rd.get(c)
                if rl:
                    lastc = {}
                    for r in rl:
                        if r.dsem is None:
                            lastc[r.eng] = r
                        else:
                            deps[id(r)] = r
                    for r in lastc.values():
                        deps[id(r)] = r
                lw[c] = o
                rd[c] = []
        for d in deps.values():
            if d is o:
                continue
            if d.dsem is None and d.eng == "pe" and eng == "pe" and dma is None:
                continue
            d.needed = True
            o.deps.append(d)
        if dma is not None:
            o.dsem = dma
            prev = self.last_dma.get(dma)
            if prev is not None and all(d is not prev for d in o.deps):
                o.deps.append(prev)
            self.last_dma[dma] = o
            self.dma_counts[dma] = self.dma_counts.get(dma, 0) + 16
            o.dval = self.dma_counts[dma]
            if is_out:
                self.out_dmas.append(o)
        self.ops[eng].append(o)
        return o

    def finish(self):
        o = Op("sp", None, "final")
        o.deps = list(self.out_dmas)
        self.ops["sp"].append(o)

    def emit(self, nc):
        with contextlib.ExitStack() as st:
            esem = {e: st.enter_context(nc.semaphore("s_" + e)) for e in self.ENGS}
            dsem = {k: st.enter_context(nc.semaphore("d_" + str(k))) for k in self.dma_counts}
            for e in self.ENGS:
                n = 0
                for o in self.ops[e]:
                    if o.dsem is None and o.needed:
                        n += 1
                        o.sigidx = n
            block = st.enter_context(nc.Block())

            def mk(e):
                def body(eng):
                    waited = {}
                    for o in self.ops[e]:
                        need = {}
                        for d in o.deps:
                            if d.dsem is not None:
                                s, v = dsem[d.dsem], d.dval
                            else:
                                s, v = esem[d.eng], d.sigidx
                            k = id(s)
                            if v > waited.get(k, 0) and v > need.get(k, (None, 0))[1]:
                                need[k] = (s, v)
                        for k, (s, v) in need.items():
                            eng.wait_ge(s, v)
                            waited[k] = v
                        if o.fn is None:
                            continue
                        inst = o.fn(eng)
                        if o.dsem is not None:
                            inst.then_inc(dsem[o.dsem], 16)
                        elif o.needed:
                            inst.then_inc(esem[e], 1)
                return body

            block.tensor(mk("pe"))
            block.scalar(mk("act"))
            block.vector(mk("dve"))
            block.gpsimd(mk("pool"))
            block.sync(mk("sp"))


class Buf:
    def __init__(self, space, ap, off, shape, es):
        self.space, self.ap, self.off, self.shape, self.es = space, ap, off, tuple(shape), es
        n = es
        for s in shape:
            n *= s
        self.nbytes = n
        st, strides = es, []
        for s in reversed(self.shape):
            strides.append(st)
            st *= s
        self.strides = list(reversed(strides))

    def whole(self):
        return (self.space, self.off, self.off + self.nbytes)

    def reg(self, *idx):
        lo = hi = 0
        for i, s in enumerate(self.shape):
            ix = idx[i] if i < len(idx) else None
            if ix is None:
                a, b = 0, s
            elif isinstance(ix, int):
                a, b = ix, ix + 1
            else:
                a, b = ix
            lo += a * self.strides[i]
            hi += (b - 1) * self.strides[i]
        return (self.space, self.off + lo, self.off + hi + self.es)


D = 2048
NCH = 16
NH = 8
HD = 128
SBW = 1024
DFF = 5632
NFC = 44
INC = 9216
TP = 1024
TS = 256
TBT = TP + TS
C_Q, C_K, C_V, C_U, C_VS, C_GA, C_GB = 0, 1024, 2048, 3072, 4096, 5120, 7168
EPS = 1e-6
SCALE = float(HD) ** -0.5
DBG = {}


def dkey(name, i=0):
    return ("dr_" + name, i * CELL, i * CELL + 1)


def build_program():
    nc = bass.Bass("TRN2", target_bir_lowering=False)
    S = Sched()

    def din(name, shape, dt=F32):
        if name in DBG.get("shrink", ()):
            return nc.dram_tensor(name, list(shape), dt).ap()
        return nc.dram_tensor(name, list(shape), dt, kind="ExternalInput").ap()

    def dout(name, shape, dt=F32):
        return nc.dram_tensor(name, list(shape), dt, kind="ExternalOutput").ap()

    xa = din("xa", [D, TP])
    xb = din("xb", [D, TBT])
    flag_d = din("flag", [128, 1])
    ck = din("ck", [2, 4, 2048, SBW])
    cv = din("cv", [2, 4, 2048, SBW])
    norm_mix = din("norm_mix", [2, D])
    w_in = din("w_in", [2, D, INC])
    b_gate = din("b_gate", [2, 2 * D])
    sgu_norm = din("sgu_norm", [2, SBW])
    w_spatial = din("w_spatial", [2, 8, 128, 128])
    b_spatial = din("b_spatial", [2, 8, 128])
    w_a = din("w_branch_a", [2, SBW, D])
    w_b = din("w_branch_b", [2, SBW, D])
    w_out = din("w_out", [2, D, D])
    norm_ffn = din("norm_ffn", [2, D])
    w_gu = din("w_gate_up", [2, D, 2 * DFF])
    w_dn = din("w_down", [2, DFF, D])
    norm_final = din("norm_final", [D])

    y_o = dout("y", [D, TBT])
    k_o = dout("ko", [2, SBW, TBT])
    v_o = dout("vo", [2, TBT, SBW])
    sv_o = dout("sv", [2, TS, SBW])

    skind = "ExternalOutput" if DBG else "Internal"
    resA = nc.dram_tensor("resA", [D, TP], F32, kind=skind).ap()
    resB = nc.dram_tensor("resB", [D, TBT], F32, kind=skind).ap()
    pk = nc.dram_tensor("pk", [2, SBW, TP], BF16, kind=skind).ap()
    pv = nc.dram_tensor("pv", [2, TP, SBW], BF16, kind=skind).ap()

    with contextlib.ExitStack() as st:
        AW = 53200
        arena_t = st.enter_context(nc.sbuf_tensor("arena", [128, AW], F32))
        arena = arena_t[:]
        ps_t = [st.enter_context(nc.psum_tensor("ps%d" % i, [128, 512], F32)) for i in range(8)]
        PS = [Buf("ps", ps_t[i][:], i * 2048, [512], 4) for i in range(8)]
        PSB = [ps_t[i][:].bitcast(BF16) for i in range(8)]

        cur = [0]

        def carve_at(off, shape, dt):
            es = 4 if dt == F32 else 2
            n = es
            for s_ in shape:
                n *= s_
            assert off % 4 == 0 and n % 4 == 0
            assert off + n <= AW * 4, (off, n)
            ap = arena[:, off // 4:(off + n) // 4]
            if dt != F32:
                ap = ap.bitcast(dt)
            if len(shape) == 2:
                ap = ap.rearrange("p (a b) -> p a b", b=shape[1])
            elif len(shape) == 3:
                ap = ap.rearrange("p (a b c) -> p a b c", b=shape[1], c=shape[2])
            return Buf("sb", ap, off, shape, es)

        def carve(shape, dt):
            es = 4 if dt == F32 else 2
            n = es
            for s_ in shape:
                n *= s_
            n = (n + 127) // 128 * 128
            b = carve_at(cur[0], shape, dt)
            cur[0] += n
            return b

        ident_b = carve([128], BF16)
        tri_b = carve([128], BF16)
        m2_b = carve([128], BF16)
        zero_b = carve([128], BF16)
        cm_f = carve([128], F32)
        cm8_f = carve([8, 64], F32)
        ident_f = carve([128], F32)
        ones_b = carve([128], BF16)
        flag_s = carve([32], F32)
        gcol = carve([5, 16], F32)
        bgc = carve([2, 32], F32)
        XN = carve([16, TBT], BF16)
        WS = [carve([4096], BF16) for _ in range(3)]
        RSTD = carve([TBT], F32)
        T5 = [carve([TBT], F32) for _ in range(4)]
        PH = cur[0]
        BU = carve([8, TBT], BF16)
        BQ = carve([10, 1024], BF16)
        BK = carve([8, TBT], BF16)
        BV = carve([10, 1024], BF16)
        MX = cur[0]
        sg = BK.off
        WmT = carve_at(sg, [8, 128], BF16); sg += 2048
        WbT = carve_at(sg, [8, 128], BF16); sg += 2048
        bSp = carve_at(sg, [8, 128], F32); sg += 4096
        bSs = carve_at(sg, [8, 128], F32); sg += 4096
        sguG = carve_at(sg, [1024], F32); sg += 4096
        Wsp_f = carve_at(sg, [8, 128], F32); sg += 4096
        Wbd_f = carve_at(sg, [8, 128], F32); sg += 4096
        VSF = [carve_at(sg + i * 4096, [1024], F32) for i in range(2)]; sg += 8192
        assert sg <= BV.off + BV.nbytes
        KPV = [(carve([1024], BF16), carve([8, 128], BF16)) for _ in range(2)]
        KC = [carve([2, 1024], BF16) for _ in range(2)]
        VC = [carve([2, 1024], BF16) for _ in range(2)]
        KT = [carve([8, 256], BF16) for _ in range(2)]
        mixer_end = cur[0]
        KVST = [carve_at(KC[0].off + i * 2048, [512], F32) for i in range(4)]
        a0 = T5[0].off
        ATT = []
        for i in range(2):
            E_ = carve_at(a0, [512], F32); a0 += 2048
            EN_ = carve_at(a0, [512], F32); a0 += 2048
            SP_ = carve_at(a0, [512], BF16); a0 += 1024
            WT_ = carve_at(a0, [512], BF16); a0 += 1024
            ATT.append((E_, EN_, SP_, WT_))
        assert a0 <= T5[3].off
        E2 = [carve_at(T5[3].off + i * 2048, [512], F32) for i in range(2)]
        ACC = carve_at(PH, [16, TBT], F32)
        ACTG = [carve_at(PH + ACC.nbytes + i * 4 * TBT * 2, [4, TBT], BF16) for i in range(2)]
        ffn_end = PH + ACC.nbytes + 2 * 4 * TBT * 2
        print("arena bytes: mixer_end", mixer_end, "ffn_end", ffn_end, "cap", AW * 4)
        assert max(mixer_end, ffn_end) <= AW * 4

        rr = {"t5": 0, "ws": 0, "psd": 0, "ps1": 0}

        def t5():
            i = rr["t5"]; rr["t5"] = (i + 1) % 4
            return T5[i]

        def psset():
            i = rr["psd"]; rr["psd"] = (i + 1) % 2
            return [PS[3 * i], PS[3 * i + 1], PS[3 * i + 2]]

        def ps1():
            i = rr["ps1"]; rr["ps1"] = (i + 1) % 8
            return i

        def tiles_of(T):
            out, t0 = [], 0
            while t0 < T:
                n = min(512, T - t0)
                out.append((t0, n)); t0 += n
            return out

        def load_w(src2d, r0, kc, c0, ncols):
            i = rr["ws"]; rr["ws"] = (i + 1) % 3
            slot = WS[i]
            assert kc * ncols <= 4096
            view = slot.ap[:, 0:kc * ncols].rearrange("p (k n) -> p k n", n=ncols)
            src = src2d[r0:r0 + kc * 128, c0:c0 + ncols].rearrange("(k p) n -> p k n", p=128)
            S.op("pool", lambda e: e.dma_start(out=view, in_=src), writes=[slot.whole()], dma="w%d" % i)
            return view, slot

        def mm_fm(pset, wview, wslot, col, rhsbuf, kcs, T, first=True, last=True, rhs_k0=0):
            tl = tiles_of(T)
            nk = len(kcs)
            for ki, k in enumerate(kcs):
                for ti, (t0, n) in enumerate(tl):
                    S.op("pe", (lambda e, ti=ti, t0=t0, n=n, ki=ki, k=k: e.matmul(
                        pset[ti].ap[:, 0:n], lhsT=wview[:, ki, col:col + 128], rhs=rhsbuf.ap[:, rhs_k0 + k, t0:t0 + n],
                        start=(first and ki == 0), stop=(last and ki == nk - 1))),
                        reads=[wslot.whole(), rhsbuf.reg(rhs_k0 + k, (t0, t0 + n))], writes=[pset[ti].reg((0, n))])

        S.op("pool", lambda e: e.memset(ident_b.ap, 1.0), writes=[ident_b.whole()])
        S.op("pool", lambda e: e.affine_select(out=ident_b.ap, in_=ident_b.ap, pattern=[[-1, 128]], compare_op=ALU.is_equal,
                                               fill=0.0, base=0, channel_multiplier=1), reads=[ident_b.whole()], writes=[ident_b.whole()])
        S.op("pool", lambda e: e.memset(ident_f.ap, 1.0), writes=[ident_f.whole()])
        S.op("pool", lambda e: e.affine_select(out=ident_f.ap, in_=ident_f.ap, pattern=[[-1, 128]], compare_op=ALU.is_equal,
                                               fill=0.0, base=0, channel_multiplier=1), reads=[ident_f.whole()], writes=[ident_f.whole()])
        S.op("pool", lambda e: e.memset(tri_b.ap, 1.0), writes=[tri_b.whole()])
        S.op("pool", lambda e: e.affine_select(out=tri_b.ap, in_=tri_b.ap, pattern=[[-1, 128]], compare_op=ALU.is_ge,
                                               fill=0.0, base=0, channel_multiplier=1), reads=[tri_b.whole()], writes=[tri_b.whole()])
        S.op("pool", lambda e: e.memset(m2_b.ap, 1.0), writes=[m2_b.whole()])
        S.op("pool", lambda e: e.affine_select(out=m2_b.ap, in_=m2_b.ap, pattern=[[1, 128]], compare_op=ALU.is_gt,
                                               fill=0.0, base=0, channel_multiplier=-1), reads=[m2_b.whole()], writes=[m2_b.whole()])
        S.op("pool", lambda e: e.memset(zero_b.ap, 0.0), writes=[zero_b.whole()])
        S.op("pool", lambda e: e.memset(ones_b.ap, 1.0), writes=[ones_b.whole()])
        S.op("pool", lambda e: e.memset(cm_f.ap, 1.0), writes=[cm_f.whole()])
        S.op("pool", lambda e: e.affine_select(out=cm_f.ap, in_=cm_f.ap, pattern=[[1, 128]], compare_op=ALU.is_gt,
                                               fill=0.0, base=0, channel_multiplier=-1), reads=[cm_f.whole()], writes=[cm_f.whole()])
        S.op("pool", lambda e: e.memset(cm8_f.ap, 1.0), writes=[cm8_f.whole()])
        for half in range(2):
            S.op("pool", lambda e, half=half: e.affine_select(
                out=cm8_f.ap[half * 64:(half + 1) * 64], in_=cm8_f.ap[half * 64:(half + 1) * 64], pattern=[[0, 8], [1, 64]],
                compare_op=ALU.is_gt, fill=0.0, base=0, channel_multiplier=-1), reads=[cm8_f.whole()], writes=[cm8_f.whole()])
        S.op("sp", lambda e: e.dma_start(out=flag_s.ap[:, 0:1], in_=flag_d[:, :]), writes=[flag_s.whole()], dma="c0")
        with nc.allow_non_contiguous_dma(reason="tiny per-feature vectors"):
            for i, src in enumerate([norm_mix[0], norm_mix[1], norm_ffn[0], norm_ffn[1], norm_final]):
                S.op("sp", lambda e, i=i, src=src: e.dma_start(out=gcol.ap[:, i, :], in_=src.rearrange("(c p) -> p c", p=128), allow_slow_non_contiguous=True),
                     writes=[gcol.whole()], dma="c0")
            for l in range(2):
                S.op("sp", lambda e, l=l: e.dma_start(out=bgc.ap[:, l, :], in_=b_gate[l].rearrange("(c p) -> p c", p=128), allow_slow_non_contiguous=True),
                     writes=[bgc.whole()], dma="c0")

        MISC = AW * 4 - 1664
        assert max(mixer_end, ffn_end) <= MISC
        eps_c = carve_at(MISC, [16], F32)
        rstdv = carve_at(MISC + 64, [16], F32)
        ssqv = carve_at(MISC + 128, [12, 4], F32)
        S.op("pool", lambda e: e.memset(eps_c.ap, EPS), writes=[eps_c.whole()])
        RESN = {id(resA): "resA", id(resB): "resB"}

        def rkey(res, c):
            return dkey(RESN[id(res)], c)

        def t5key(b):
            return "t5_%d" % T5.index(b)

        def load_x(xsrc, T):
            for c in range(NCH):
                S.op("sp", lambda e, c=c: e.dma_start(out=ACC.ap[:, c, 0:T], in_=xsrc[c * 128:(c + 1) * 128, 0:T]),
                     writes=[ACC.reg(c, (0, T))], dma="accl%d" % (c % 8))

        def norm_stats(res, T, from_sbuf):
            tl = tiles_of(T)
            if not from_sbuf:
                for c in range(NCH):
                    S.op("sp", lambda e, c=c: e.dma_start(out=ACC.ap[:, c, 0:T], in_=res[c * 128:(c + 1) * 128, 0:T]),
                         reads=[rkey(res, c)], writes=[ACC.reg(c, (0, T))], dma="accl%d" % (c % 8))
            pset = psset()
            for c in range(NCH):
                sqb = t5()
                sq_ap = sqb.ap.bitcast(BF16)
                S.op("act", lambda e, c=c, sq_ap=sq_ap: e.activation(out=sq_ap[:, 0:T], in_=ACC.ap[:, c, 0:T], func=AF.Square),
                     reads=[ACC.reg(c, (0, T))], writes=[sqb.reg((0, T // 2))])
                for ti, (t0, n) in enumerate(tl):
                    S.op("pe", lambda e, ti=ti, t0=t0, n=n, c=c, sq_ap=sq_ap: e.matmul(
                        pset[ti].ap[:, 0:n], lhsT=ones_b.ap, rhs=sq_ap[:, t0:t0 + n], start=(c == 0), stop=(c == NCH - 1)),
                        reads=[sqb.reg((0, T // 2)), ones_b.whole()], writes=[pset[ti].reg((0, n))])
            for ti, (t0, n) in enumerate(tl):
                S.op("act", lambda e, ti=ti, t0=t0, n=n: e.activation(out=RSTD.ap[:, t0:t0 + n], in_=pset[ti].ap[:, 0:n], func=AF.Ln, bias=eps_c.ap[:, 0:1], scale=1.0 / D),
                     reads=[pset[ti].reg((0, n)), eps_c.whole()], writes=[RSTD.reg((t0, t0 + n))])
                S.op("act", lambda e, t0=t0, n=n: e.activation(out=RSTD.ap[:, t0:t0 + n], in_=RSTD.ap[:, t0:t0 + n], func=AF.Exp, scale=-0.5),
                     reads=[RSTD.reg((t0, t0 + n))], writes=[RSTD.reg((t0, t0 + n))])

        def norm_apply(res, T, gi):
            for c in range(NCH):
                S.op("dve", lambda e, c=c: e.scalar_tensor_tensor(out=XN.ap[:, c, 0:T], in0=ACC.ap[:, c, 0:T], scalar=gcol.ap[:, gi, c:c + 1],
                                                                in1=RSTD.ap[:, 0:T], op0=ALU.mult, op1=ALU.mult),
                     reads=[ACC.reg(c, (0, T)), gcol.whole(), RSTD.reg((0, T))], writes=[XN.reg(c, (0, T))])

        def sgu_stage(l, T, write_sv):
            NTB = T // 128
            S.op("sp", lambda e: e.dma_start(out=sguG.ap, in_=sgu_norm[l].partition_broadcast(128)), writes=[sguG.whole()], dma="c1")
            tag = 'A%d_' % l if T == TP else 'B%d_' % l
            chk(tag + 'sgu_c', [('wmt', WmT), ('wbt', WbT), ('bsp', bSp), ('bss', bSs), ('sgug', sguG)])
            tl = tiles_of(T)
            for g4 in range(4):
                wv, wsl = load_w(w_in[l], 0, 16, C_U + g4 * 256, 256)
                chk(tag + 'sgu_w', [('ws', wsl)])
                for j in range(2):
                    g = g4 * 2 + j
                    pset = psset()
                    mm_fm(pset, wv, wsl, j * 128, XN, list(range(16)), T)
                    if DBG.get("stop") == tag + 'sgu_m':
                        tt = t5()
                        S.op("dve", lambda e: e.tensor_copy(out=tt.ap[:, 0:512], in_=pset[0].ap[:, 0:512]), reads=[pset[0].whole()], writes=[tt.whole()])
                        chk(tag + 'sgu_m', [('ps', tt)])
                    for ti, (t0, n) in enumerate(tl):
                        S.op("act", lambda e, ti=ti, t0=t0, n=n, g=g, pset=pset: e.activation(out=BU.ap[:, g, t0:t0 + n], in_=pset[ti].ap[:, 0:n], func=AF.Gelu_apprx_tanh),
                             reads=[pset[ti].reg((0, n))], writes=[BU.reg(g, (t0, t0 + n))])
                    if g == DBG.get('gstop', -1):
                        chk(tag + 'sgu_g', [('bu', BU)])
            chk(tag + 'sgu_u', [('bu', BU)])
            junk = T5[3]
            for wt in range(4):
                wv, wsl = load_w(w_in[l], 0, 16, C_VS + wt * 256, 256)
                for tb in range(NTB):
                    b = ps1()
                    for k in range(16):
                        S.op("pe", lambda e, b=b, k=k, tb=tb, wv=wv: e.matmul(PS[b].ap[:, 0:256], lhsT=XN.ap[:, k, tb * 128:(tb + 1) * 128], rhs=wv[:, k, :],
                                                                           start=(k == 0), stop=(k == 15)),
                             reads=[XN.reg(k, (tb * 128, (tb + 1) * 128)), wsl.whole()], writes=[PS[b].reg((0, 256))])
                    if tb < 8:
                        dst, dreg = BQ.ap[:, tb, wt * 256:(wt + 1) * 256], BQ.reg(tb, (wt * 256, (wt + 1) * 256))
                    else:
                        dst, dreg = VSF[tb - 8].ap[:, wt * 256:(wt + 1) * 256], VSF[tb - 8].reg((wt * 256, (wt + 1) * 256))
                    S.op("act", lambda e, b=b, dst=dst: e.activation(out=dst, in_=PS[b].ap[:, 0:256], func=AF.Gelu_apprx_tanh),
                         reads=[PS[b].reg((0, 256))], writes=[dreg])
                    S.op("act", lambda e, dst=dst, tb=tb, wt=wt: e.activation(out=junk.ap[:, 0:256], in_=dst, func=AF.Square, accum_out=ssqv.ap[:, tb, wt:wt + 1]),
                         reads=[dreg], writes=[junk.reg((0, 256)), ssqv.reg(tb, wt)])
            S.op("sp", lambda e: e.dma_start(out=bSp.ap, in_=b_spatial[l].partition_broadcast(128)), writes=[bSp.whole()], dma="c1")
            for hf in range(2):
                S.op("sp", lambda e, hf=hf: e.dma_start(out=bSs.ap[:, :, hf * 64:(hf + 1) * 64], in_=b_spatial[l][:, 0:64].partition_broadcast(128)),
                     writes=[bSs.whole()], dma="c1")
            S.op("sp", lambda e: e.dma_start(out=Wsp_f.ap, in_=w_spatial[l].rearrange("g t s -> t g s")), writes=[Wsp_f.whole()], dma="c1")
            S.op("dve", lambda e: e.memset(Wbd_f.ap, 0.0), writes=[Wbd_f.whole()])
            S.op("dve", lambda e: e.memset(WmT.ap, 0.0), writes=[WmT.whole()])
            for hf in range(2):
                S.op("sp", lambda e, hf=hf: e.dma_start(out=Wbd_f.ap[hf * 64:(hf + 1) * 64, :, hf * 64:(hf + 1) * 64],
                                                      in_=w_spatial[l][:, 0:64, 0:64].rearrange("g t s -> t g s")),
                     writes=[Wbd_f.whole()], dma="c1")
            for g in range(8):
                b = ps1()
                S.op("pe", lambda e, b=b, g=g: e.transpose(out=PS[b].ap[:, 0:128], in_=Wsp_f.ap[:, g, :], identity=ident_f.ap),
                     reads=[Wsp_f.whole(), ident_f.whole()], writes=[PS[b].reg((0, 128))])
                S.op("pe", lambda e, b=b, g=g: e.transpose(out=PS[b].ap[:, 128:256], in_=Wbd_f.ap[:, g, :], identity=ident_f.ap),
                     reads=[Wbd_f.whole(), ident_f.whole()], writes=[PS[b].reg((128, 256))])
                S.op("dve", lambda e, b=b, g=g: e.tensor_copy(out=WmT.ap[0:64, g, :], in_=PS[b].ap[0:64, 0:128]),
                     reads=[PS[b].reg((0, 256))], writes=[WmT.whole()])
                S.op("dve", lambda e, b=b, g=g: e.tensor_copy(out=WmT.ap[64:128, g, 64:128], in_=PS[b].ap[64:128, 64:128]),
                     reads=[PS[b].reg((0, 256))], writes=[WmT.whole()])
                S.op("dve", lambda e, b=b, g=g: e.tensor_copy(out=WbT.ap[:, g, :], in_=PS[b].ap[:, 128:256]),
                     reads=[PS[b].reg((0, 256))], writes=[WbT.whole()])
            S.op("dve", lambda e: e.tensor_reduce(out=rstdv.ap[:, 0:NTB], in_=ssqv.ap[:, 0:NTB, :], axis=mybir.AxisListType.X, op=ALU.add),
                 reads=[ssqv.whole()], writes=[rstdv.whole()])
            S.op("act", lambda e: e.activation(out=rstdv.ap[:, 0:NTB], in_=rstdv.ap[:, 0:NTB], func=AF.Ln, bias=eps_c.ap[:, 0:1], scale=1.0 / SBW),
                 reads=[rstdv.whole(), eps_c.whole()], writes=[rstdv.whole()])
            S.op("act", lambda e: e.activation(out=rstdv.ap[:, 0:NTB], in_=rstdv.ap[:, 0:NTB], func=AF.Exp, scale=-0.5),
                 reads=[rstdv.whole()], writes=[rstdv.whole()])
            for tb in range(NTB):
                if tb < 8:
                    S.op("dve", lambda e, tb=tb: e.scalar_tensor_tensor(out=BQ.ap[:, tb, :], in0=BQ.ap[:, tb, :], scalar=rstdv.ap[:, tb:tb + 1], in1=sguG.ap,
                                                                      op0=ALU.mult, op1=ALU.mult),
                         reads=[BQ.reg(tb), rstdv.whole(), sguG.whole()], writes=[BQ.reg(tb)])
                else:
                    vf = VSF[tb - 8]
                    S.op("dve", lambda e, tb=tb, vf=vf: e.scalar_tensor_tensor(out=vf.ap, in0=vf.ap, scalar=rstdv.ap[:, tb:tb + 1], in1=sguG.ap,
                                                                             op0=ALU.mult, op1=ALU.mult),
                         reads=[vf.whole(), rstdv.whole(), sguG.whole()], writes=[vf.whole()])
                    S.op("dve", lambda e, tb=tb, vf=vf: e.tensor_copy(out=BQ.ap[:, tb, :], in_=vf.ap), reads=[vf.whole()], writes=[BQ.reg(tb)])
                    if write_sv:
                        S.op("sp", lambda e, tb=tb, vf=vf: e.dma_start(out=sv_o[l, (tb - 8) * 128:(tb - 7) * 128, :], in_=vf.ap),
                             reads=[vf.whole()], dma="o_sv", is_out=True)
            chk(tag + 'sgu_vn', [('bq', BQ)])
            for g in range(8):
                for t4 in range(0, NTB, 4):
                    nb = min(4, NTB - t4)
                    b = ps1()
                    for j in range(nb):
                        tb = t4 + j
                        wt_ = WmT if tb < 8 else WbT
                        S.op("pe", lambda e, b=b, j=j, tb=tb, g=g, wt_=wt_: e.matmul(PS[b].ap[:, j * 128:(j + 1) * 128], lhsT=BQ.ap[:, tb, g * 128:(g + 1) * 128],
                                                                                   rhs=wt_.ap[:, g, :], start=True, stop=True),
                             reads=[BQ.reg(tb, (g * 128, (g + 1) * 128)), wt_.whole()], writes=[PS[b].reg((j * 128, (j + 1) * 128))])
                    tmp = t5()
                    bs_ = bSp if t4 < 8 else bSs
                    S.op("dve", lambda e, b=b, g=g, bs_=bs_, tmp=tmp, nb=nb: e.tensor_tensor(
                        out=tmp.ap[:, 0:nb * 128].rearrange("p (b t) -> p b t", t=128), in0=PS[b].ap[:, 0:nb * 128].rearrange("p (b t) -> p b t", t=128),
                        in1=bs_.ap[:, g:g + 1, :].to_broadcast([128, nb, 128]), op=ALU.add),
                        reads=[PS[b].reg((0, nb * 128)), bs_.whole()], writes=[tmp.reg((0, nb * 128))])
                    S.op("dve", lambda e, g=g, t4=t4, nb=nb, tmp=tmp: e.tensor_tensor(out=BU.ap[:, g, t4 * 128:(t4 + nb) * 128], in0=BU.ap[:, g, t4 * 128:(t4 + nb) * 128],
                                                                                    in1=tmp.ap[:, 0:nb * 128], op=ALU.mult),
                         reads=[BU.reg(g, (t4 * 128, (t4 + nb) * 128)), tmp.reg((0, nb * 128))], writes=[BU.reg(g, (t4 * 128, (t4 + nb) * 128))])
                    if g * 10 + t4 == DBG.get('gstop', -1):
                        chk(tag + 'sgu_s', [('bu', BU), ('tmp', tmp)])

        def qkv_stage(l, T, grpA, need_q=True):
            NTB = T // 128
            tl = tiles_of(T)
            BQf = BQ.ap.rearrange("p a b -> p (a b)").rearrange("p (h t) -> p h t", t=TBT)

            def bqreg(h, t0, t1):
                return ("sb", BQ.off + (h * TBT + t0) * 2, BQ.off + (h * TBT + t1) * 2)
            for which, cbase in (("q", C_Q), ("k", C_K)):
                if which == "q" and not need_q:
                    continue
                for g4 in range(4):
                    wv, wsl = load_w(w_in[l], 0, 16, cbase + g4 * 256, 256)
                    for j in range(2):
                        h = g4 * 2 + j
                        pset = psset()
                        mm_fm(pset, wv, wsl, j * 128, XN, list(range(16)), T)
                        for ti, (t0, n) in enumerate(tl):
                            if which == "q":
                                S.op("act", lambda e, ti=ti, t0=t0, n=n, h=h, pset=pset: e.copy(out=BQf[:, h, t0:t0 + n], in_=pset[ti].ap[:, 0:n]),
                                     reads=[pset[ti].reg((0, n))], writes=[bqreg(h, t0, t0 + n)])
                            elif grpA:
                                S.op("dve", lambda e, ti=ti, t0=t0, n=n, h=h, pset=pset: e.tensor_copy(out=BK.ap[:, h, t0:t0 + n], in_=pset[ti].ap[:, 0:n]),
                                     reads=[pset[ti].reg((0, n))], writes=[BK.reg(h, (t0, t0 + n))])
                        if which == "k" and grpA:
                            S.op("sp", lambda e, h=h: e.dma_start(out=pk[l, h * 128:(h + 1) * 128, :], in_=BK.ap[:, h, 0:TP]),
                                 reads=[BK.reg(h, (0, TP))], writes=[dkey("pk%d" % l, h)], dma="pkw")
                        if which == "k" and not grpA:
                            kf = t5()
                            for ti, (t0, n) in enumerate(tl):
                                S.op("act", lambda e, ti=ti, t0=t0, n=n, pset=pset, kf=kf: e.copy(out=kf.ap[:, t0:t0 + n], in_=pset[ti].ap[:, 0:n]),
                                     reads=[pset[ti].reg((0, n))], writes=[kf.reg((t0, t0 + n))])
                                S.op("dve", lambda e, t0=t0, n=n, h=h, kf=kf: e.tensor_copy(out=BK.ap[:, h, t0:t0 + n], in_=kf.ap[:, t0:t0 + n]),
                                     reads=[kf.reg((t0, t0 + n))], writes=[BK.reg(h, (t0, t0 + n))])
                            S.op("sp", lambda e, h=h, kf=kf: e.dma_start(out=k_o[l, h * 128:(h + 1) * 128, 0:T], in_=kf.ap[:, 0:T]),
                                 reads=[kf.reg((0, T))], dma=t5key(kf), is_out=True)
            for which, cbase in (("v", C_V),):
                for wt in range(4):
                    wv, wsl = load_w(w_in[l], 0, 16, cbase + wt * 256, 256)
                    for tb in range(NTB):
                        b = ps1()
                        for k in range(16):
                            S.op("pe", lambda e, b=b, k=k, tb=tb, wv=wv: e.matmul(PS[b].ap[:, 0:256], lhsT=XN.ap[:, k, tb * 128:(tb + 1) * 128], rhs=wv[:, k, :],
                                                                               start=(k == 0), stop=(k == 15)),
                                 reads=[XN.reg(k, (tb * 128, (tb + 1) * 128)), wsl.whole()], writes=[PS[b].reg((0, 256))])
                        if not grpA:
                            stg = KVST[(tb * 4 + wt) % 4]
                            S.op("act", lambda e, b=b, stg=stg: e.copy(out=stg.ap[:, 0:256], in_=PS[b].ap[:, 0:256]),
                                 reads=[PS[b].reg((0, 256))], writes=[stg.reg((0, 256))])
                            dst = (k_o if which == "k" else v_o)
                            S.op("sp", lambda e, stg=stg, dst=dst, tb=tb, wt=wt: e.dma_start(out=dst[l, tb * 128:(tb + 1) * 128, wt * 256:(wt + 1) * 256], in_=stg.ap[:, 0:256]),
                                 reads=[stg.reg((0, 256))], dma="o_kv%d" % ((tb * 4 + wt) % 4), is_out=True)
                        if which == "v":
                            if grpA:
                                S.op("dve", lambda e, b=b, tb=tb, wt=wt: e.tensor_scalar(out=BV.ap[:, tb, wt * 256:(wt + 1) * 256], in0=PS[b].ap[:, 0:256],
                                                                                       scalar1=flag_s.ap[:, 0:1], scalar2=None, op0=ALU.mult),
                                     reads=[PS[b].reg((0, 256)), flag_s.whole()], writes=[BV.reg(tb, (wt * 256, (wt + 1) * 256))])
                            else:
                                S.op("dve", lambda e, b=b, tb=tb, wt=wt: e.tensor_copy(out=BV.ap[:, tb, wt * 256:(wt + 1) * 256], in_=PS[b].ap[:, 0:256]),
                                     reads=[PS[b].reg((0, 256))], writes=[BV.reg(tb, (wt * 256, (wt + 1) * 256))])
            if grpA:
                for tb in range(NTB):
                    S.op("sp", lambda e, tb=tb: e.dma_start(out=pv[l, tb * 128:(tb + 1) * 128, :], in_=BV.ap[:, tb, :]),
                         reads=[BV.reg(tb)], writes=[dkey("pv%d" % l, tb)], dma="pvw")

        zrhs = carve_at(MISC + 512, [512], BF16)
        assert MISC + 512 + 1024 <= AW * 4
        S.op("pool", lambda e: e.memset(zrhs.ap, 0.0), writes=[zrhs.whole()])

        def zero_bank(b):
            S.op("pe", lambda e: e.matmul(PS[b].ap[:, 0:512], lhsT=zero_b.ap, rhs=zrhs.ap, start=True, stop=False, skip_group_check=True),
                 reads=[zero_b.whole(), zrhs.whole()], writes=[PS[b].whole()])

        def run_units(units):
            for u in units:
                zero_bank(u["C"]); zero_bank(u["O"])
            nmax = max(len(u["tiles"]) for u in units)

            def geo(u, i):
                s_fn, mask_fn, pv_fn, nparts, p0, n0 = u["tiles"][i]
                return slice(p0, p0 + nparts), slice(n0, 512), (n0, 512)

            def front(u, i):
                s_fn, mask_fn, pv_fn, nparts, p0, n0 = u["tiles"][i]
                E_, EN_, SP_, WT_ = ATT[u["att"]]
                if i % 2 == 1:
                    E_ = E2[u["att"]]
                sb = u["S"]
                pr, cs, rg = geo(u, i)
                S.op("act", lambda e: e.activation(out=E_.ap[pr, cs], in_=PS[sb].ap[pr, cs], func=AF.Exp, scale=SCALE),
                     reads=[PS[sb].reg(rg)], writes=[E_.reg(rg)])
                if mask_fn is not None:
                    mask_fn(E_)
                S.op("act", lambda e: e.activation(out=SP_.ap[pr, cs], in_=E_.ap[pr, cs], func=AF.Ln, bias=1.0, scale=1.0),
                     reads=[E_.reg(rg)], writes=[SP_.reg(rg)])

            for u in units:
                u["tiles"][0][0](u["S"])
            for u in units:
                front(u, 0)
            for i in range(nmax):
                act = [u for u in units if i < len(u["tiles"])]
                for u in act:
                    E_, EN_, SP_, WT_ = ATT[u["att"]]
                    cb = u["C"]
                    pr, cs, rg = geo(u, i)
                    S.op("pe", lambda e, cb=cb, SP_=SP_, pr=pr, cs=cs: e.matmul(PS[cb].ap[:, cs], lhsT=tri_b.ap[pr, :], rhs=SP_.ap[pr, cs], start=False, stop=False, skip_group_check=True),
                         reads=[SP_.reg(rg), tri_b.whole()], writes=[PS[cb].reg(rg)])
                    if i + 1 < len(u["tiles"]):
                        u["tiles"][i + 1][0](u["S"])
                for u in act:
                    E_, EN_, SP_, WT_ = ATT[u["att"]]
                    cb = u["C"]
                    pr, cs, rg = geo(u, i)
                    S.op("act", lambda e, cb=cb, EN_=EN_, pr=pr, cs=cs: e.activation(out=EN_.ap[pr, cs], in_=PS[cb].ap[pr, cs], func=AF.Exp, scale=-1.0),
                         reads=[PS[cb].reg(rg)], writes=[EN_.reg(rg)])
                for u in act:
                    s_fn, mask_fn, pv_fn, nparts, p0, n0 = u["tiles"][i]
                    E_, EN_, SP_, WT_ = ATT[u["att"]]
                    if i % 2 == 1:
                        E_ = E2[u["att"]]
                    cb = u["C"]
                    pr, cs, rg = geo(u, i)
                    S.op("pe", lambda e, cb=cb, SP_=SP_, pr=pr, cs=cs: e.matmul(PS[cb].ap[:, cs], lhsT=m2_b.ap[pr, :], rhs=SP_.ap[pr, cs], start=False, stop=False, skip_group_check=True),
                         reads=[SP_.reg(rg), m2_b.whole()], writes=[PS[cb].reg(rg)])
                    S.op("dve", lambda e, E_=E_, EN_=EN_, WT_=WT_, pr=pr, cs=cs: e.tensor_tensor(out=WT_.ap[pr, cs], in0=E_.ap[pr, cs], in1=EN_.ap[pr, cs], op=ALU.mult),
                         reads=[E_.reg(rg), EN_.reg(rg)], writes=[WT_.reg(rg)])
                    pv_fn(u["O"], WT_)
                for u in act:
                    if i + 1 < len(u["tiles"]):
                        front(u, i + 1)
            for u in units:
                u["fin"](u["O"])

        def attention_stage(l, T, use_prev, has_s):
            BQf = BQ.ap.rearrange("p a b -> p (a b)").rearrange("p (h t) -> p h t", t=TBT)

            def bqreg(h, t0, t1):
                return ("sb", BQ.off + (h * TBT + t0) * 2, BQ.off + (h * TBT + t1) * 2)

            def prompt_unit(h, qg, slot, kp, vp):
                q0 = qg * 512
                blocks = [("own", kb) for kb in range(qg * 4 + 3, -1, -1)]
                if use_prev:
                    blocks += [("prev", kb) for kb in range(7, -1, -1)]
                tiles = []
                for kind, kb in blocks:
                    if kind == "own":
                        n0 = max(q0, kb * 128) - q0
                        diag = kb * 128 >= q0
                        klhs, klr = BK.ap[:, h, kb * 128:(kb + 1) * 128], BK.reg(h, (kb * 128, (kb + 1) * 128))
                        vlhs, vlr = BV.ap[:, kb, h * 128:(h + 1) * 128], BV.reg(kb, (h * 128, (h + 1) * 128))
                    else:
                        n0, diag = 0, False
                        klhs, klr = kp.ap[:, kb * 128:(kb + 1) * 128], kp.reg((kb * 128, (kb + 1) * 128))
                        vlhs, vlr = vp.ap[:, kb, :], vp.reg(kb)

                    def s_fn(sb, klhs=klhs, klr=klr, n0=n0):
                        S.op("pe", lambda e: e.matmul(PS[sb].ap[:, n0:512], lhsT=klhs, rhs=BQf[:, h, q0 + n0:q0 + 512], start=True, stop=True),
                             reads=[klr, bqreg(h, q0 + n0, q0 + 512)], writes=[PS[sb].reg((n0, 512))])

                    def m_fn(E_, n0=n0):
                        S.op("dve", lambda e: e.tensor_tensor(out=E_.ap[:, n0:n0 + 128], in0=E_.ap[:, n0:n0 + 128], in1=cm_f.ap, op=ALU.mult),
                             reads=[E_.reg((n0, n0 + 128)), cm_f.whole()], writes=[E_.reg((n0, n0 + 128))])

                    def pv_fn(ob, WT_, vlhs=vlhs, vlr=vlr, n0=n0):
                        S.op("pe", lambda e: e.matmul(PS[ob].ap[:, n0:512], lhsT=vlhs, rhs=WT_.ap[:, n0:512], start=False, stop=False, skip_group_check=True),
                             reads=[vlr, WT_.reg((n0, 512))], writes=[PS[ob].reg((n0, 512))])
                    tiles.append((s_fn, m_fn if diag else None, pv_fn, 128, 0, n0))

                def fin(ob):
                    S.op("act", lambda e: e.copy(out=BQf[:, h, q0:q0 + 512], in_=PS[ob].ap[:, 0:512]),
                         reads=[PS[ob].whole()], writes=[bqreg(h, q0, q0 + 512)])
                return {"S": slot, "C": 2 + slot, "O": 4 + slot, "att": slot, "tiles": tiles, "fin": fin}

            for h2 in range(0, NH, 2):
                kvs = []
                for j in range(2):
                    h = h2 + j
                    kp, vp = KPV[j]
                    if use_prev:
                        S.op("sp", lambda e, kp=kp, h=h: e.dma_start(out=kp.ap, in_=pk[l, h * 128:(h + 1) * 128, :]),
                             reads=[dkey("pk%d" % l, h)], writes=[kp.whole()], dma="kp%d" % j)
                        S.op("sp", lambda e, vp=vp, h=h: e.dma_start(out=vp.ap, in_=pv[l, :, h * 128:(h + 1) * 128].rearrange("(b p) d -> p b d", p=128)),
                             reads=[dkey("pv%d" % l, tb) for tb in range(8)], writes=[vp.whole()], dma="vp%d" % j)
                    kvs.append((kp, vp))
                for qg in range(2):
                    run_units([prompt_unit(h2 + j, qg, j, kvs[j][0], kvs[j][1]) for j in range(2)])
            if not has_s:
                return
            sunits = []
            for s in range(4):
                slot = s % 2
                qc = TP + s * 64
                p0 = (s % 2) * 64
                tbn = 8 + s // 2
                kc0 = TP + (s // 2) * 128
                tiles = []

                def s_new(sb, s=s, qc=qc, kc0=kc0):
                    for h in range(NH):
                        if s % 2 == 0:
                            S.op("pe", lambda e, h=h: e.matmul(PS[sb].ap[0:64, h * 64:(h + 1) * 64], lhsT=BK.ap[:, h, qc:qc + 64], rhs=BQf[:, h, qc:qc + 64], start=True, stop=True),
                                 reads=[BK.reg(h, (qc, qc + 64)), bqreg(h, qc, qc + 64)], writes=[PS[sb].reg((h * 64, (h + 1) * 64))])
                        else:
                            S.op("pe", lambda e, h=h: e.matmul(PS[sb].ap[:, h * 64:(h + 1) * 64], lhsT=BK.ap[:, h, kc0:kc0 + 128], rhs=BQf[:, h, qc:qc + 64], start=True, stop=True),
                                 reads=[BK.reg(h, (kc0, kc0 + 128)), bqreg(h, qc, qc + 64)], writes=[PS[sb].reg((h * 64, (h + 1) * 64))])

                def m_new(E_, p0=p0):
                    S.op("dve", lambda e: e.tensor_tensor(out=E_.ap[p0:p0 + 64, :].rearrange("p (h t) -> p h t", t=64), in0=E_.ap[p0:p0 + 64, :].rearrange("p (h t) -> p h t", t=64),
                                                          in1=cm8_f.ap[p0:p0 + 64], op=ALU.mult),
                         reads=[E_.whole(), cm8_f.whole()], writes=[E_.whole()])

                def pv_new(ob, WT_, p0=p0, tbn=tbn):
                    for h in range(NH):
                        S.op("pe", lambda e, h=h: e.matmul(PS[ob].ap[:, h * 64:(h + 1) * 64], lhsT=BV.ap[p0:p0 + 64, tbn, h * 128:(h + 1) * 128], rhs=WT_.ap[p0:p0 + 64, h * 64:(h + 1) * 64],
                                                          start=False, stop=False, skip_group_check=True),
                             reads=[BV.reg(tbn, (h * 128, (h + 1) * 128)), WT_.whole()], writes=[PS[ob].reg((h * 64, (h + 1) * 64))])
                tiles.append((s_new, m_new, pv_new, 64, p0, 0))
                kt = KT[slot]
                for cg in range(7, -1, -1):
                    kcb, vcb = KC[slot], VC[slot]
                    for bi in (1, 0):
                        def s_c(sb, bi=bi, cg=cg, kcb=kcb, vcb=vcb, s=s, qc=qc, kt=kt, slot=slot):
                            if bi == 1:
                                S.op("pool", lambda e: e.dma_start(out=kcb.ap, in_=ck[l, s, cg * 256:(cg + 1) * 256, :].rearrange("(b p) f -> p b f", p=128)),
                                     writes=[kcb.whole()], dma="kc%d" % slot)
                                S.op("pool", lambda e: e.dma_start(out=vcb.ap, in_=cv[l, s, cg * 256:(cg + 1) * 256, :].rearrange("(b p) f -> p b f", p=128)),
                                     writes=[vcb.whole()], dma="vc%d" % slot)
                                for b2 in range(2):
                                    tb_ = 6 + slot
                                    for h in range(NH):
                                        S.op("pe", lambda e, b2=b2, h=h, tb_=tb_: e.transpose(out=PSB[tb_][:, h * 128:(h + 1) * 128], in_=kcb.ap[:, b2, h * 128:(h + 1) * 128], identity=ident_b.ap),
                                             reads=[kcb.reg(b2, (h * 128, (h + 1) * 128)), ident_b.whole()], writes=[PS[tb_].reg((h * 64, (h + 1) * 64))])
                                    if b2 == 0:
                                        S.op("dve", lambda e, b2=b2, tb_=tb_: e.tensor_copy(out=kt.ap[:, :, b2 * 128:(b2 + 1) * 128], in_=PSB[tb_].rearrange("p (h k) -> p h k", k=128)),
                                             reads=[PS[tb_].whole()], writes=[kt.whole()])
                                    else:
                                        S.op("act", lambda e, b2=b2, tb_=tb_: e.copy(out=kt.ap[:, :, b2 * 128:(b2 + 1) * 128], in_=PSB[tb_].rearrange("p (h k) -> p h k", k=128)),
                                             reads=[PS[tb_].whole()], writes=[kt.whole()])
                            for h in range(NH):
                                S.op("pe", lambda e, h=h: e.matmul(PS[sb].ap[:, h * 64:(h + 1) * 64], lhsT=kt.ap[:, h, bi * 128:(bi + 1) * 128], rhs=BQf[:, h, qc:qc + 64], start=True, stop=True),
                                     reads=[kt.whole(), bqreg(h, qc, qc + 64)], writes=[PS[sb].reg((h * 64, (h + 1) * 64))])

                        def pv_c(ob, WT_, bi=bi, vcb=vcb):
                            for h in range(NH):
                                S.op("pe", lambda e, h=h: e.matmul(PS[ob].ap[:, h * 64:(h + 1) * 64], lhsT=vcb.ap[:, bi, h * 128:(h + 1) * 128], rhs=WT_.ap[:, h * 64:(h + 1) * 64],
                                                                  start=False, stop=False, skip_group_check=True),
                                     reads=[vcb.reg(bi, (h * 128, (h + 1) * 128)), WT_.whole()], writes=[PS[ob].reg((h * 64, (h + 1) * 64))])
                        tiles.append((s_c, None, pv_c, 128, 0, 0))

                def fin(ob, qc=qc):
                    S.op("act", lambda e: e.copy(out=BQf[:, :, qc:qc + 64], in_=PS[ob].ap[:, 0:512].rearrange("p (h t) -> p h t", t=64)),
                         reads=[PS[ob].whole()], writes=[("sb", BQ.off, BQ.off + BQ.nbytes)])
                sunits.append({"S": slot, "C": 2 + slot, "O": 4 + slot, "att": slot, "tiles": tiles, "fin": fin})
                if slot == 1:
                    run_units(sunits)
                    sunits = []

        def merge_stage(l, T, res, res_in=None):
            tl = tiles_of(T)
            assert BV.off == BK.off + BK.nbytes
            MG = carve_at(BK.off, [16, TBT], BF16)
            BQf = Buf("sb", BQ.ap.rearrange("p a b -> p (a b)").rearrange("p (h t) -> p h t", t=TBT), BQ.off, [8, TBT], 2)
            for m2 in range(8):
                wga, sga = load_w(w_in[l], 0, 16, C_GA + m2 * 256, 256)
                wa, sa = load_w(w_a[l], 0, 8, m2 * 256, 256)
                tmps = []
                for j in range(2):
                    m = m2 * 2 + j
                    pg = psset(); mm_fm(pg, wga, sga, j * 128, XN, list(range(16)), T)
                    tg = t5()
                    for ti, (t0, n) in enumerate(tl):
                        S.op("act", lambda e, ti=ti, t0=t0, n=n, m=m, pg=pg, tg=tg: e.activation(out=tg.ap[:, t0:t0 + n], in_=pg[ti].ap[:, 0:n], func=AF.Sigmoid, bias=bgc.ap[:, l, m:m + 1], scale=1.0),
                             reads=[pg[ti].reg((0, n)), bgc.whole()], writes=[tg.reg((t0, t0 + n))])
                    pa = psset(); mm_fm(pa, wa, sa, j * 128, BQf, list(range(8)), T)
                    for ti, (t0, n) in enumerate(tl):
                        S.op("dve", lambda e, ti=ti, t0=t0, n=n, pa=pa, tg=tg: e.tensor_tensor(out=tg.ap[:, t0:t0 + n], in0=pa[ti].ap[:, 0:n], in1=tg.ap[:, t0:t0 + n], op=ALU.mult),
                             reads=[pa[ti].reg((0, n)), tg.reg((t0, t0 + n))], writes=[tg.reg((t0, t0 + n))])
                    tmps.append(tg)
                wgb, sgb = load_w(w_in[l], 0, 16, C_GB + m2 * 256, 256)
                wb, sb_ = load_w(w_b[l], 0, 8, m2 * 256, 256)
                for j in range(2):
                    m = m2 * 2 + j
                    tg = tmps[j]
                    pg = psset(); mm_fm(pg, wgb, sgb, j * 128, XN, list(range(16)), T)
                    t2 = t5()
                    for ti, (t0, n) in enumerate(tl):
                        S.op("act", lambda e, ti=ti, t0=t0, n=n, m=m, pg=pg, t2=t2: e.activation(out=t2.ap[:, t0:t0 + n], in_=pg[ti].ap[:, 0:n], func=AF.Sigmoid, bias=bgc.ap[:, l, 16 + m:17 + m], scale=1.0),
                             reads=[pg[ti].reg((0, n)), bgc.whole()], writes=[t2.reg((t0, t0 + n))])
                    pb = psset(); mm_fm(pb, wb, sb_, j * 128, BU, list(range(8)), T)
                    for ti, (t0, n) in enumerate(tl):
                        S.op("dve", lambda e, ti=ti, t0=t0, n=n, pb=pb, t2=t2: e.tensor_tensor(out=t2.ap[:, t0:t0 + n], in0=pb[ti].ap[:, 0:n], in1=t2.ap[:, t0:t0 + n], op=ALU.mult),
                             reads=[pb[ti].reg((0, n)), t2.reg((t0, t0 + n))], writes=[t2.reg((t0, t0 + n))])
                        S.op("dve", lambda e, t0=t0, n=n, m=m, tg=tg, t2=t2: e.tensor_tensor(out=MG.ap[:, m, t0:t0 + n], in0=tg.ap[:, t0:t0 + n], in1=t2.ap[:, t0:t0 + n], op=ALU.add),
                             reads=[tg.reg((t0, t0 + n)), t2.reg((t0, t0 + n))], writes=[MG.reg(m, (t0, t0 + n))])
            for m2 in range(8):
                wo, so = load_w(w_out[l], 0, 16, m2 * 256, 256)
                for j in range(2):
                    m = m2 * 2 + j
                    po = psset(); mm_fm(po, wo, so, j * 128, MG, list(range(16)), T)
                    xt = t5()
                    rsrc = res if res_in is None else res_in
                    S.op("sp", lambda e, xt=xt, m=m, rsrc=rsrc: e.dma_start(out=xt.ap[:, 0:T], in_=rsrc[m * 128:(m + 1) * 128, 0:T]),
                         reads=([rkey(res, m)] if res_in is None else []), writes=[xt.reg((0, T))], dma=t5key(xt))
                    for ti, (t0, n) in enumerate(tl):
                        S.op("dve", lambda e, ti=ti, t0=t0, n=n, po=po, xt=xt: e.tensor_tensor(out=xt.ap[:, t0:t0 + n], in0=po[ti].ap[:, 0:n], in1=xt.ap[:, t0:t0 + n], op=ALU.add),
                             reads=[po[ti].reg((0, n)), xt.reg((t0, t0 + n))], writes=[xt.reg((t0, t0 + n))])
                    S.op("sp", lambda e, xt=xt, m=m: e.dma_start(out=res[m * 128:(m + 1) * 128, 0:T], in_=xt.ap[:, 0:T]),
                         reads=[xt.reg((0, T))], writes=[rkey(res, m)], dma=t5key(xt))

        def ffn_stage(l, T, res, store_res=True):
            tl = tiles_of(T)
            def gate_up(fg):
                ag = ACTG[fg % 2]
                for half in range(2):
                    wg, sg_ = load_w(w_gu[l], 0, 16, fg * 512 + half * 256, 256)
                    wu, su_ = load_w(w_gu[l], 0, 16, DFF + fg * 512 + half * 256, 256)
                    for j in range(2):
                        jj = half * 2 + j
                        pg = psset(); mm_fm(pg, wg, sg_, j * 128, XN, list(range(16)), T)
                        tg = t5()
                        for ti, (t0, n) in enumerate(tl):
                            S.op("act", lambda e, ti=ti, t0=t0, n=n, pg=pg, tg=tg: e.activation(out=tg.ap[:, t0:t0 + n], in_=pg[ti].ap[:, 0:n], func=AF.Silu),
                                 reads=[pg[ti].reg((0, n))], writes=[tg.reg((t0, t0 + n))])
                        pu = psset(); mm_fm(pu, wu, su_, j * 128, XN, list(range(16)), T)
                        for ti, (t0, n) in enumerate(tl):
                            S.op("dve", lambda e, ti=ti, t0=t0, n=n, pu=pu, tg=tg, jj=jj, ag=ag: e.tensor_tensor(out=ag.ap[:, jj, t0:t0 + n], in0=pu[ti].ap[:, 0:n], in1=tg.ap[:, t0:t0 + n], op=ALU.mult),
                                 reads=[pu[ti].reg((0, n)), tg.reg((t0, t0 + n))], writes=[ag.reg(jj, (t0, t0 + n))])
            def down(fg):
                ag = ACTG[fg % 2]
                for mh in range(2):
                    wd, sd = load_w(w_dn[l], fg * 512, 4, mh * 1024, 1024)
                    for j in range(8):
                        m = mh * 8 + j
                        pd = psset(); mm_fm(pd, wd, sd, j * 128, ag, list(range(4)), T)
                        for ti, (t0, n) in enumerate(tl):
                            if fg == 0:
                                S.op("act", lambda e, ti=ti, t0=t0, n=n, pd=pd, m=m: e.copy(out=ACC.ap[:, m, t0:t0 + n], in_=pd[ti].ap[:, 0:n]),
                                     reads=[pd[ti].reg((0, n))], writes=[ACC.reg(m, (t0, t0 + n))])
                            else:
                                S.op("dve", lambda e, ti=ti, t0=t0, n=n, pd=pd, m=m: e.tensor_tensor(out=ACC.ap[:, m, t0:t0 + n], in0=pd[ti].ap[:, 0:n], in1=ACC.ap[:, m, t0:t0 + n], op=ALU.add),
                                     reads=[pd[ti].reg((0, n)), ACC.reg(m, (t0, t0 + n))], writes=[ACC.reg(m, (t0, t0 + n))])
            gate_up(0)
            for fg in range(11):
                if fg + 1 < 11:
                    gate_up(fg + 1)
                down(fg)
            xts = {}

            def ld(m):
                xt = t5()
                xts[m] = xt
                S.op("sp", lambda e, xt=xt, m=m: e.dma_start(out=xt.ap[:, 0:T], in_=res[m * 128:(m + 1) * 128, 0:T]),
                     reads=[rkey(res, m)], writes=[xt.reg((0, T))], dma=t5key(xt))
            for m in range(4):
                ld(m)
            for m in range(NCH):
                xt = xts[m]
                S.op("dve", lambda e, xt=xt, m=m: e.tensor_tensor(out=ACC.ap[:, m, 0:T], in0=ACC.ap[:, m, 0:T], in1=xt.ap[:, 0:T], op=ALU.add),
                     reads=[ACC.reg(m, (0, T)), xt.reg((0, T))], writes=[ACC.reg(m, (0, T))])
                if m + 4 < NCH:
                    ld(m + 4)
            if store_res:
                for m in range(NCH):
                    S.op("sp", lambda e, m=m: e.dma_start(out=res[m * 128:(m + 1) * 128, 0:T], in_=ACC.ap[:, m, 0:T]),
                         reads=[ACC.reg(m, (0, T))], writes=[rkey(res, m)], dma="accs%d" % (m % 4))

        def final_stage(res, T):
            norm_stats(res, T, True)
            NTB = T // 128
            yv = y_o.rearrange("(tb p) f -> p tb f", p=128)
            for c in range(NCH):
                xt = t5()
                S.op("dve", lambda e, xt=xt, c=c: e.scalar_tensor_tensor(out=xt.ap[:, 0:T], in0=ACC.ap[:, c, 0:T], scalar=gcol.ap[:, 4, c:c + 1],
                                                                       in1=RSTD.ap[:, 0:T], op0=ALU.mult, op1=ALU.mult),
                     reads=[ACC.reg(c, (0, T)), gcol.whole(), RSTD.reg((0, T))], writes=[xt.reg((0, T))])
                S.op("sp", lambda e, xt=xt, c=c: e.dma_start(out=y_o[c * 128:(c + 1) * 128, 0:T], in_=xt.ap[:, 0:T]),
                     reads=[xt.reg((0, T))], dma=t5key(xt), is_out=True)

        class Stop(Exception):
            pass

        def dump(buf, name):
            dt_ = F32 if buf.es == 4 else BF16
            t = nc.dram_tensor("dbg_" + name, [128] + list(buf.shape), dt_, kind="ExternalOutput").ap()
            S.op("sp", lambda e: e.dma_start(out=t, in_=buf.ap), reads=[buf.whole()], dma="dbg_" + name, is_out=True)

        def chk(name, bufs):
            if DBG.get("stop") == name or name in DBG.get("dumps", ()):
                for nm, bf in bufs:
                    dump(bf, name + "_" + nm)
            if DBG.get("stop") == name:
                raise Stop()

        def layer(l, T, res, grpA, last=False, xin=None):
            tag = "%s%d_" % ("A" if grpA else "B", l)
            norm_stats(res, T, True)
            norm_apply(res, T, l)
            chk(tag + "norm", [("xn", XN), ("rstd", RSTD)])
            sgu_stage(l, T, write_sv=not grpA)
            chk(tag + "sgu", [("bu", BU), ("bq", BQ)])
            qkv_stage(l, T, grpA)
            chk(tag + "qkv", [("bq", BQ), ("bk", BK), ("bv", BV)])
            attention_stage(l, T, use_prev=not grpA, has_s=not grpA)
            chk(tag + "att", [("bq", BQ)])
            merge_stage(l, T, res, res_in=xin)
            chk(tag + "merge", [("bk", BK), ("bv", BV)])
            norm_stats(res, T, False)
            norm_apply(res, T, 2 + l)
            ffn_stage(l, T, res, store_res=not last)
            chk(tag + "ffn", [])

        try:
            load_x(xa, TP)
            layer(0, TP, resA, True, xin=xa)
            norm_stats(resA, TP, True)
            norm_apply(resA, TP, 1)
            qkv_stage(1, TP, True, need_q=False)
            chk("A1_kv", [])
            load_x(xb, TBT)
            layer(0, TBT, resB, False, xin=xb)
            layer(1, TBT, resB, False, last=True)
            final_stage(resB, TBT)
        except Stop:
            pass
        S.finish()
        S.emit(nc)
    return nc


_NC_CACHE = {}


def kernel(x_prompt, x_sample, cache_k, cache_v, norm_mix, w_in, b_gate, sgu_norm, w_spatial, b_spatial,
           w_branch_a, w_branch_b, w_out, norm_ffn, w_gate_up, w_down, norm_final):
    f = lambda a: np.ascontiguousarray(np.asarray(a, dtype=np.float32))
    x_prompt, x_sample, cache_k, cache_v = f(x_prompt), f(x_sample), f(cache_k), f(cache_v)
    shared = {
        "norm_mix": f(norm_mix), "w_in": f(w_in), "b_gate": f(b_gate), "sgu_norm": f(sgu_norm),
        "w_spatial": f(w_spatial), "b_spatial": f(b_spatial), "w_branch_a": f(w_branch_a),
        "w_branch_b": f(w_branch_b), "w_out": f(w_out), "norm_ffn": f(norm_ffn),
        "w_gate_up": f(w_gate_up), "w_down": f(w_down), "norm_final": f(norm_final),
    }
    if "nc" not in _NC_CACHE:
        _NC_CACHE["nc"] = build_program()
    nc = _NC_CACHE["nc"]
    ncores = DBG.get("ncores", 8)
    in_maps = []
    for c in range(ncores):
        c = c + DBG.get('core0', 0)
        b, hf = c // 2, c % 2
        m = dict(shared)
        m["xa"] = np.ascontiguousarray(x_prompt[b, 0:TP].T)
        m["xb"] = np.ascontiguousarray(np.concatenate(
            [x_prompt[b, hf * TP:(hf + 1) * TP], x_sample[4 * c:4 * c + 4].reshape(TS, D)], axis=0).T)
        m["flag"] = np.full((128, 1), float(hf), np.float32)
        m["ck"] = np.ascontiguousarray(cache_k[:, 4 * c:4 * c + 4].reshape(2, 4, 2048, SBW))
        m["cv"] = np.ascontiguousarray(cache_v[:, 4 * c:4 * c + 4].reshape(2, 4, 2048, SBW))
        for nm in DBG.get('shrink', ()):
            m.pop(nm, None)
        in_maps.append(m)
    res = run_bass_kernel_spmd(nc, in_maps, core_ids=list(range(ncores)))
    r = res.results
    if DBG:
        return r
    B, SEQ = x_prompt.shape[0], x_prompt.shape[1]
    y_p = np.empty((B, SEQ, D), np.float32)
    y_s = np.empty((32, 64, D), np.float32)
    k_p = np.empty((2, B, SEQ, NH, HD), np.float32)
    v_p = np.empty((2, B, SEQ, NH, HD), np.float32)
    k_s = np.empty((2, 32, 64, NH, HD), np.float32)
    v_s = np.empty((2, 32, 64, NH, HD), np.float32)
    sv_s = np.empty((2, 32, 64, SBW), np.float32)
    for c in range(8):
        b, hf = c // 2, c % 2
        y = np.asarray(r[c]["y"]).T; ko = np.asarray(r[c]["ko"]).transpose(0, 2, 1); vo = np.asarray(r[c]["vo"]); sv = np.asarray(r[c]["sv"])
        y_p[b, hf * TP:(hf + 1) * TP] = y[:TP]
        y_s[4 * c:4 * c + 4] = y[TP:].reshape(4, 64, D)
        k_p[:, b, hf * TP:(hf + 1) * TP] = ko[:, :TP].reshape(2, TP, NH, HD)
        v_p[:, b, hf * TP:(hf + 1) * TP] = vo[:, :TP].reshape(2, TP, NH, HD)
        k_s[:, 4 * c:4 * c + 4] = ko[:, TP:].reshape(2, 4, 64, NH, HD)
        v_s[:, 4 * c:4 * c + 4] = vo[:, TP:].reshape(2, 4, 64, NH, HD)
        sv_s[:, 4 * c:4 * c + 4] = sv.reshape(2, 4, 64, SBW)
    return (y_p, y_s, k_p, v_p, k_s, v_s, sv_s)
```

```python
import contextlib
import numpy as np
import concourse.bass as bass
import concourse.mybir as mybir
from concourse.bass_utils import run_bass_kernel_spmd

F32 = mybir.dt.float32
BF16 = mybir.dt.bfloat16
AF = mybir.ActivationFunctionType
ALU = mybir.AluOpType

CELL = 128


class Op:
    __slots__ = ("eng", "fn", "deps", "needed", "sigidx", "dsem", "dval", "tag")

    def __init__(self, eng, fn, tag=""):
        self.eng = eng
        self.fn = fn
        self.deps = []
        self.needed = False
        self.sigidx = 0
        self.dsem = None
        self.dval = 0
        self.tag = tag


class Sched:
    ENGS = ("pe", "act", "dve", "pool", "sp")

    def __init__(self):
        self.ops = {e: [] for e in self.ENGS}
        self.lastw = {}
        self.readers = {}
        self.dma_counts = {}
        self.last_dma = {}
        self.out_dmas = []

    def op(self, eng, fn, reads=(), writes=(), dma=None, tag="", is_out=False):
        o = Op(eng, fn, tag)
        deps = {}
        def _bank(acc):
            sp_, lo_, hi_ = acc
            if sp_ == "ps":
                lo_ = lo_ // 2048 * 2048
                hi_ = (hi_ + 2047) // 2048 * 2048
            return sp_, lo_, hi_
        reads = [_bank(a) for a in reads]
        writes = [_bank(a) for a in writes]
        for (space, lo, hi) in reads:
            lw = self.lastw.setdefault(space, {})
            rd = self.readers.setdefault(space, {})
            for c in range(lo // CELL, (hi - 1) // CELL + 1):
                w = lw.get(c)
                if w is not None:
                    deps[id(w)] = w
                rl = rd.setdefault(c, [])
                if space == "ps":
                    for r in rl:
                        if r.eng != eng:
                            deps[id(r)] = r
                rl.append(o)
        for (space, lo, hi) in writes:
            lw = self.lastw.setdefault(space, {})
            rd = self.readers.setdefault(space, {})
            for c in range(lo // CELL, (hi - 1) // CELL + 1):
                w = lw.get(c)
                if w is not None:
                    deps[id(w)] = w
                rl = rd.get(c)
                if rl:
                    lastc = {}
                    for r in rl:
                        if r.dsem is None:
                            lastc[r.eng] = r
                        else:
                            deps[id(r)] = r
                    for r in lastc.values():
                        deps[id(r)] = r
                lw[c] = o
                rd[c] = []
        for d in deps.values():
            if d is o:
                continue
            if d.dsem is None and d.eng == "pe" and eng == "pe" and dma is None:
                continue
            d.needed = True
            o.deps.append(d)
        if dma is not None:
            o.dsem = dma
            prev = self.last_dma.get(dma)
            if prev is not None and all(d is not prev for d in o.deps):
                o.deps.append(prev)
            self.last_dma[dma] = o
            self.dma_counts[dma] = self.dma_counts.get(dma, 0) + 16
            o.dval = self.dma_counts[dma]
            if is_out:
                self.out_dmas.append(o)
        self.ops[eng].append(o)
        return o

    def finish(self):
        o = Op("sp", None, "final")
        o.deps = list(self.out_dmas)
        self.ops["sp"].append(o)

    def emit(self, nc):
        with contextlib.ExitStack() as st:
            esem = {e: st.enter_context(nc.semaphore("s_" + e)) for e in self.ENGS}
            dsem = {k: st.enter_context(nc.semaphore("d_" + str(k))) for k in self.dma_counts}
            for e in self.ENGS:
                n = 0
                for o in self.ops[e]:
                    if o.dsem is None and o.needed:
                        n += 1
                        o.sigidx = n
            block = st.enter_context(nc.Block())

            def mk(e):
                def body(eng):
                    waited = {}
                    for o in self.ops[e]:
                        need = {}
                        for d in o.deps:
                            if d.dsem is not None:
                                s, v = dsem[d.dsem], d.dval
                            else:
                                s, v = esem[d.eng], d.sigidx
                            k = id(s)
                            if v > waited.get(k, 0) and v > need.get(k, (None, 0))[1]:
                                need[k] = (s, v)
                        for k, (s, v) in need.items():
                            eng.wait_ge(s, v)
                            waited[k] = v
                        if o.fn is None:
                            continue
                        inst = o.fn(eng)
                        if o.dsem is not None:
                            inst.then_inc(dsem[o.dsem], 16)
                        elif o.needed:
                            inst.then_inc(esem[e], 1)
                return body

            block.tensor(mk("pe"))
            block.scalar(mk("act"))
            block.vector(mk("dve"))
            block.gpsimd(mk("pool"))
            block.sync(mk("sp"))


class Buf:
    def __init__(self, space, ap, off, shape, es):
        self.space, self.ap, self.off, self.shape, self.es = space, ap, off, tuple(shape), es
        n = es
        for s in shape:
            n *= s
        self.nbytes = n
        st, strides = es, []
        for s in reversed(self.shape):
            strides.append(st)
            st *= s
        self.strides = list(reversed(strides))

    def whole(self):
        return (self.space, self.off, self.off + self.nbytes)

    def reg(self, *idx):
        lo = hi = 0
        for i, s in enumerate(self.shape):
            ix = idx[i] if i < len(idx) else None
            if ix is None:
                a, b = 0, s
            elif isinstance(ix, int):
                a, b = ix, ix + 1
            else:
                a, b = ix
            lo += a * self.strides[i]
            hi += (b - 1) * self.strides[i]
        return (self.space, self.off + lo, self.off + hi + self.es)


D = 2048
NCH = 16
NH = 8
HD = 128
SBW = 1024
DFF = 5632
NFC = 44
INC = 9216
TP = 1024
TS = 256
TBT = TP + TS
C_Q, C_K, C_V, C_U, C_VS, C_GA, C_GB = 0, 1024, 2048, 3072, 4096, 5120, 7168
EPS = 1e-6
SCALE = float(HD) ** -0.5
DBG = {}


def dkey(name, i=0):
    return ("dr_" + name, i * CELL, i * CELL + 1)


def build_program():
    nc = bass.Bass("TRN2", target_bir_lowering=False)
    S = Sched()

    def din(name, shape, dt=F32):
        if name in DBG.get("shrink", ()):
            return nc.dram_tensor(name, list(shape), dt).ap()
        return nc.dram_tensor(name, list(shape), dt, kind="ExternalInput").ap()

    def dout(name, shape, dt=F32):
        return nc.dram_tensor(name, list(shape), dt, kind="ExternalOutput").ap()

    xa = din("xa", [D, TP])
    xb = din("xb", [D, TBT])
    flag_d = din("flag", [128, 1])
    ck = din("ck", [2, 4, 2048, SBW])
    cv = din("cv", [2, 4, 2048, SBW])
    norm_mix = din("norm_mix", [2, D])
    w_in = din("w_in", [2, D, INC])
    b_gate = din("b_gate", [2, 2 * D])
    sgu_norm = din("sgu_norm", [2, SBW])
    w_spatial = din("w_spatial", [2, 8, 128, 128])
    b_spatial = din("b_spatial", [2, 8, 128])
    w_a = din("w_branch_a", [2, SBW, D])
    w_b = din("w_branch_b", [2, SBW, D])
    w_out = din("w_out", [2, D, D])
    norm_ffn = din("norm_ffn", [2, D])
    w_gu = din("w_gate_up", [2, D, 2 * DFF])
    w_dn = din("w_down", [2, DFF, D])
    norm_final = din("norm_final", [D])

    y_o = dout("y", [D, TBT])
    k_o = dout("ko", [2, SBW, TBT])
    v_o = dout("vo", [2, TBT, SBW])
    sv_o = dout("sv", [2, TS, SBW])

    skind = "ExternalOutput" if DBG else "Internal"
    resA = nc.dram_tensor("resA", [D, TP], F32, kind=skind).ap()
    resB = nc.dram_tensor("resB", [D, TBT], F32, kind=skind).ap()
    pk = nc.dram_tensor("pk", [2, SBW, TP], BF16, kind=skind).ap()
    pv = nc.dram_tensor("pv", [2, TP, SBW], BF16, kind=skind).ap()

    with contextlib.ExitStack() as st:
        AW = 53200
        arena_t = st.enter_context(nc.sbuf_tensor("arena", [128, AW], F32))
        arena = arena_t[:]
        ps_t = [st.enter_context(nc.psum_tensor("ps%d" % i, [128, 512], F32)) for i in range(8)]
        PS = [Buf("ps", ps_t[i][:], i * 2048, [512], 4) for i in range(8)]
        PSB = [ps_t[i][:].bitcast(BF16) for i in range(8)]

        cur = [0]

        def carve_at(off, shape, dt):
            es = 4 if dt == F32 else 2
            n = es
            for s_ in shape:
                n *= s_
            assert off % 4 == 0 and n % 4 == 0
            assert off + n <= AW * 4, (off, n)
            ap = arena[:, off // 4:(off + n) // 4]
            if dt != F32:
                ap = ap.bitcast(dt)
            if len(shape) == 2:
                ap = ap.rearrange("p (a b) -> p a b", b=shape[1])
            elif len(shape) == 3:
                ap = ap.rearrange("p (a b c) -> p a b c", b=shape[1], c=shape[2])
            return Buf("sb", ap, off, shape, es)

        def carve(shape, dt):
            es = 4 if dt == F32 else 2
            n = es
            for s_ in shape:
                n *= s_
            n = (n + 127) // 128 * 128
            b = carve_at(cur[0], shape, dt)
            cur[0] += n
            return b

        ident_b = carve([128], BF16)
        tri_b = carve([128], BF16)
        m2_b = carve([128], BF16)
        zero_b = carve([128], BF16)
        cm_f = carve([128], F32)
        cm8_f = carve([8, 64], F32)
        ident_f = carve([128], F32)
        ones_b = carve([128], BF16)
        flag_s = carve([32], F32)
        gcol = carve([5, 16], F32)
        bgc = carve([2, 32], F32)
        XN = carve([16, TBT], BF16)
        WS = [carve([4096], BF16) for _ in range(3)]
        RSTD = carve([TBT], F32)
        T5 = [carve([TBT], F32) for _ in range(4)]
        PH = cur[0]
        BU = carve([8, TBT], BF16)
        BQ = carve([10, 1024], BF16)
        BK = carve([8, TBT], BF16)
        BV = carve([10, 1024], BF16)
        MX = cur[0]
        sg = BK.off
        WmT = carve_at(sg, [8, 128], BF16); sg += 2048
        WbT = carve_at(sg, [8, 128], BF16); sg += 2048
        bSp = carve_at(sg, [8, 128], F32); sg += 4096
        bSs = carve_at(sg, [8, 128], F32); sg += 4096
        sguG = carve_at(sg, [1024], F32); sg += 4096
        Wsp_f = carve_at(sg, [8, 128], F32); sg += 4096
        Wbd_f = carve_at(sg, [8, 128], F32); sg += 4096
        VSF = [carve_at(sg + i * 4096, [1024], F32) for i in range(2)]; sg += 8192
        assert sg <= BV.off + BV.nbytes
        KPV = [(carve([1024], BF16), carve([8, 128], BF16)) for _ in range(2)]
        KC = [carve([2, 1024], BF16) for _ in range(2)]
        VC = [carve([2, 1024], BF16) for _ in range(2)]
        KT = [carve([8, 256], BF16) for _ in range(2)]
        mixer_end = cur[0]
        KVST = [carve_at(KC[0].off + i * 2048, [512], F32) for i in range(4)]
        a0 = T5[0].off
        ATT = []
        for i in range(2):
            E_ = carve_at(a0, [512], F32); a0 += 2048
            EN_ = carve_at(a0, [512], F32); a0 += 2048
            SP_ = carve_at(a0, [512], BF16); a0 += 1024
            WT_ = carve_at(a0, [512], BF16); a0 += 1024
            ATT.append((E_, EN_, SP_, WT_))
        assert a0 <= T5[3].off
        E2 = [carve_at(T5[3].off + i * 2048, [512], F32) for i in range(2)]
        SP2 = [carve_at(RSTD.off + i * 1024, [512], BF16) for i in range(2)]
        WT2 = [carve_at(RSTD.off + 2048 + i * 1024, [512], BF16) for i in range(2)]
        ACC = carve_at(PH, [16, TBT], F32)
        ACTG = [carve_at(PH + ACC.nbytes + i * 4 * TBT * 2, [4, TBT], BF16) for i in range(2)]
        ffn_end = PH + ACC.nbytes + 2 * 4 * TBT * 2
        print("arena bytes: mixer_end", mixer_end, "ffn_end", ffn_end, "cap", AW * 4)
        assert max(mixer_end, ffn_end) <= AW * 4

        rr = {"t5": 0, "ws": 0, "psd": 0, "ps1": 0}

        def t5():
            i = rr["t5"]; rr["t5"] = (i + 1) % 4
            return T5[i]

        def psset():
            i = rr["psd"]; rr["psd"] = (i + 1) % 2
            return [PS[3 * i], PS[3 * i + 1], PS[3 * i + 2]]

        def ps1():
            i = rr["ps1"]; rr["ps1"] = (i + 1) % 8
            return i

        def tiles_of(T):
            out, t0 = [], 0
            while t0 < T:
                n = min(512, T - t0)
                out.append((t0, n)); t0 += n
            return out

        def load_w(src2d, r0, kc, c0, ncols):
            i = rr["ws"]; rr["ws"] = (i + 1) % 3
            slot = WS[i]
            assert kc * ncols <= 4096
            view = slot.ap[:, 0:kc * ncols].rearrange("p (k n) -> p k n", n=ncols)
            src = src2d[r0:r0 + kc * 128, c0:c0 + ncols].rearrange("(k p) n -> p k n", p=128)
            S.op("pool", lambda e: e.dma_start(out=view, in_=src), writes=[slot.whole()], dma="w%d" % i)
            return view, slot

        def mm_fm(pset, wview, wslot, col, rhsbuf, kcs, T, first=True, last=True, rhs_k0=0):
            tl = tiles_of(T)
            nk = len(kcs)
            for ki, k in enumerate(kcs):
                for ti, (t0, n) in enumerate(tl):
                    S.op("pe", (lambda e, ti=ti, t0=t0, n=n, ki=ki, k=k: e.matmul(
                        pset[ti].ap[:, 0:n], lhsT=wview[:, ki, col:col + 128], rhs=rhsbuf.ap[:, rhs_k0 + k, t0:t0 + n],
                        start=(first and ki == 0), stop=(last and ki == nk - 1))),
                        reads=[wslot.whole(), rhsbuf.reg(rhs_k0 + k, (t0, t0 + n))], writes=[pset[ti].reg((0, n))])

        S.op("pool", lambda e: e.memset(ident_b.ap, 1.0), writes=[ident_b.whole()])
        S.op("pool", lambda e: e.affine_select(out=ident_b.ap, in_=ident_b.ap, pattern=[[-1, 128]], compare_op=ALU.is_equal,
                                               fill=0.0, base=0, channel_multiplier=1), reads=[ident_b.whole()], writes=[ident_b.whole()])
        S.op("pool", lambda e: e.memset(ident_f.ap, 1.0), writes=[ident_f.whole()])
        S.op("pool", lambda e: e.affine_select(out=ident_f.ap, in_=ident_f.ap, pattern=[[-1, 128]], compare_op=ALU.is_equal,
                                               fill=0.0, base=0, channel_multiplier=1), reads=[ident_f.whole()], writes=[ident_f.whole()])
        S.op("pool", lambda e: e.memset(tri_b.ap, 1.0), writes=[tri_b.whole()])
        S.op("pool", lambda e: e.affine_select(out=tri_b.ap, in_=tri_b.ap, pattern=[[-1, 128]], compare_op=ALU.is_ge,
                                               fill=0.0, base=0, channel_multiplier=1), reads=[tri_b.whole()], writes=[tri_b.whole()])
        S.op("pool", lambda e: e.memset(m2_b.ap, 1.0), writes=[m2_b.whole()])
        S.op("pool", lambda e: e.affine_select(out=m2_b.ap, in_=m2_b.ap, pattern=[[1, 128]], compare_op=ALU.is_gt,
                                               fill=0.0, base=0, channel_multiplier=-1), reads=[m2_b.whole()], writes=[m2_b.whole()])
        S.op("pool", lambda e: e.memset(zero_b.ap, 0.0), writes=[zero_b.whole()])
        S.op("pool", lambda e: e.memset(ones_b.ap, 1.0), writes=[ones_b.whole()])
        S.op("pool", lambda e: e.memset(cm_f.ap, 1.0), writes=[cm_f.whole()])
        S.op("pool", lambda e: e.affine_select(out=cm_f.ap, in_=cm_f.ap, pattern=[[1, 128]], compare_op=ALU.is_gt,
                                               fill=0.0, base=0, channel_multiplier=-1), reads=[cm_f.whole()], writes=[cm_f.whole()])
        S.op("pool", lambda e: e.memset(cm8_f.ap, 1.0), writes=[cm8_f.whole()])
        for half in range(2):
            S.op("pool", lambda e, half=half: e.affine_select(
                out=cm8_f.ap[half * 64:(half + 1) * 64], in_=cm8_f.ap[half * 64:(half + 1) * 64], pattern=[[0, 8], [1, 64]],
                compare_op=ALU.is_gt, fill=0.0, base=0, channel_multiplier=-1), reads=[cm8_f.whole()], writes=[cm8_f.whole()])
        S.op("sp", lambda e: e.dma_start(out=flag_s.ap[:, 0:1], in_=flag_d[:, :]), writes=[flag_s.whole()], dma="c0")
        with nc.allow_non_contiguous_dma(reason="tiny per-feature vectors"):
            for i, src in enumerate([norm_mix[0], norm_mix[1], norm_ffn[0], norm_ffn[1], norm_final]):
                S.op("sp", lambda e, i=i, src=src: e.dma_start(out=gcol.ap[:, i, :], in_=src.rearrange("(c p) -> p c", p=128), allow_slow_non_contiguous=True),
                     writes=[gcol.whole()], dma="c0")
            for l in range(2):
                S.op("sp", lambda e, l=l: e.dma_start(out=bgc.ap[:, l, :], in_=b_gate[l].rearrange("(c p) -> p c", p=128), allow_slow_non_contiguous=True),
                     writes=[bgc.whole()], dma="c0")

        MISC = AW * 4 - 1664
        assert max(mixer_end, ffn_end) <= MISC
        eps_c = carve_at(MISC, [16], F32)
        rstdv = carve_at(MISC + 64, [16], F32)
        ssqv = carve_at(MISC + 128, [12, 4], F32)
        S.op("pool", lambda e: e.memset(eps_c.ap, EPS), writes=[eps_c.whole()])
        RESN = {id(resA): "resA", id(resB): "resB"}

        def rkey(res, c):
            return dkey(RESN[id(res)], c)

        def t5key(b):
            return "t5_%d" % T5.index(b)

        def load_x(xsrc, T):
            for c in range(NCH):
                S.op("sp", lambda e, c=c: e.dma_start(out=ACC.ap[:, c, 0:T], in_=xsrc[c * 128:(c + 1) * 128, 0:T]),
                     writes=[ACC.reg(c, (0, T))], dma="accl%d" % (c % 8))

        def norm_stats(res, T, from_sbuf):
            tl = tiles_of(T)
            if not from_sbuf:
                for c in range(NCH):
                    S.op("sp", lambda e, c=c: e.dma_start(out=ACC.ap[:, c, 0:T], in_=res[c * 128:(c + 1) * 128, 0:T]),
                         reads=[rkey(res, c)], writes=[ACC.reg(c, (0, T))], dma="accl%d" % (c % 8))
            pset = psset()
            for c in range(NCH):
                sqb = t5()
                sq_ap = sqb.ap.bitcast(BF16)
                S.op("act", lambda e, c=c, sq_ap=sq_ap: e.activation(out=sq_ap[:, 0:T], in_=ACC.ap[:, c, 0:T], func=AF.Square),
                     reads=[ACC.reg(c, (0, T))], writes=[sqb.reg((0, T // 2))])
                for ti, (t0, n) in enumerate(tl):
                    S.op("pe", lambda e, ti=ti, t0=t0, n=n, c=c, sq_ap=sq_ap: e.matmul(
                        pset[ti].ap[:, 0:n], lhsT=ones_b.ap, rhs=sq_ap[:, t0:t0 + n], start=(c == 0), stop=(c == NCH - 1)),
                        reads=[sqb.reg((0, T // 2)), ones_b.whole()], writes=[pset[ti].reg((0, n))])
            for ti, (t0, n) in enumerate(tl):
                S.op("act", lambda e, ti=ti, t0=t0, n=n: e.activation(out=RSTD.ap[:, t0:t0 + n], in_=pset[ti].ap[:, 0:n], func=AF.Ln, bias=eps_c.ap[:, 0:1], scale=1.0 / D),
                     reads=[pset[ti].reg((0, n)), eps_c.whole()], writes=[RSTD.reg((t0, t0 + n))])
                S.op("act", lambda e, t0=t0, n=n: e.activation(out=RSTD.ap[:, t0:t0 + n], in_=RSTD.ap[:, t0:t0 + n], func=AF.Exp, scale=-0.5),
                     reads=[RSTD.reg((t0, t0 + n))], writes=[RSTD.reg((t0, t0 + n))])

        def norm_apply(res, T, gi):
            for c in range(NCH):
                S.op("dve", lambda e, c=c: e.scalar_tensor_tensor(out=XN.ap[:, c, 0:T], in0=ACC.ap[:, c, 0:T], scalar=gcol.ap[:, gi, c:c + 1],
                                                                in1=RSTD.ap[:, 0:T], op0=ALU.mult, op1=ALU.mult),
                     reads=[ACC.reg(c, (0, T)), gcol.whole(), RSTD.reg((0, T))], writes=[XN.reg(c, (0, T))])

        def sgu_stage(l, T, write_sv):
            NTB = T // 128
            S.op("sp", lambda e: e.dma_start(out=sguG.ap, in_=sgu_norm[l].partition_broadcast(128)), writes=[sguG.whole()], dma="c1")
            tag = 'A%d_' % l if T == TP else 'B%d_' % l
            chk(tag + 'sgu_c', [('wmt', WmT), ('wbt', WbT), ('bsp', bSp), ('bss', bSs), ('sgug', sguG)])
            tl = tiles_of(T)
            for g4 in range(4):
                wv, wsl = load_w(w_in[l], 0, 16, C_U + g4 * 256, 256)
                chk(tag + 'sgu_w', [('ws', wsl)])
                for j in range(2):
                    g = g4 * 2 + j
                    pset = psset()
                    mm_fm(pset, wv, wsl, j * 128, XN, list(range(16)), T)
                    if DBG.get("stop") == tag + 'sgu_m':
                        tt = t5()
                        S.op("dve", lambda e: e.tensor_copy(out=tt.ap[:, 0:512], in_=pset[0].ap[:, 0:512]), reads=[pset[0].whole()], writes=[tt.whole()])
                        chk(tag + 'sgu_m', [('ps', tt)])
                    for ti, (t0, n) in enumerate(tl):
                        S.op("act", lambda e, ti=ti, t0=t0, n=n, g=g, pset=pset: e.activation(out=BU.ap[:, g, t0:t0 + n], in_=pset[ti].ap[:, 0:n], func=AF.Gelu_apprx_tanh),
                             reads=[pset[ti].reg((0, n))], writes=[BU.reg(g, (t0, t0 + n))])
                    if g == DBG.get('gstop', -1):
                        chk(tag + 'sgu_g', [('bu', BU)])
            chk(tag + 'sgu_u', [('bu', BU)])
            junk = T5[3]
            for wt in range(4):
                wv, wsl = load_w(w_in[l], 0, 16, C_VS + wt * 256, 256)
                for tb in range(NTB):
                    b = ps1()
                    for k in range(16):
                        S.op("pe", lambda e, b=b, k=k, tb=tb, wv=wv: e.matmul(PS[b].ap[:, 0:256], lhsT=XN.ap[:, k, tb * 128:(tb + 1) * 128], rhs=wv[:, k, :],
                                                                           start=(k == 0), stop=(k == 15)),
                             reads=[XN.reg(k, (tb * 128, (tb + 1) * 128)), wsl.whole()], writes=[PS[b].reg((0, 256))])
                    if tb < 8:
                        dst, dreg = BQ.ap[:, tb, wt * 256:(wt + 1) * 256], BQ.reg(tb, (wt * 256, (wt + 1) * 256))
                    else:
                        dst, dreg = VSF[tb - 8].ap[:, wt * 256:(wt + 1) * 256], VSF[tb - 8].reg((wt * 256, (wt + 1) * 256))
                    S.op("act", lambda e, b=b, dst=dst: e.activation(out=dst, in_=PS[b].ap[:, 0:256], func=AF.Gelu_apprx_tanh),
                         reads=[PS[b].reg((0, 256))], writes=[dreg])
                    S.op("act", lambda e, dst=dst, tb=tb, wt=wt: e.activation(out=junk.ap[:, 0:256], in_=dst, func=AF.Square, accum_out=ssqv.ap[:, tb, wt:wt + 1]),
                         reads=[dreg], writes=[junk.reg((0, 256)), ssqv.reg(tb, wt)])
            S.op("sp", lambda e: e.dma_start(out=bSp.ap, in_=b_spatial[l].partition_broadcast(128)), writes=[bSp.whole()], dma="c1")
            for hf in range(2):
                S.op("sp", lambda e, hf=hf: e.dma_start(out=bSs.ap[:, :, hf * 64:(hf + 1) * 64], in_=b_spatial[l][:, 0:64].partition_broadcast(128)),
                     writes=[bSs.whole()], dma="c1")
            S.op("sp", lambda e: e.dma_start(out=Wsp_f.ap, in_=w_spatial[l].rearrange("g t s -> t g s")), writes=[Wsp_f.whole()], dma="c1")
            S.op("dve", lambda e: e.memset(Wbd_f.ap, 0.0), writes=[Wbd_f.whole()])
            S.op("dve", lambda e: e.memset(WmT.ap, 0.0), writes=[WmT.whole()])
            for hf in range(2):
                S.op("sp", lambda e, hf=hf: e.dma_start(out=Wbd_f.ap[hf * 64:(hf + 1) * 64, :, hf * 64:(hf + 1) * 64],
                                                      in_=w_spatial[l][:, 0:64, 0:64].rearrange("g t s -> t g s")),
                     writes=[Wbd_f.whole()], dma="c1")
            for g in range(8):
                b = ps1()
                S.op("pe", lambda e, b=b, g=g: e.transpose(out=PS[b].ap[:, 0:128], in_=Wsp_f.ap[:, g, :], identity=ident_f.ap),
                     reads=[Wsp_f.whole(), ident_f.whole()], writes=[PS[b].reg((0, 128))])
                S.op("pe", lambda e, b=b, g=g: e.transpose(out=PS[b].ap[:, 128:256], in_=Wbd_f.ap[:, g, :], identity=ident_f.ap),
                     reads=[Wbd_f.whole(), ident_f.whole()], writes=[PS[b].reg((128, 256))])
                S.op("dve", lambda e, b=b, g=g: e.tensor_copy(out=WmT.ap[0:64, g, :], in_=PS[b].ap[0:64, 0:128]),
                     reads=[PS[b].reg((0, 256))], writes=[WmT.whole()])
                S.op("dve", lambda e, b=b, g=g: e.tensor_copy(out=WmT.ap[64:128, g, 64:128], in_=PS[b].ap[64:128, 64:128]),
                     reads=[PS[b].reg((0, 256))], writes=[WmT.whole()])
                S.op("dve", lambda e, b=b, g=g: e.tensor_copy(out=WbT.ap[:, g, :], in_=PS[b].ap[:, 128:256]),
                     reads=[PS[b].reg((0, 256))], writes=[WbT.whole()])
            S.op("dve", lambda e: e.tensor_reduce(out=rstdv.ap[:, 0:NTB], in_=ssqv.ap[:, 0:NTB, :], axis=mybir.AxisListType.X, op=ALU.add),
                 reads=[ssqv.whole()], writes=[rstdv.whole()])
            S.op("act", lambda e: e.activation(out=rstdv.ap[:, 0:NTB], in_=rstdv.ap[:, 0:NTB], func=AF.Ln, bias=eps_c.ap[:, 0:1], scale=1.0 / SBW),
                 reads=[rstdv.whole(), eps_c.whole()], writes=[rstdv.whole()])
            S.op("act", lambda e: e.activation(out=rstdv.ap[:, 0:NTB], in_=rstdv.ap[:, 0:NTB], func=AF.Exp, scale=-0.5),
                 reads=[rstdv.whole()], writes=[rstdv.whole()])
            for tb in range(NTB):
                if tb < 8:
                    S.op("dve", lambda e, tb=tb: e.scalar_tensor_tensor(out=BQ.ap[:, tb, :], in0=BQ.ap[:, tb, :], scalar=rstdv.ap[:, tb:tb + 1], in1=sguG.ap,
                                                                      op0=ALU.mult, op1=ALU.mult),
                         reads=[BQ.reg(tb), rstdv.whole(), sguG.whole()], writes=[BQ.reg(tb)])
                else:
                    vf = VSF[tb - 8]
                    S.op("dve", lambda e, tb=tb, vf=vf: e.scalar_tensor_tensor(out=vf.ap, in0=vf.ap, scalar=rstdv.ap[:, tb:tb + 1], in1=sguG.ap,
                                                                             op0=ALU.mult, op1=ALU.mult),
                         reads=[vf.whole(), rstdv.whole(), sguG.whole()], writes=[vf.whole()])
                    S.op("dve", lambda e, tb=tb, vf=vf: e.tensor_copy(out=BQ.ap[:, tb, :], in_=vf.ap), reads=[vf.whole()], writes=[BQ.reg(tb)])
                    if write_sv:
                        S.op("sp", lambda e, tb=tb, vf=vf: e.dma_start(out=sv_o[l, (tb - 8) * 128:(tb - 7) * 128, :], in_=vf.ap),
                             reads=[vf.whole()], dma="o_sv", is_out=True)
            chk(tag + 'sgu_vn', [('bq', BQ)])
            for g in range(8):
                for t4 in range(0, NTB, 4):
                    nb = min(4, NTB - t4)
                    b = ps1()
                    for j in range(nb):
                        tb = t4 + j
                        wt_ = WmT if tb < 8 else WbT
                        S.op("pe", lambda e, b=b, j=j, tb=tb, g=g, wt_=wt_: e.matmul(PS[b].ap[:, j * 128:(j + 1) * 128], lhsT=BQ.ap[:, tb, g * 128:(g + 1) * 128],
                                                                                   rhs=wt_.ap[:, g, :], start=True, stop=True),
                             reads=[BQ.reg(tb, (g * 128, (g + 1) * 128)), wt_.whole()], writes=[PS[b].reg((j * 128, (j + 1) * 128))])
                    tmp = t5()
                    bs_ = bSp if t4 < 8 else bSs
                    S.op("dve", lambda e, b=b, g=g, bs_=bs_, tmp=tmp, nb=nb: e.tensor_tensor(
                        out=tmp.ap[:, 0:nb * 128].rearrange("p (b t) -> p b t", t=128), in0=PS[b].ap[:, 0:nb * 128].rearrange("p (b t) -> p b t", t=128),
                        in1=bs_.ap[:, g:g + 1, :].to_broadcast([128, nb, 128]), op=ALU.add),
                        reads=[PS[b].reg((0, nb * 128)), bs_.whole()], writes=[tmp.reg((0, nb * 128))])
                    S.op("dve", lambda e, g=g, t4=t4, nb=nb, tmp=tmp: e.tensor_tensor(out=BU.ap[:, g, t4 * 128:(t4 + nb) * 128], in0=BU.ap[:, g, t4 * 128:(t4 + nb) * 128],
                                                                                    in1=tmp.ap[:, 0:nb * 128], op=ALU.mult),
                         reads=[BU.reg(g, (t4 * 128, (t4 + nb) * 128)), tmp.reg((0, nb * 128))], writes=[BU.reg(g, (t4 * 128, (t4 + nb) * 128))])
                    if g * 10 + t4 == DBG.get('gstop', -1):
                        chk(tag + 'sgu_s', [('bu', BU), ('tmp', tmp)])

        def qkv_stage(l, T, grpA, need_q=True):
            NTB = T // 128
            tl = tiles_of(T)
            BQf = BQ.ap.rearrange("p a b -> p (a b)").rearrange("p (h t) -> p h t", t=TBT)

            def bqreg(h, t0, t1):
                return ("sb", BQ.off + (h * TBT + t0) * 2, BQ.off + (h * TBT + t1) * 2)
            for which, cbase in (("q", C_Q), ("k", C_K)):
                if which == "q" and not need_q:
                    continue
                for g4 in range(4):
                    wv, wsl = load_w(w_in[l], 0, 16, cbase + g4 * 256, 256)
                    for j in range(2):
                        h = g4 * 2 + j
                        pset = psset()
                        mm_fm(pset, wv, wsl, j * 128, XN, list(range(16)), T)
                        for ti, (t0, n) in enumerate(tl):
                            if which == "q":
                                S.op("act", lambda e, ti=ti, t0=t0, n=n, h=h, pset=pset: e.copy(out=BQf[:, h, t0:t0 + n], in_=pset[ti].ap[:, 0:n]),
                                     reads=[pset[ti].reg((0, n))], writes=[bqreg(h, t0, t0 + n)])
                            elif grpA:
                                S.op("dve", lambda e, ti=ti, t0=t0, n=n, h=h, pset=pset: e.tensor_copy(out=BK.ap[:, h, t0:t0 + n], in_=pset[ti].ap[:, 0:n]),
                                     reads=[pset[ti].reg((0, n))], writes=[BK.reg(h, (t0, t0 + n))])
                        if which == "k" and grpA:
                            S.op("sp", lambda e, h=h: e.dma_start(out=pk[l, h * 128:(h + 1) * 128, :], in_=BK.ap[:, h, 0:TP]),
                                 reads=[BK.reg(h, (0, TP))], writes=[dkey("pk%d" % l, h)], dma="pkw")
                        if which == "k" and not grpA:
                            kf = t5()
                            for ti, (t0, n) in enumerate(tl):
                                S.op("act", lambda e, ti=ti, t0=t0, n=n, pset=pset, kf=kf: e.copy(out=kf.ap[:, t0:t0 + n], in_=pset[ti].ap[:, 0:n]),
                                     reads=[pset[ti].reg((0, n))], writes=[kf.reg((t0, t0 + n))])
                                S.op("dve", lambda e, t0=t0, n=n, h=h, kf=kf: e.tensor_copy(out=BK.ap[:, h, t0:t0 + n], in_=kf.ap[:, t0:t0 + n]),
                                     reads=[kf.reg((t0, t0 + n))], writes=[BK.reg(h, (t0, t0 + n))])
                            S.op("sp", lambda e, h=h, kf=kf: e.dma_start(out=k_o[l, h * 128:(h + 1) * 128, 0:T], in_=kf.ap[:, 0:T]),
                                 reads=[kf.reg((0, T))], dma=t5key(kf), is_out=True)
            for which, cbase in (("v", C_V),):
                for wt in range(4):
                    wv, wsl = load_w(w_in[l], 0, 16, cbase + wt * 256, 256)
                    for tb in range(NTB):
                        b = ps1()
                        for k in range(16):
                            S.op("pe", lambda e, b=b, k=k, tb=tb, wv=wv: e.matmul(PS[b].ap[:, 0:256], lhsT=XN.ap[:, k, tb * 128:(tb + 1) * 128], rhs=wv[:, k, :],
                                                                               start=(k == 0), stop=(k == 15)),
                                 reads=[XN.reg(k, (tb * 128, (tb + 1) * 128)), wsl.whole()], writes=[PS[b].reg((0, 256))])
                        if not grpA:
                            stg = KVST[(tb * 4 + wt) % 4]
                            S.op("act", lambda e, b=b, stg=stg: e.copy(out=stg.ap[:, 0:256], in_=PS[b].ap[:, 0:256]),
                                 reads=[PS[b].reg((0, 256))], writes=[stg.reg((0, 256))])
                            dst = (k_o if which == "k" else v_o)
                            S.op("sp", lambda e, stg=stg, dst=dst, tb=tb, wt=wt: e.dma_start(out=dst[l, tb * 128:(tb + 1) * 128, wt * 256:(wt + 1) * 256], in_=stg.ap[:, 0:256]),
                                 reads=[stg.reg((0, 256))], dma="o_kv%d" % ((tb * 4 + wt) % 4), is_out=True)
                        if which == "v":
                            if grpA:
                                S.op("dve", lambda e, b=b, tb=tb, wt=wt: e.tensor_scalar(out=BV.ap[:, tb, wt * 256:(wt + 1) * 256], in0=PS[b].ap[:, 0:256],
                                                                                       scalar1=flag_s.ap[:, 0:1], scalar2=None, op0=ALU.mult),
                                     reads=[PS[b].reg((0, 256)), flag_s.whole()], writes=[BV.reg(tb, (wt * 256, (wt + 1) * 256))])
                            else:
                                S.op("dve", lambda e, b=b, tb=tb, wt=wt: e.tensor_copy(out=BV.ap[:, tb, wt * 256:(wt + 1) * 256], in_=PS[b].ap[:, 0:256]),
                                     reads=[PS[b].reg((0, 256))], writes=[BV.reg(tb, (wt * 256, (wt + 1) * 256))])
            if grpA:
                for tb in range(NTB):
                    S.op("sp", lambda e, tb=tb: e.dma_start(out=pv[l, tb * 128:(tb + 1) * 128, :], in_=BV.ap[:, tb, :]),
                         reads=[BV.reg(tb)], writes=[dkey("pv%d" % l, tb)], dma="pvw")

        zrhs = carve_at(MISC + 512, [512], BF16)
        assert MISC + 512 + 1024 <= AW * 4
        S.op("pool", lambda e: e.memset(zrhs.ap, 0.0), writes=[zrhs.whole()])

        def zero_bank(b):
            S.op("pe", lambda e: e.matmul(PS[b].ap[:, 0:512], lhsT=zero_b.ap, rhs=zrhs.ap, start=True, stop=False, skip_group_check=True),
                 reads=[zero_b.whole(), zrhs.whole()], writes=[PS[b].whole()])

        def run_units(units):
            for u in units:
                zero_bank(u["C"]); zero_bank(u["O"])
            nmax = max(len(u["tiles"]) for u in units)

            def geo(u, i):
                s_fn, mask_fn, pv_fn, nparts, p0, n0 = u["tiles"][i]
                return slice(p0, p0 + nparts), slice(n0, 512), (n0, 512)

            def bufs(u, i):
                E_, EN_, SP_, WT_ = ATT[u["att"]]
                if i % 2 == 1:
                    E_, SP_, WT_ = E2[u["att"]], SP2[u["att"]], WT2[u["att"]]
                return E_, EN_, SP_, WT_

            def front_exp(u, i):
                s_fn, mask_fn, pv_fn, nparts, p0, n0 = u["tiles"][i]
                E_, EN_, SP_, WT_ = bufs(u, i)
                sb = u["S"]
                pr, cs, rg = geo(u, i)
                S.op("act", lambda e: e.activation(out=E_.ap[pr, cs], in_=PS[sb].ap[pr, cs], func=AF.Exp, scale=SCALE),
                     reads=[PS[sb].reg(rg)], writes=[E_.reg(rg)])
                if mask_fn is not None:
                    mask_fn(E_)

            def front_ln(u, i):
                E_, EN_, SP_, WT_ = bufs(u, i)
                pr, cs, rg = geo(u, i)
                S.op("act", lambda e: e.activation(out=SP_.ap[pr, cs], in_=E_.ap[pr, cs], func=AF.Ln, bias=1.0, scale=1.0),
                     reads=[E_.reg(rg)], writes=[SP_.reg(rg)])

            for u in units:
                u["tiles"][0][0](u["S"])
            for u in units:
                front_exp(u, 0)
            for u in units:
                front_ln(u, 0)
            for i in range(nmax):
                act = [u for u in units if i < len(u["tiles"])]
                for u in act:
                    E_, EN_, SP_, WT_ = bufs(u, i)
                    cb = u["C"]
                    pr, cs, rg = geo(u, i)
                    S.op("pe", lambda e, cb=cb, SP_=SP_, pr=pr, cs=cs: e.matmul(PS[cb].ap[:, cs], lhsT=tri_b.ap[pr, :], rhs=SP_.ap[pr, cs], start=False, stop=False, skip_group_check=True),
                         reads=[SP_.reg(rg), tri_b.whole()], writes=[PS[cb].reg(rg)])
                    if i + 1 < len(u["tiles"]):
                        u["tiles"][i + 1][0](u["S"])
                for u in act:
                    E_, EN_, SP_, WT_ = bufs(u, i)
                    cb = u["C"]
                    pr, cs, rg = geo(u, i)
                    S.op("act", lambda e, cb=cb, EN_=EN_, pr=pr, cs=cs: e.activation(out=EN_.ap[pr, cs], in_=PS[cb].ap[pr, cs], func=AF.Exp, scale=-1.0),
                         reads=[PS[cb].reg(rg)], writes=[EN_.reg(rg)])
                for u in act:
                    s_fn, mask_fn, pv_fn, nparts, p0, n0 = u["tiles"][i]
                    E_, EN_, SP_, WT_ = bufs(u, i)
                    cb = u["C"]
                    pr, cs, rg = geo(u, i)
                    S.op("pe", lambda e, cb=cb, SP_=SP_, pr=pr, cs=cs: e.matmul(PS[cb].ap[:, cs], lhsT=m2_b.ap[pr, :], rhs=SP_.ap[pr, cs], start=False, stop=False, skip_group_check=True),
                         reads=[SP_.reg(rg), m2_b.whole()], writes=[PS[cb].reg(rg)])
                    S.op("dve", lambda e, E_=E_, EN_=EN_, WT_=WT_, pr=pr, cs=cs: e.tensor_tensor(out=WT_.ap[pr, cs], in0=E_.ap[pr, cs], in1=EN_.ap[pr, cs], op=ALU.mult),
                         reads=[E_.reg(rg), EN_.reg(rg)], writes=[WT_.reg(rg)])
                    pv_fn(u["O"], WT_)
                nxt = [u for u in act if i + 1 < len(u["tiles"])]
                for u in nxt:
                    front_exp(u, i + 1)
                for u in nxt:
                    front_ln(u, i + 1)
            for u in units:
                u["fin"](u["O"])

        def attention_stage(l, T, use_prev, has_s):
            BQf = BQ.ap.rearrange("p a b -> p (a b)").rearrange("p (h t) -> p h t", t=TBT)

            def bqreg(h, t0, t1):
                return ("sb", BQ.off + (h * TBT + t0) * 2, BQ.off + (h * TBT + t1) * 2)

            def prompt_unit(h, qg, slot, kp, vp):
                q0 = qg * 512
                blocks = [("own", kb) for kb in range(qg * 4 + 3, -1, -1)]
                if use_prev:
                    blocks += [("prev", kb) for kb in range(7, -1, -1)]
                tiles = []
                for kind, kb in blocks:
                    if kind == "own":
                        n0 = max(q0, kb * 128) - q0
                        diag = kb * 128 >= q0
                        klhs, klr = BK.ap[:, h, kb * 128:(kb + 1) * 128], BK.reg(h, (kb * 128, (kb + 1) * 128))
                        vlhs, vlr = BV.ap[:, kb, h * 128:(h + 1) * 128], BV.reg(kb, (h * 128, (h + 1) * 128))
                    else:
                        n0, diag = 0, False
                        klhs, klr = kp.ap[:, kb * 128:(kb + 1) * 128], kp.reg((kb * 128, (kb + 1) * 128))
                        vlhs, vlr = vp.ap[:, kb, :], vp.reg(kb)

                    def s_fn(sb, klhs=klhs, klr=klr, n0=n0):
                        S.op("pe", lambda e: e.matmul(PS[sb].ap[:, n0:512], lhsT=klhs, rhs=BQf[:, h, q0 + n0:q0 + 512], start=True, stop=True),
                             reads=[klr, bqreg(h, q0 + n0, q0 + 512)], writes=[PS[sb].reg((n0, 512))])

                    def m_fn(E_, n0=n0):
                        S.op("dve", lambda e: e.tensor_tensor(out=E_.ap[:, n0:n0 + 128], in0=E_.ap[:, n0:n0 + 128], in1=cm_f.ap, op=ALU.mult),
                             reads=[E_.reg((n0, n0 + 128)), cm_f.whole()], writes=[E_.reg((n0, n0 + 128))])

                    def pv_fn(ob, WT_, vlhs=vlhs, vlr=vlr, n0=n0):
                        S.op("pe", lambda e: e.matmul(PS[ob].ap[:, n0:512], lhsT=vlhs, rhs=WT_.ap[:, n0:512], start=False, stop=False, skip_group_check=True),
                             reads=[vlr, WT_.reg((n0, 512))], writes=[PS[ob].reg((n0, 512))])
                    tiles.append((s_fn, m_fn if diag else None, pv_fn, 128, 0, n0))

                def fin(ob):
                    S.op("act", lambda e: e.copy(out=BQf[:, h, q0:q0 + 512], in_=PS[ob].ap[:, 0:512]),
                         reads=[PS[ob].whole()], writes=[bqreg(h, q0, q0 + 512)])
                return {"S": slot, "C": 2 + slot, "O": 4 + slot, "att": slot, "tiles": tiles, "fin": fin}

            for h2 in range(0, NH, 2):
                kvs = []
                for j in range(2):
                    h = h2 + j
                    kp, vp = KPV[j]
                    if use_prev:
                        S.op("sp", lambda e, kp=kp, h=h: e.dma_start(out=kp.ap, in_=pk[l, h * 128:(h + 1) * 128, :]),
                             reads=[dkey("pk%d" % l, h)], writes=[kp.whole()], dma="kp%d" % j)
                        S.op("sp", lambda e, vp=vp, h=h: e.dma_start(out=vp.ap, in_=pv[l, :, h * 128:(h + 1) * 128].rearrange("(b p) d -> p b d", p=128)),
                             reads=[dkey("pv%d" % l, tb) for tb in range(8)], writes=[vp.whole()], dma="vp%d" % j)
                    kvs.append((kp, vp))
                for qg in range(2):
                    run_units([prompt_unit(h2 + j, qg, j, kvs[j][0], kvs[j][1]) for j in range(2)])
            if not has_s:
                return
            sunits = []
            for s in range(4):
                slot = s % 2
                qc = TP + s * 64
                p0 = (s % 2) * 64
                tbn = 8 + s // 2
                kc0 = TP + (s // 2) * 128
                tiles = []

                def s_new(sb, s=s, qc=qc, kc0=kc0):
                    for h in range(NH):
                        if s % 2 == 0:
                            S.op("pe", lambda e, h=h: e.matmul(PS[sb].ap[0:64, h * 64:(h + 1) * 64], lhsT=BK.ap[:, h, qc:qc + 64], rhs=BQf[:, h, qc:qc + 64], start=True, stop=True),
                                 reads=[BK.reg(h, (qc, qc + 64)), bqreg(h, qc, qc + 64)], writes=[PS[sb].reg((h * 64, (h + 1) * 64))])
                        else:
                            S.op("pe", lambda e, h=h: e.matmul(PS[sb].ap[:, h * 64:(h + 1) * 64], lhsT=BK.ap[:, h, kc0:kc0 + 128], rhs=BQf[:, h, qc:qc + 64], start=True, stop=True),
                                 reads=[BK.reg(h, (kc0, kc0 + 128)), bqreg(h, qc, qc + 64)], writes=[PS[sb].reg((h * 64, (h + 1) * 64))])

                def m_new(E_, p0=p0):
                    S.op("dve", lambda e: e.tensor_tensor(out=E_.ap[p0:p0 + 64, :].rearrange("p (h t) -> p h t", t=64), in0=E_.ap[p0:p0 + 64, :].rearrange("p (h t) -> p h t", t=64),
                                                          in1=cm8_f.ap[p0:p0 + 64], op=ALU.mult),
                         reads=[E_.whole(), cm8_f.whole()], writes=[E_.whole()])

                def pv_new(ob, WT_, p0=p0, tbn=tbn):
                    for h in range(NH):
                        S.op("pe", lambda e, h=h: e.matmul(PS[ob].ap[:, h * 64:(h + 1) * 64], lhsT=BV.ap[p0:p0 + 64, tbn, h * 128:(h + 1) * 128], rhs=WT_.ap[p0:p0 + 64, h * 64:(h + 1) * 64],
                                                          start=False, stop=False, skip_group_check=True),
                             reads=[BV.reg(tbn, (h * 128, (h + 1) * 128)), WT_.whole()], writes=[PS[ob].reg((h * 64, (h + 1) * 64))])
                tiles.append((s_new, m_new, pv_new, 64, p0, 0))
                kt = KT[slot]
                for cg in range(7, -1, -1):
                    kcb, vcb = KC[slot], VC[slot]
                    for bi in (1, 0):
                        def s_c(sb, bi=bi, cg=cg, kcb=kcb, vcb=vcb, s=s, qc=qc, kt=kt, slot=slot):
                            if bi == 1:
                                S.op("pool", lambda e: e.dma_start(out=kcb.ap, in_=ck[l, s, cg * 256:(cg + 1) * 256, :].rearrange("(b p) f -> p b f", p=128)),
                                     writes=[kcb.whole()], dma="kc%d" % slot)
                                S.op("pool", lambda e: e.dma_start(out=vcb.ap, in_=cv[l, s, cg * 256:(cg + 1) * 256, :].rearrange("(b p) f -> p b f", p=128)),
                                     writes=[vcb.whole()], dma="vc%d" % slot)
                                for b2 in range(2):
                                    tb_ = 6 + slot
                                    for h in range(NH):
                                        S.op("pe", lambda e, b2=b2, h=h, tb_=tb_: e.transpose(out=PSB[tb_][:, h * 128:(h + 1) * 128], in_=kcb.ap[:, b2, h * 128:(h + 1) * 128], identity=ident_b.ap),
                                             reads=[kcb.reg(b2, (h * 128, (h + 1) * 128)), ident_b.whole()], writes=[PS[tb_].reg((h * 64, (h + 1) * 64))])
                                    if b2 == 0:
                                        S.op("dve", lambda e, b2=b2, tb_=tb_: e.tensor_copy(out=kt.ap[:, :, b2 * 128:(b2 + 1) * 128], in_=PSB[tb_].rearrange("p (h k) -> p h k", k=128)),
                                             reads=[PS[tb_].whole()], writes=[kt.whole()])
                                    else:
                                        S.op("act", lambda e, b2=b2, tb_=tb_: e.copy(out=kt.ap[:, :, b2 * 128:(b2 + 1) * 128], in_=PSB[tb_].rearrange("p (h k) -> p h k", k=128)),
                                             reads=[PS[tb_].whole()], writes=[kt.whole()])
                            for h in range(NH):
                                S.op("pe", lambda e, h=h: e.matmul(PS[sb].ap[:, h * 64:(h + 1) * 64], lhsT=kt.ap[:, h, bi * 128:(bi + 1) * 128], rhs=BQf[:, h, qc:qc + 64], start=True, stop=True),
                                     reads=[kt.whole(), bqreg(h, qc, qc + 64)], writes=[PS[sb].reg((h * 64, (h + 1) * 64))])

                        def pv_c(ob, WT_, bi=bi, vcb=vcb):
                            for h in range(NH):
                                S.op("pe", lambda e, h=h: e.matmul(PS[ob].ap[:, h * 64:(h + 1) * 64], lhsT=vcb.ap[:, bi, h * 128:(h + 1) * 128], rhs=WT_.ap[:, h * 64:(h + 1) * 64],
                                                                  start=False, stop=False, skip_group_check=True),
                                     reads=[vcb.reg(bi, (h * 128, (h + 1) * 128)), WT_.whole()], writes=[PS[ob].reg((h * 64, (h + 1) * 64))])
                        tiles.append((s_c, None, pv_c, 128, 0, 0))

                def fin(ob, qc=qc):
                    S.op("act", lambda e: e.copy(out=BQf[:, :, qc:qc + 64], in_=PS[ob].ap[:, 0:512].rearrange("p (h t) -> p h t", t=64)),
                         reads=[PS[ob].whole()], writes=[("sb", BQ.off, BQ.off + BQ.nbytes)])
                sunits.append({"S": slot, "C": 2 + slot, "O": 4 + slot, "att": slot, "tiles": tiles, "fin": fin})
                if slot == 1:
                    run_units(sunits)
                    sunits = []

        def merge_stage(l, T, res, res_in=None):
            tl = tiles_of(T)
            assert BV.off == BK.off + BK.nbytes
            MG = carve_at(BK.off, [16, TBT], BF16)
            BQf = Buf("sb", BQ.ap.rearrange("p a b -> p (a b)").rearrange("p (h t) -> p h t", t=TBT), BQ.off, [8, TBT], 2)
            for m2 in range(8):
                wga, sga = load_w(w_in[l], 0, 16, C_GA + m2 * 256, 256)
                wa, sa = load_w(w_a[l], 0, 8, m2 * 256, 256)
                tmps = []
                for j in range(2):
                    m = m2 * 2 + j
                    pg = psset(); mm_fm(pg, wga, sga, j * 128, XN, list(range(16)), T)
                    tg = t5()
                    for ti, (t0, n) in enumerate(tl):
                        S.op("act", lambda e, ti=ti, t0=t0, n=n, m=m, pg=pg, tg=tg: e.activation(out=tg.ap[:, t0:t0 + n], in_=pg[ti].ap[:, 0:n], func=AF.Sigmoid, bias=bgc.ap[:, l, m:m + 1], scale=1.0),
                             reads=[pg[ti].reg((0, n)), bgc.whole()], writes=[tg.reg((t0, t0 + n))])
                    pa = psset(); mm_fm(pa, wa, sa, j * 128, BQf, list(range(8)), T)
                    for ti, (t0, n) in enumerate(tl):
                        S.op("dve", lambda e, ti=ti, t0=t0, n=n, pa=pa, tg=tg: e.tensor_tensor(out=tg.ap[:, t0:t0 + n], in0=pa[ti].ap[:, 0:n], in1=tg.ap[:, t0:t0 + n], op=ALU.mult),
                             reads=[pa[ti].reg((0, n)), tg.reg((t0, t0 + n))], writes=[tg.reg((t0, t0 + n))])
                    tmps.append(tg)
                wgb, sgb = load_w(w_in[l], 0, 16, C_GB + m2 * 256, 256)
                wb, sb_ = load_w(w_b[l], 0, 8, m2 * 256, 256)
                for j in range(2):
                    m = m2 * 2 + j
                    tg = tmps[j]
                    pg = psset(); mm_fm(pg, wgb, sgb, j * 128, XN, list(range(16)), T)
                    t2 = t5()
                    for ti, (t0, n) in enumerate(tl):
                        S.op("act", lambda e, ti=ti, t0=t0, n=n, m=m, pg=pg, t2=t2: e.activation(out=t2.ap[:, t0:t0 + n], in_=pg[ti].ap[:, 0:n], func=AF.Sigmoid, bias=bgc.ap[:, l, 16 + m:17 + m], scale=1.0),
                             reads=[pg[ti].reg((0, n)), bgc.whole()], writes=[t2.reg((t0, t0 + n))])
                    pb = psset(); mm_fm(pb, wb, sb_, j * 128, BU, list(range(8)), T)
                    for ti, (t0, n) in enumerate(tl):
                        S.op("dve", lambda e, ti=ti, t0=t0, n=n, pb=pb, t2=t2: e.tensor_tensor(out=t2.ap[:, t0:t0 + n], in0=pb[ti].ap[:, 0:n], in1=t2.ap[:, t0:t0 + n], op=ALU.mult),
                             reads=[pb[ti].reg((0, n)), t2.reg((t0, t0 + n))], writes=[t2.reg((t0, t0 + n))])
                        S.op("dve", lambda e, t0=t0, n=n, m=m, tg=tg, t2=t2: e.tensor_tensor(out=MG.ap[:, m, t0:t0 + n], in0=tg.ap[:, t0:t0 + n], in1=t2.ap[:, t0:t0 + n], op=ALU.add),
                             reads=[tg.reg((t0, t0 + n)), t2.reg((t0, t0 + n))], writes=[MG.reg(m, (t0, t0 + n))])
            for m2 in range(8):
                wo, so = load_w(w_out[l], 0, 16, m2 * 256, 256)
                for j in range(2):
                    m = m2 * 2 + j
                    po = psset(); mm_fm(po, wo, so, j * 128, MG, list(range(16)), T)
                    xt = t5()
                    rsrc = res if res_in is None else res_in
                    S.op("sp", lambda e, xt=xt, m=m, rsrc=rsrc: e.dma_start(out=xt.ap[:, 0:T], in_=rsrc[m * 128:(m + 1) * 128, 0:T]),
                         reads=([rkey(res, m)] if res_in is None else []), writes=[xt.reg((0, T))], dma=t5key(xt))
                    for ti, (t0, n) in enumerate(tl):
                        S.op("dve", lambda e, ti=ti, t0=t0, n=n, po=po, xt=xt: e.tensor_tensor(out=xt.ap[:, t0:t0 + n], in0=po[ti].ap[:, 0:n], in1=xt.ap[:, t0:t0 + n], op=ALU.add),
                             reads=[po[ti].reg((0, n)), xt.reg((t0, t0 + n))], writes=[xt.reg((t0, t0 + n))])
                    S.op("sp", lambda e, xt=xt, m=m: e.dma_start(out=res[m * 128:(m + 1) * 128, 0:T], in_=xt.ap[:, 0:T]),
                         reads=[xt.reg((0, T))], writes=[rkey(res, m)], dma=t5key(xt))

        def ffn_stage(l, T, res, store_res=True):
            tl = tiles_of(T)
            def gate_up(fg):
                ag = ACTG[fg % 2]
                for half in range(2):
                    wg, sg_ = load_w(w_gu[l], 0, 16, fg * 512 + half * 256, 256)
                    wu, su_ = load_w(w_gu[l], 0, 16, DFF + fg * 512 + half * 256, 256)
                    for j in range(2):
                        jj = half * 2 + j
                        pg = psset(); mm_fm(pg, wg, sg_, j * 128, XN, list(range(16)), T)
                        tg = t5()
                        for ti, (t0, n) in enumerate(tl):
                            S.op("act", lambda e, ti=ti, t0=t0, n=n, pg=pg, tg=tg: e.activation(out=tg.ap[:, t0:t0 + n], in_=pg[ti].ap[:, 0:n], func=AF.Silu),
                                 reads=[pg[ti].reg((0, n))], writes=[tg.reg((t0, t0 + n))])
                        pu = psset(); mm_fm(pu, wu, su_, j * 128, XN, list(range(16)), T)
                        for ti, (t0, n) in enumerate(tl):
                            S.op("dve", lambda e, ti=ti, t0=t0, n=n, pu=pu, tg=tg, jj=jj, ag=ag: e.tensor_tensor(out=ag.ap[:, jj, t0:t0 + n], in0=pu[ti].ap[:, 0:n], in1=tg.ap[:, t0:t0 + n], op=ALU.mult),
                                 reads=[pu[ti].reg((0, n)), tg.reg((t0, t0 + n))], writes=[ag.reg(jj, (t0, t0 + n))])
            def down(fg):
                ag = ACTG[fg % 2]
                for mh in range(2):
                    wd, sd = load_w(w_dn[l], fg * 512, 4, mh * 1024, 1024)
                    for j in range(8):
                        m = mh * 8 + j
                        pd = psset(); mm_fm(pd, wd, sd, j * 128, ag, list(range(4)), T)
                        for ti, (t0, n) in enumerate(tl):
                            if fg == 0:
                                S.op("act", lambda e, ti=ti, t0=t0, n=n, pd=pd, m=m: e.copy(out=ACC.ap[:, m, t0:t0 + n], in_=pd[ti].ap[:, 0:n]),
                                     reads=[pd[ti].reg((0, n))], writes=[ACC.reg(m, (t0, t0 + n))])
                            else:
                                S.op("dve", lambda e, ti=ti, t0=t0, n=n, pd=pd, m=m: e.tensor_tensor(out=ACC.ap[:, m, t0:t0 + n], in0=pd[ti].ap[:, 0:n], in1=ACC.ap[:, m, t0:t0 + n], op=ALU.add),
                                     reads=[pd[ti].reg((0, n)), ACC.reg(m, (t0, t0 + n))], writes=[ACC.reg(m, (t0, t0 + n))])
            gate_up(0)
            for fg in range(11):
                if fg + 1 < 11:
                    gate_up(fg + 1)
                down(fg)
            xts = {}

            def ld(m):
                xt = t5()
                xts[m] = xt
                S.op("sp", lambda e, xt=xt, m=m: e.dma_start(out=xt.ap[:, 0:T], in_=res[m * 128:(m + 1) * 128, 0:T]),
                     reads=[rkey(res, m)], writes=[xt.reg((0, T))], dma=t5key(xt))
            for m in range(4):
                ld(m)
            for m in range(NCH):
                xt = xts[m]
                S.op("dve", lambda e, xt=xt, m=m: e.tensor_tensor(out=ACC.ap[:, m, 0:T], in0=ACC.ap[:, m, 0:T], in1=xt.ap[:, 0:T], op=ALU.add),
                     reads=[ACC.reg(m, (0, T)), xt.reg((0, T))], writes=[ACC.reg(m, (0, T))])
                if m + 4 < NCH:
                    ld(m + 4)
            if store_res:
                for m in range(NCH):
                    S.op("sp", lambda e, m=m: e.dma_start(out=res[m * 128:(m + 1) * 128, 0:T], in_=ACC.ap[:, m, 0:T]),
                         reads=[ACC.reg(m, (0, T))], writes=[rkey(res, m)], dma="accs%d" % (m % 4))

        def final_stage(res, T):
            norm_stats(res, T, True)
            NTB = T // 128
            yv = y_o.rearrange("(tb p) f -> p tb f", p=128)
            for c in range(NCH):
                xt = t5()
                S.op("dve", lambda e, xt=xt, c=c: e.scalar_tensor_tensor(out=xt.ap[:, 0:T], in0=ACC.ap[:, c, 0:T], scalar=gcol.ap[:, 4, c:c + 1],
                                                                       in1=RSTD.ap[:, 0:T], op0=ALU.mult, op1=ALU.mult),
                     reads=[ACC.reg(c, (0, T)), gcol.whole(), RSTD.reg((0, T))], writes=[xt.reg((0, T))])
                S.op("sp", lambda e, xt=xt, c=c: e.dma_start(out=y_o[c * 128:(c + 1) * 128, 0:T], in_=xt.ap[:, 0:T]),
                     reads=[xt.reg((0, T))], dma=t5key(xt), is_out=True)

        class Stop(Exception):
            pass

        def dump(buf, name):
            dt_ = F32 if buf.es == 4 else BF16
            t = nc.dram_tensor("dbg_" + name, [128] + list(buf.shape), dt_, kind="ExternalOutput").ap()
            S.op("sp", lambda e: e.dma_start(out=t, in_=buf.ap), reads=[buf.whole()], dma="dbg_" + name, is_out=True)

        def chk(name, bufs):
            if DBG.get("stop") == name or name in DBG.get("dumps", ()):
                for nm, bf in bufs:
                    dump(bf, name + "_" + nm)
            if DBG.get("stop") == name:
                raise Stop()

        def layer(l, T, res, grpA, last=False, xin=None):
            tag = "%s%d_" % ("A" if grpA else "B", l)
            norm_stats(res, T, True)
            norm_apply(res, T, l)
            chk(tag + "norm", [("xn", XN), ("rstd", RSTD)])
            sgu_stage(l, T, write_sv=not grpA)
            chk(tag + "sgu", [("bu", BU), ("bq", BQ)])
            qkv_stage(l, T, grpA)
            chk(tag + "qkv", [("bq", BQ), ("bk", BK), ("bv", BV)])
            attention_stage(l, T, use_prev=not grpA, has_s=not grpA)
            chk(tag + "att", [("bq", BQ)])
            merge_stage(l, T, res, res_in=xin)
            chk(tag + "merge", [("bk", BK), ("bv", BV)])
            norm_stats(res, T, False)
            norm_apply(res, T, 2 + l)
            ffn_stage(l, T, res, store_res=not last)
            chk(tag + "ffn", [])

        try:
            load_x(xa, TP)
            layer(0, TP, resA, True, xin=xa)
            norm_stats(resA, TP, True)
            norm_apply(resA, TP, 1)
            qkv_stage(1, TP, True, need_q=False)
            chk("A1_kv", [])
            load_x(xb, TBT)
            layer(0, TBT, resB, False, xin=xb)
            layer(1, TBT, resB, False, last=True)
            final_stage(resB, TBT)
        except Stop:
            pass
        S.finish()
        S.emit(nc)
    return nc


_NC_CACHE = {}


def kernel(x_prompt, x_sample, cache_k, cache_v, norm_mix, w_in, b_gate, sgu_norm, w_spatial, b_spatial,
           w_branch_a, w_branch_b, w_out, norm_ffn, w_gate_up, w_down, norm_final):
    f = lambda a: np.ascontiguousarray(np.asarray(a, dtype=np.float32))
    x_prompt, x_sample, cache_k, cache_v = f(x_prompt), f(x_sample), f(cache_k), f(cache_v)
    shared = {
        "norm_mix": f(norm_mix), "w_in": f(w_in), "b_gate": f(b_gate), "sgu_norm": f(sgu_norm),
        "w_spatial": f(w_spatial), "b_spatial": f(b_spatial), "w_branch_a": f(w_branch_a),
        "w_branch_b": f(w_branch_b), "w_out": f(w_out), "norm_ffn": f(norm_ffn),
        "w_gate_up": f(w_gate_up), "w_down": f(w_down), "norm_final": f(norm_final),
    }
    if "nc" not in _NC_CACHE:
        _NC_CACHE["nc"] = build_program()
    nc = _NC_CACHE["nc"]
    ncores = DBG.get("ncores", 8)
    in_maps = []
    for c in range(ncores):
        c = c + DBG.get('core0', 0)
        b, hf = c // 2, c % 2
        m = dict(shared)
        m["xa"] = np.ascontiguousarray(x_prompt[b, 0:TP].T)
        m["xb"] = np.ascontiguousarray(np.concatenate(
            [x_prompt[b, hf * TP:(hf + 1) * TP], x_sample[4 * c:4 * c + 4].reshape(TS, D)], axis=0).T)
        m["flag"] = np.full((128, 1), float(hf), np.float32)
        m["ck"] = np.ascontiguousarray(cache_k[:, 4 * c:4 * c + 4].reshape(2, 4, 2048, SBW))
        m["cv"] = np.ascontiguousarray(cache_v[:, 4 * c:4 * c + 4].reshape(2, 4, 2048, SBW))
        for nm in DBG.get('shrink', ()):
            m.pop(nm, None)
        in_maps.append(m)
    res = run_bass_kernel_spmd(nc, in_maps, core_ids=list(range(ncores)))
    r = res.results
    if DBG:
        return r
    B, SEQ = x_prompt.shape[0], x_prompt.shape[1]
    y_p = np.empty((B, SEQ, D), np.float32)
    y_s = np.empty((32, 64, D), np.float32)
    k_p = np.empty((2, B, SEQ, NH, HD), np.float32)
    v_p = np.empty((2, B, SEQ, NH, HD), np.float32)
    k_s = np.empty((2, 32, 64, NH, HD), np.float32)
    v_s = np.empty((2, 32, 64, NH, HD), np.float32)
    sv_s = np.empty((2, 32, 64, SBW), np.float32)
    for c in range(8):
        b, hf = c // 2, c % 2
        y = np.asarray(r[c]["y"]).T; ko = np.asarray(r[c]["ko"]).transpose(0, 2, 1); vo = np.asarray(r[c]["vo"]); sv = np.asarray(r[c]["sv"])
        y_p[b, hf * TP:(hf + 1) * TP] = y[:TP]
        y_s[4 * c:4 * c + 4] = y[TP:].reshape(4, 64, D)
        k_p[:, b, hf * TP:(hf + 1) * TP] = ko[:, :TP].reshape(2, TP, NH, HD)
        v_p[:, b, hf * TP:(hf + 1) * TP] = vo[:, :TP].reshape(2, TP, NH, HD)
        k_s[:, 4 * c:4 * c + 4] = ko[:, TP:].reshape(2, 4, 64, NH, HD)
        v_s[:, 4 * c:4 * c + 4] = vo[:, TP:].reshape(2, 4, 64, NH, HD)
        sv_s[:, 4 * c:4 * c + 4] = sv.reshape(2, 4, 64, SBW)
    return (y_p, y_s, k_p, v_p, k_s, v_s, sv_s)
```
